# Optimizing a Trainium2 kernel written in Bass

```python
import jax, jax.numpy as jnp
from jax import lax
import numpy as np

D_MODEL = 2048
BATCH = 8
SEQ = 2048
DEPTH = 4

N_A_LAYERS = DEPTH // 2
N_B_LAYERS = DEPTH - N_A_LAYERS
HEAD_DIM = 128
A_HEADS = D_MODEL // HEAD_DIM
A_WIDTH = A_HEADS * HEAD_DIM
DILATED_GROUPS = ((128, 1), (512, 4), (2048, 16))
N_GROUPS = len(DILATED_GROUPS)
B_HEADS_PER_GROUP = 4
B_WIDTH = N_GROUPS * B_HEADS_PER_GROUP * HEAD_DIM
BAND_BLOCK = 128
QUERY_BLOCK = 128
ROT_DIM = HEAD_DIM // 4
ROPE_THETA = 500000.0
D_FF = 256 * ((8 * D_MODEL // 3 + 255) // 256)
CONV_WIDTH = 3
PLE_DIM = 256
NORM_EPS = 1e-6

kernel_name = "yoco_stickbreaking_dilated_hybrid"


def rms_norm(x, w):
    xf = x.astype(jnp.float32)
    y = xf * lax.rsqrt(jnp.mean(xf * xf, axis=-1, keepdims=True) + NORM_EPS)
    return (y * w.astype(jnp.float32)).astype(x.dtype)


def partial_rotary(x, positions):
    inv_freq = ROPE_THETA ** (-jnp.arange(0, ROT_DIM, 2, dtype=jnp.float32) / ROT_DIM)
    ang = positions.astype(jnp.float32)[..., None] * inv_freq
    ang = ang.reshape(ang.shape[:2] + (1,) * (x.ndim - 3) + ang.shape[-1:])
    cos, sin = jnp.cos(ang), jnp.sin(ang)
    xr = x[..., :ROT_DIM].astype(jnp.float32)
    x1, x2 = xr[..., : ROT_DIM // 2], xr[..., ROT_DIM // 2:]
    rot = jnp.concatenate([x1 * cos - x2 * sin, x2 * cos + x1 * sin], axis=-1).astype(x.dtype)
    return jnp.concatenate([rot, x[..., ROT_DIM:]], axis=-1)


def stick_breaking_attention(q, k, v):
    b, s, h, dh = q.shape
    n_blk = s // QUERY_BLOCK
    scale = dh ** -0.5
    q_blocks = q.reshape(b, n_blk, QUERY_BLOCK, h, dh).swapaxes(0, 1)
    key_pos = jnp.arange(s)

    def one_block(args):
        qb, blk = args
        z = jnp.einsum('bqhd,bkhd->bhqk', qb, k, preferred_element_type=jnp.float32) * scale
        q_pos = blk * QUERY_BLOCK + jnp.arange(QUERY_BLOCK)
        causal = key_pos[None, :] < q_pos[:, None]
        log_1m = jnp.where(causal, jax.nn.log_sigmoid(-z), 0.0)
        shifted = jnp.concatenate([log_1m[..., 1:], jnp.zeros_like(log_1m[..., :1])], axis=-1)
        log_stick = lax.cumsum(shifted, axis=shifted.ndim - 1, reverse=True)
        a = jnp.where(causal, jnp.exp(jax.nn.log_sigmoid(z) + log_stick), 0.0)
        return jnp.einsum('bhqk,bkhd->bqhd', a.astype(v.dtype), v)

    out = lax.map(one_block, (q_blocks, jnp.arange(n_blk)))
    return out.swapaxes(0, 1).reshape(b, s, h, dh)


def mixer_a(h, w_qkv, w_o):
    b, s, _ = h.shape
    qkv = (h @ w_qkv).reshape(b, s, 3, A_HEADS, HEAD_DIM)
    o = stick_breaking_attention(qkv[:, :, 0], qkv[:, :, 1], qkv[:, :, 2])
    return o.reshape(b, s, A_WIDTH) @ w_o


def to_dilated_blocks(x, dil):
    b, s, h, dh = x.shape
    sub_len = s // dil
    n_blk = -(-sub_len // BAND_BLOCK)
    xs = x.reshape(b, sub_len, dil, h, dh).transpose(0, 2, 1, 3, 4)
    xs = jnp.pad(xs, ((0, 0), (0, 0), (0, n_blk * BAND_BLOCK - sub_len), (0, 0), (0, 0)))
    return xs.reshape(b, dil, n_blk, BAND_BLOCK, h, dh)


def with_previous_block(xb):
    prev = jnp.pad(xb[:, :, :-1], ((0, 0), (0, 0), (1, 0), (0, 0), (0, 0), (0, 0)))
    return jnp.concatenate([prev, xb], axis=3)


def from_dilated_blocks(xb, seq):
    b, dil, n_blk, blk = xb.shape[:4]
    rest = xb.shape[4:]
    sub_len = seq // dil
    xs = xb.reshape((b, dil, n_blk * blk) + rest)[:, :, :sub_len]
    return jnp.moveaxis(xs, 1, 2).reshape((b, seq) + rest)


def dilated_band_attention(q, kb, vb, dil, steps):
    s, dh = q.shape[1], q.shape[-1]
    qb = to_dilated_blocks(q, dil)
    n_blk = qb.shape[2]
    scores = jnp.einsum('brnqhd,brnkhd->brnhqk', qb, kb,
                        preferred_element_type=jnp.float32) * (dh ** -0.5)
    qi = jnp.arange(BAND_BLOCK)[:, None]
    kj = jnp.arange(2 * BAND_BLOCK)[None, :]
    dist = BAND_BLOCK + qi - kj
    blk = jnp.arange(n_blk)[:, None, None]
    valid = (dist >= 0) & (dist <= steps) & ((blk - 1) * BAND_BLOCK + kj >= 0)
    scores = jnp.where(valid[:, None], scores, -jnp.inf)
    m = jnp.max(scores, axis=-1, keepdims=True)
    e = jnp.exp(scores - m)
    den = jnp.sum(e, axis=-1, keepdims=True)
    o = jnp.einsum('brnhqk,brnkhd->brnqhd', (e / den).astype(vb.dtype), vb)
    lse = jnp.swapaxes((m + jnp.log(den))[..., 0], -1, -2)
    return from_dilated_blocks(o, s), from_dilated_blocks(lse, s)


def shared_kv_blocks(x, kv_norm_w, w_kv, positions):
    b, s, _ = x.shape
    kv = (rms_norm(x, kv_norm_w) @ w_kv).reshape(b, s, 2, N_GROUPS, B_HEADS_PER_GROUP, HEAD_DIM)
    k = partial_rotary(kv[:, :, 0], positions)
    v = kv[:, :, 1]
    k_blocks = [with_previous_block(to_dilated_blocks(k[:, :, g], dil))
                for g, (_, dil) in enumerate(DILATED_GROUPS)]
    v_blocks = [with_previous_block(to_dilated_blocks(v[:, :, g], dil))
                for g, (_, dil) in enumerate(DILATED_GROUPS)]
    return k_blocks, v_blocks


def mixer_b(h, w_q, w_o, k_blocks, v_blocks, positions):
    b, s, _ = h.shape
    q = partial_rotary((h @ w_q).reshape(b, s, N_GROUPS, B_HEADS_PER_GROUP, HEAD_DIM), positions)
    outs, lses = [], []
    for g, (window, dil) in enumerate(DILATED_GROUPS):
        o_g, lse_g = dilated_band_attention(q[:, :, g], k_blocks[g], v_blocks[g], dil, window // dil)
        outs.append(o_g)
        lses.append(lse_g)
    alpha = jax.nn.softmax(jnp.stack(lses, axis=2), axis=2)
    o = jnp.stack(outs, axis=2) * alpha[..., None].astype(outs[0].dtype)
    return o.reshape(b, s, B_WIDTH) @ w_o


def causal_depthwise_conv(u, w, bias):
    c = u.shape[-1]
    y = lax.conv_general_dilated(u, w[:, None, :].astype(u.dtype), window_strides=(1,),
                                 padding=[(CONV_WIDTH - 1, 0)],
                                 dimension_numbers=('NWC', 'WIO', 'NWC'),
                                 feature_group_count=c)
    return y + bias.astype(u.dtype)


def conv_ffn(h, w_up, conv_w, conv_b, w_down):
    u = causal_depthwise_conv(h @ w_up, conv_w, conv_b)
    gate, val = u[..., :D_FF], u[..., D_FF:]
    return (jax.nn.silu(gate) * val) @ w_down


def setup_inputs(seed: int = 0) -> dict:
    key = jax.random.key(seed)
    ks = jax.random.split(key, 20)

    def nrm(k, shape, scale=1.0):
        return jax.random.normal(k, shape, jnp.float32) * scale

    def gain(k, shape):
        return 1.0 + nrm(k, shape, 0.02)

    res = (2.0 * DEPTH) ** -0.5
    positions = (jnp.arange(SEQ, dtype=jnp.int32)[None, :]
                 + jax.random.randint(ks[2], (BATCH, 1), 0, 1024, dtype=jnp.int32))
    return {
        'x': nrm(ks[0], (BATCH, SEQ, D_MODEL)),
        'p': nrm(ks[1], (DEPTH, BATCH, SEQ, PLE_DIM)),
        'positions': positions,
        'attn_norm_w': gain(ks[3], (DEPTH, D_MODEL)),
        'a_w_qkv': nrm(ks[4], (N_A_LAYERS, D_MODEL, 3 * A_WIDTH), D_MODEL ** -0.5),
        'a_w_o': nrm(ks[5], (N_A_LAYERS, A_WIDTH, D_MODEL), res * A_WIDTH ** -0.5),
        'kv_norm_w': gain(ks[6], (D_MODEL,)),
        'b_w_kv': nrm(ks[7], (D_MODEL, 2 * B_WIDTH), D_MODEL ** -0.5),
        'b_w_q': nrm(ks[8], (N_B_LAYERS, D_MODEL, B_WIDTH), D_MODEL ** -0.5),
        'b_w_o': nrm(ks[9], (N_B_LAYERS, B_WIDTH, D_MODEL), res * B_WIDTH ** -0.5),
        'ffn_norm_w': gain(ks[10], (DEPTH, D_MODEL)),
        'ffn_w_up': nrm(ks[11], (DEPTH, D_MODEL, 2 * D_FF), D_MODEL ** -0.5),
        'ffn_conv_w': nrm(ks[12], (DEPTH, CONV_WIDTH, 2 * D_FF), CONV_WIDTH ** -0.5),
        'ffn_conv_b': nrm(ks[13], (DEPTH, 2 * D_FF), 0.01),
        'ffn_w_down': nrm(ks[14], (DEPTH, D_FF, D_MODEL), res * D_FF ** -0.5),
        'ple_norm_w': gain(ks[15], (DEPTH, D_MODEL)),
        'ple_w_gate': nrm(ks[16], (DEPTH, D_MODEL, D_MODEL), D_MODEL ** -0.5),
        'ple_w_proj': nrm(ks[17], (DEPTH, PLE_DIM, D_MODEL), res * PLE_DIM ** -0.5),
        'final_norm_w': gain(ks[18], (D_MODEL,)),
    }


def reference(x, p, positions, attn_norm_w, a_w_qkv, a_w_o, kv_norm_w, b_w_kv, b_w_q, b_w_o,
              ffn_norm_w, ffn_w_up, ffn_conv_w, ffn_conv_b, ffn_w_down,
              ple_norm_w, ple_w_gate, ple_w_proj, final_norm_w):
    k_blocks, v_blocks = None, None
    for i in range(DEPTH):
        h = rms_norm(x, attn_norm_w[i])
        if i < N_A_LAYERS:
            x = x + mixer_a(h, a_w_qkv[i], a_w_o[i])
        else:
            if i == N_A_LAYERS:
                k_blocks, v_blocks = shared_kv_blocks(x, kv_norm_w, b_w_kv, positions)
            j = i - N_A_LAYERS
            x = x + mixer_b(h, b_w_q[j], b_w_o[j], k_blocks, v_blocks, positions)
        x = x + conv_ffn(rms_norm(x, ffn_norm_w[i]), ffn_w_up[i], ffn_conv_w[i], ffn_conv_b[i], ffn_w_down[i])
        gate = jax.nn.sigmoid(rms_norm(x, ple_norm_w[i]) @ ple_w_gate[i])
        x = x + gate * (p[i].astype(x.dtype) @ ple_w_proj[i])
    return rms_norm(x, final_norm_w)
```

```python
from contextlib import ExitStack
import numpy as np
import concourse.bass as bass
import concourse.mybir as mybir
from concourse.bass_utils import run_bass_kernel_spmd

F32 = mybir.dt.float32
BF16 = mybir.dt.bfloat16
I32 = mybir.dt.int32
AF = mybir.ActivationFunctionType
ALU = mybir.AluOpType

D = 2048
S = 2048
DFF = 5632
NFC = DFF // 128
QSCALE = 128.0 ** -0.5
PI = float(np.pi)
SNAP = False


class DSem:
    __slots__ = ("sem", "cnt")

    def __init__(self, sem):
        self.sem, self.cnt = sem, 0


class Buf:
    __slots__ = ("name", "w", "r", "ds", "excl")

    def __init__(self, name="", excl=False):
        self.name, self.w, self.r, self.ds, self.excl = name, [], [], None, excl


class Eng:
    def __init__(self, name, obj, sem):
        self.name, self.obj, self.sem = name, obj, sem
        self.cnt = 0
        self.pending = False
        self.seen = {}

    def wait(self, tok):
        sem, val = tok
        if sem is self.sem and self.name == "pe":
            return
        k = id(sem)
        if self.seen.get(k, 0) >= val:
            return
        self.seen[k] = val
        self.obj.wait_ge(sem, val)


class FW:
    def __init__(self, nc):
        self.nc = nc
        self._cms = []
        self.pe = Eng("pe", nc.tensor, self._sem("s_pe"))
        self.act = Eng("act", nc.scalar, self._sem("s_act"))
        self.dve = Eng("dve", nc.vector, self._sem("s_dve"))
        self.pool = Eng("pool", nc.gpsimd, self._sem("s_pool"))
        self.sp = Eng("sp", nc.sync, self._sem("s_sp"))
        self.engs = [self.pe, self.act, self.dve, self.pool, self.sp]
        self.free_ds, self.all_ds, self.live = [], [], []
        self.uid = 0

    def _sem(self, name):
        cm = self.nc.semaphore(name)
        self._cms.append(cm)
        return cm.__enter__()

    def close(self):
        for cm in reversed(self._cms):
            cm.__exit__(None, None, None)

    def _ds(self, buf):
        if buf.ds is None:
            if self.free_ds:
                buf.ds = self.free_ds.pop()
            else:
                buf.ds = DSem(self._sem("d%d" % len(self.all_ds)))
                self.all_ds.append(buf.ds)
            self.live.append(buf)
        return buf.ds

    def _deps(self, eng, reads, writes):
        for b in reads:
            for t in b.w:
                eng.wait(t)
            if b.excl:
                for t in b.r:
                    if t[0] is not eng.sem:
                        eng.wait(t)
        for b in writes:
            for t in b.w:
                eng.wait(t)
            for t in b.r:
                eng.wait(t)

    def _mark(self, tok, reads, writes):
        for b in reads:
            b.r.append(tok)
            if len(b.r) > 10:
                best = {}
                for s_, v in b.r:
                    if id(s_) not in best or best[id(s_)][1] < v:
                        best[id(s_)] = (s_, v)
                b.r = list(best.values())
        for b in writes:
            b.w = [tok]
            b.r = []

    def op(self, eng, fn, reads=(), writes=(), inc=True):
        self._deps(eng, reads, writes)
        ins = fn(eng.obj)
        if inc:
            eng.cnt += 1
            ins.then_inc(eng.sem, 1)
            eng.pending = False
            tok = (eng.sem, eng.cnt)
        else:
            eng.pending = True
            tok = (eng.sem, eng.cnt + 1)
        self._mark(tok, reads, writes)
        return ins

    def dma(self, q, out_ap, in_ap, reads=(), writes=(), owner=None):
        owner = owner or (writes[0] if writes else reads[0])
        ds = self._ds(owner)
        self._deps(q, reads, writes)
        ins = q.obj.dma_start(out=out_ap, in_=in_ap)
        ds.cnt += 16
        ins.then_inc(ds.sem, 16)
        self._mark((ds.sem, ds.cnt), reads, writes)
        return ins

    def barrier(self):
        assert not self.pe.pending
        toks = [(e.sem, e.cnt) for e in self.engs if e.cnt > 0]
        toks += [(d.sem, d.cnt) for d in self.all_ds if d.cnt > 0]
        for e in self.engs:
            for t in toks:
                e.wait(t)
        for b in self.live:
            self.free_ds.append(b.ds)
            b.ds = None
        self.live = []


class Phase:
    def __init__(self, fw):
        self.fw, self.es = fw, ExitStack()

    def __enter__(self):
        return self

    def tile(self, shape, dtype, name="t"):
        self.fw.uid += 1
        t = self.es.enter_context(self.fw.nc.sbuf_tensor("%s_%d" % (name, self.fw.uid), list(shape), dtype))
        return t, Buf(name)

    def __exit__(self, *a):
        self.fw.barrier()
        self.es.close()
        return False


def build_program(nlayers=4, dbg=False, stop=None, start=0, lite=()):
    nc = bass.Bass("TRN2", target_bir_lowering=False)
    fw = FW(nc)
    pe, act, dve, pool, sp = fw.pe, fw.act, fw.dve, fw.pool, fw.sp

    def din(name, shape, dt=F32):
        if name in lite:
            return nc.dram_tensor(name, [1] * len(shape), dt, kind="Internal").ap()
        return nc.dram_tensor(name, list(shape), dt, kind="ExternalInput").ap()

    def dscr(name, shape, dt):
        return nc.dram_tensor(name, list(shape), dt, kind=("ExternalOutput" if dbg else "Internal")).ap()

    xT_in = din("xT_in", [D, S])
    pT = din("pT", [4, 256, S])
    pos32 = din("pos32", [32, S], I32)
    normw_d = din("normw", [128, 14 * 16])
    convw_d = din("convw", [128, 4 * 3 * 88])
    convb_d = din("convb", [128, 4 * 88])
    a_w_qkv = din("a_w_qkv", [2, D, 3 * D])
    a_w_o = din("a_w_o", [2, D, D])
    b_w_kv = din("b_w_kv", [D, 3072])
    b_w_q = din("b_w_q", [2, D, 1536])
    b_w_o = din("b_w_o", [2, 1536, D])
    ffn_w_up = din("ffn_w_up", [4, D, 2 * DFF])
    ffn_w_down = din("ffn_w_down", [4, DFF, D])
    ple_w_gate = din("ple_w_gate", [4, D, D])
    ple_w_proj = din("ple_w_proj", [4, 256, D])
    c_tri = din("c_tri", [128, 128])
    c_ones = din("c_ones", [128, 128])
    c_amask = din("c_amask", [128, 4 * 512])
    c_bmask = din("c_bmask", [128, 3 * 1024])
    c_sel33 = din("c_sel33", [33, 128])
    c_perm = din("c_perm", [32, 32])
    c_invf = din("c_invf", [32, 1])
    yT = nc.dram_tensor("yT", [D, S], F32, kind="ExternalOutput").ap()

    xT = dscr("xT_s", [D, S], F32)
    qT = dscr("qT_s", [D, S], BF16)
    kT = dscr("kT_s", [D, S], BF16)
    Vd = dscr("V_s", [S, D], BF16)
    AT = dscr("AT_s", [DFF, S], BF16)
    kTB = dscr("kTB_s", [1536, S], BF16)
    VB = dscr("VB_s", [3, S, 512], BF16)

    def ptile(name, shape, dt):
        return nc.alloc_sbuf_tensor(name, list(shape), dt), Buf(name)

    ones_b, onesB = ptile("ones_b", [128, 128], BF16)
    tri_b, triB = ptile("tri_b", [128, 128], BF16)
    amask, amaskB = ptile("amask", [128, 4, 512], BF16)
    bmask, bmaskB = ptile("bmask", [128, 3, 1024], BF16)
    sel33, selB = ptile("sel33", [33, 128], BF16)
    perm32, permB = ptile("perm32", [32, 32], BF16)
    invf, invfB = ptile("invf", [32, 1], F32)
    normw, normwB = ptile("normw_t", [128, 14, 16], F32)
    convw, convwB = ptile("convw_t", [128, 4, 3, 88], F32)
    convb, convbB = ptile("convb_t", [128, 4, 88], F32)
    COS, cosB = ptile("COS", [32, S], F32)
    SIN, sinB = ptile("SIN", [32, S], F32)
    ps = nc.alloc_psum_tensor("ps", [128, 8, 512], F32)
    PSB = [Buf("ps%d" % i, excl=True) for i in range(8)]

    with Phase(fw) as ph:
        fw.dma(pool, ones_b[:], c_ones, writes=[onesB])
        fw.dma(pool, tri_b[:], c_tri, writes=[triB])
        fw.dma(pool, amask[:], c_amask.rearrange("p (i t) -> p i t", i=4), writes=[amaskB])
        fw.dma(pool, bmask[:], c_bmask.rearrange("p (i t) -> p i t", i=3), writes=[bmaskB])
        fw.dma(pool, sel33[:], c_sel33, writes=[selB])
        fw.dma(pool, perm32[:], c_perm, writes=[permB])
        fw.dma(sp, invf[:], c_invf, writes=[invfB])
        fw.dma(sp, normw[:], normw_d.rearrange("p (a b) -> p a b", a=14), writes=[normwB])
        fw.dma(sp, convw[:], convw_d.rearrange("p (a b c) -> p a b c", a=4, b=3), writes=[convwB])
        fw.dma(sp, convb[:], convb_d.rearrange("p (a b) -> p a b", a=4), writes=[convbB])
        pi_t, piB = ph.tile([32, S], I32, "posi")
        ang, angB = ph.tile([32, S], F32, "ang")
        tmp, tmpB = ph.tile([32, S], F32, "angt")
        fw.dma(sp, pi_t[:], pos32, writes=[piB])
        fw.op(dve, lambda e: e.tensor_copy(out=ang[:], in_=pi_t[:]), reads=[piB], writes=[angB])
        fw.op(dve, lambda e: e.tensor_scalar(out=ang[:], in0=ang[:], scalar1=invf[:, 0:1], scalar2=None, op0=ALU.mult),
              reads=[angB, invfB], writes=[angB])
        MAGIC = 12582912.0
        C1 = 6.28125
        C2 = 2.0 * PI - C1
        nf, nfB = ph.tile([32, S], F32, "nf")
        for (dst_t, dst_b, shift) in ((SIN, sinB, 0.0), (COS, cosB, 0.5 * PI)):
            fw.op(dve, lambda e: e.tensor_scalar(out=tmp[:], in0=ang[:], scalar1=shift, scalar2=None, op0=ALU.add), reads=[angB], writes=[tmpB])
            fw.op(dve, lambda e: e.tensor_scalar(out=nf[:], in0=tmp[:], scalar1=1.0 / (2.0 * PI), scalar2=MAGIC, op0=ALU.mult, op1=ALU.add), reads=[tmpB], writes=[nfB])
            fw.op(dve, lambda e: e.tensor_scalar(out=nf[:], in0=nf[:], scalar1=MAGIC, scalar2=None, op0=ALU.subtract), reads=[nfB], writes=[nfB])
            fw.op(dve, lambda e: e.scalar_tensor_tensor(out=tmp[:], in0=nf[:], scalar=-C1, in1=tmp[:], op0=ALU.mult, op1=ALU.add), reads=[nfB, tmpB], writes=[tmpB])
            fw.op(dve, lambda e: e.scalar_tensor_tensor(out=tmp[:], in0=nf[:], scalar=-C2, in1=tmp[:], op0=ALU.mult, op1=ALU.add), reads=[nfB, tmpB], writes=[tmpB])
            fw.op(act, lambda e: e.activation(out=dst_t[:], in_=tmp[:], func=AF.Sin), reads=[tmpB], writes=[dst_b])

    snapB = Buf("snap")

    def snap(name):
        if not (dbg and SNAP):
            return
        d_ = nc.dram_tensor(name, [D, S], F32, kind="ExternalOutput").ap()
        fw.barrier()
        fw.dma(sp, d_, xT, owner=snapB)
        fw.barrier()

    def kview(ap2d, rows=128):
        return ap2d.rearrange("(kc kp) f -> kp kc f", kp=rows)

    def mm(out, lhsT, rhs, start, stop):
        return lambda e: e.matmul(out, lhsT=lhsT, rhs=rhs, start=start, stop=stop)

    def do_norm(xsrc, widx, hT=None, hTB=None, out_dram=None):
        NT = 256
        with Phase(fw) as ph:
            xs = [ph.tile([128, 16, NT], F32, "nx") for _ in range(2)]
            sq, sqB = ph.tile([128, 16, NT], BF16, "nsq")
            r1 = [ph.tile([128, NT], F32, "nr1") for _ in range(2)]
            r2 = [ph.tile([128, NT], F32, "nr2") for _ in range(2)]
            ob = [ph.tile([128, 16, NT], F32, "nob") for _ in range(2)] if out_dram is not None else None
            for tc in range(S // NT):
                X, XB = xs[tc % 2]
                tsl = slice(tc * NT, (tc + 1) * NT)
                fw.dma(sp, X[:], kview(xsrc)[:, :, tsl], writes=[XB])
                fw.op(act, lambda e: e.activation(out=sq[:], in_=X[:], func=AF.Square), reads=[XB], writes=[sqB])
                bk = tc % 2
                for kc in range(16):
                    fw.op(pe, mm(ps[:, bk, 0:NT], ones_b[:], sq[:, kc, :], kc == 0, kc == 15), reads=[sqB, onesB], writes=[PSB[bk]], inc=(kc == 15))
                R1, R1B = r1[tc % 2]
                R2, R2B = r2[tc % 2]
                fw.op(act, lambda e: e.activation(out=R1[:], in_=ps[:, bk, 0:NT], func=AF.Sqrt, scale=1.0 / D, bias=1e-6), reads=[PSB[bk]], writes=[R1B])
                fw.op(dve, lambda e: e.reciprocal(out=R2[:], in_=R1[:]), reads=[R1B], writes=[R2B])
                if out_dram is None:
                    for kc in range(16):
                        fw.op(dve, lambda e: e.scalar_tensor_tensor(out=hT[:, kc, tsl], in0=X[:, kc, :], scalar=normw[:, widx, kc:kc + 1], in1=R2[:], op0=ALU.mult, op1=ALU.mult),
                              reads=[XB, R2B, normwB], writes=[hTB])
                else:
                    O, OB = ob[tc % 2]
                    for kc in range(16):
                        fw.op(dve, lambda e: e.scalar_tensor_tensor(out=O[:, kc, :], in0=X[:, kc, :], scalar=normw[:, widx, kc:kc + 1], in1=R2[:], op0=ALU.mult, op1=ALU.mult),
                              reads=[XB, R2B, normwB], writes=[OB])
                    fw.dma(sp, kview(out_dram)[:, :, tsl], O[:], reads=[OB], owner=OB)

    class PSet:
        def __init__(self):
            self.n = 0

        def next(self):
            s = self.n % 2
            self.n += 1
            return s

    def proj_fm(ph, hT, hTB, KCn, W2d, ncols, evac, wname="w"):
        wb = [ph.tile([128, KCn, 512], BF16, wname) for _ in range(2)]
        pset = PSet()
        ng = ncols // 512
        fw.dma(pool, wb[0][0][:], kview(W2d)[:, :, 0:512], writes=[wb[0][1]])
        for g in range(ng):
            Wt, WB = wb[g % 2]
            if g + 1 < ng:
                fw.dma(pool, wb[(g + 1) % 2][0][:], kview(W2d)[:, :, (g + 1) * 512:(g + 2) * 512], writes=[wb[(g + 1) % 2][1]])
            for fc in range(4):
                for half in range(2):
                    s = pset.next()
                    for kc in range(KCn):
                        for t2 in range(2):
                            tcs = slice((half * 2 + t2) * 512, (half * 2 + t2 + 1) * 512)
                            fw.op(pe, mm(ps[:, 2 * s + t2, :], Wt[:, kc, fc * 128:(fc + 1) * 128], hT[:, kc, tcs], kc == 0, kc == KCn - 1),
                                  reads=[WB, hTB], writes=[PSB[2 * s + t2]], inc=(kc == KCn - 1 and t2 == 1))
                    evac(g * 4 + fc, half, s)

    def v1024(ap):
        return ap.rearrange("p (b t) -> p b t", b=2)

    def make_resid_evac(ph, xsrc):
        xts = [ph.tile([128, S], F32, "xr") for _ in range(3)]
        st = {"n": 0}

        def evac(c, half, s):
            X, XB = xts[st["n"] % 3]
            if half == 0:
                fw.dma(sp, X[:], xsrc[c * 128:(c + 1) * 128, :], writes=[XB])
            hs = slice(half * 1024, (half + 1) * 1024)
            fw.op(dve, lambda e: e.tensor_tensor(out=v1024(X[:, hs]), in0=v1024(X[:, hs]), in1=ps[:, 2 * s:2 * s + 2, :], op=ALU.add),
                  reads=[XB, PSB[2 * s], PSB[2 * s + 1]], writes=[XB])
            if half == 1:
                fw.dma(sp, xT[c * 128:(c + 1) * 128, :], X[:], reads=[XB], owner=XB)
                st["n"] += 1
        return evac

    def make_rotary_evac(ph, dst_of, after=None):
        q32s = [ph.tile([32, 1024], BF16, "q32") for _ in range(2)]
        qlos = [ph.tile([32, 1024], BF16, "qlo") for _ in range(2)]
        t1s = [ph.tile([32, 1024], F32, "rt1") for _ in range(2)]
        t2s = [ph.tile([32, 1024], F32, "rt2") for _ in range(2)]
        st = {"n": 0}

        def evac(c, half, s):
            k = st["n"] % 2
            st["n"] += 1
            dil = (1, 4, 16)[c // 4]
            ni = 1024 // dil
            dst, dstB = dst_of(c)
            Q, QB = q32s[k]
            T1, T1B = t1s[k]
            T2, T2B = t2s[k]
            hs = slice(half * 1024, (half + 1) * 1024)
            src = [PSB[2 * s], PSB[2 * s + 1]]
            rb = 4 + 2 * k
            QL, QLB = qlos[k]
            import os
            LVL = int(os.environ.get("ROTLVL", "9"))
            if LVL >= 1:
                fw.op(act, lambda e: e.activation(out=v1024(Q[:]), in_=ps[0:32, 2 * s:2 * s + 2, :], func=AF.Copy), reads=src, writes=[QB])
                fw.op(dve, lambda e: e.tensor_tensor(out=v1024(QL[:]), in0=ps[0:32, 2 * s:2 * s + 2, :], in1=v1024(Q[:]), op=ALU.subtract), reads=src + [QB], writes=[QLB])
            for t2 in range(2 if LVL >= 2 else 0):
                fw.op(pe, mm(ps[0:32, rb + t2, :], perm32[:], Q[:, t2 * 512:(t2 + 1) * 512], True, False), reads=[QB, permB], writes=[PSB[rb + t2]], inc=False)
                fw.op(pe, mm(ps[0:32, rb + t2, :], perm32[:], QL[:, t2 * 512:(t2 + 1) * 512], False, True), reads=[QLB, permB], writes=[PSB[rb + t2]], inc=(t2 == 1))
            if LVL >= 3:
                fw.op(dve, lambda e: e.tensor_tensor(out=v1024(T1[:]), in0=ps[0:32, 2 * s:2 * s + 2, :], in1=v1024(COS[:, hs]), op=ALU.mult),
                      reads=src + [cosB], writes=[T1B])
            if LVL >= 4:
                fw.op(dve, lambda e: e.tensor_tensor(out=v1024(T2[:]), in0=ps[0:32, rb:rb + 2, :], in1=v1024(SIN[:, hs]), op=ALU.mult),
                      reads=[PSB[rb], PSB[rb + 1], sinB], writes=[T2B])

            def dview(rows):
                if dil == 1:
                    return dst[rows, hs]
                return dst[rows, :].rearrange("p (r i) -> p r i", r=dil)[:, :, half * ni:(half + 1) * ni]

            def sview(ap):
                if dil == 1:
                    return ap
                return ap.rearrange("p (i r) -> p r i", r=dil)
            all_src = ps[:, 2 * s:2 * s + 2, :].rearrange("p b t -> p (b t)")
            fw.op(act, lambda e: e.activation(out=dview(slice(0, 128)), in_=sview(all_src), func=AF.Copy), reads=src, writes=[dstB])
            if LVL >= 5:
                fw.op(dve, lambda e: e.tensor_tensor(out=dview(slice(0, 32)), in0=sview(T1[:]), in1=sview(T2[:]), op=ALU.add),
                      reads=[T1B, T2B], writes=[dstB])
            if after is not None:
                after(c, half, dst, dstB)
        return evac

    xsrc = xT_in
    kv_done = False
    if start > 0:
        xT = xT_in
        xsrc = xT
    for layer in range(start, nlayers):
        if layer < 2:
            with Phase(fw) as ph:
                hT, hTB = ph.tile([128, 16, S], BF16, "hT")
                do_norm(xsrc, layer, hT, hTB)
                obs = [ph.tile([128, S], BF16, "qk_o") for _ in range(2)]
                stq = {"n": 0}

                def evac_qk(c, half, s):
                    O, OB = obs[stq["n"] % 2]
                    hs = slice(half * 1024, (half + 1) * 1024)
                    srcb = [PSB[2 * s], PSB[2 * s + 1]]
                    if c < 16:
                        fw.op(act, lambda e: e.activation(out=v1024(O[:, hs]), in_=ps[:, 2 * s:2 * s + 2, :], func=AF.Copy, scale=QSCALE), reads=srcb, writes=[OB])
                    else:
                        fw.op(dve, lambda e: e.tensor_copy(out=v1024(O[:, hs]), in_=ps[:, 2 * s:2 * s + 2, :]), reads=srcb, writes=[OB])
                    if half == 1:
                        dstd = qT if c < 16 else kT
                        cc = c % 16
                        fw.dma(sp, dstd[cc * 128:(cc + 1) * 128, :], O[:], reads=[OB], owner=OB)
                        stq["n"] += 1
                proj_fm(ph, hT, hTB, 16, a_w_qkv[layer][:, 0:2 * D], 2 * D, evac_qk, "wqk")
                wv = [ph.tile([128, 16, 512], BF16, "wv") for _ in range(2)]
                vbig = [ph.tile([128, 16, 512], BF16, "vbig") for _ in range(2)]
                Wv2d = a_w_qkv[layer][:, 2 * D:3 * D]
                fw.dma(pool, wv[0][0][:], kview(Wv2d)[:, :, 0:512], writes=[wv[0][1]])
                nb = 0
                for g in range(4):
                    Wt, WB = wv[g % 2]
                    if g + 1 < 4:
                        fw.dma(pool, wv[(g + 1) % 2][0][:], kview(Wv2d)[:, :, (g + 1) * 512:(g + 2) * 512], writes=[wv[(g + 1) % 2][1]])
                    VBt, VBB = vbig[g % 2]
                    for tb in range(16):
                        bk = 4 + nb % 4
                        nb += 1
                        for kc in range(16):
                            fw.op(pe, mm(ps[:, bk, :], hT[:, kc, tb * 128:(tb + 1) * 128], Wt[:, kc, :], kc == 0, kc == 15), reads=[WB, hTB], writes=[PSB[bk]], inc=(kc == 15))
                        if tb % 2 == 0:
                            fw.op(act, lambda e: e.activation(out=VBt[:, tb, :], in_=ps[:, bk, :], func=AF.Copy), reads=[PSB[bk]], writes=[VBB])
                        else:
                            fw.op(dve, lambda e: e.tensor_copy(out=VBt[:, tb, :], in_=ps[:, bk, :]), reads=[PSB[bk]], writes=[VBB])
                    fw.dma(sp, Vd.rearrange("(tb tp) f -> tp tb f", tp=128)[:, :, g * 512:(g + 1) * 512], VBt[:], reads=[VBB], owner=VBB)
            if stop == ("qkv", layer):
                break
            with Phase(fw) as ph:
                OT, OTB = ph.tile([128, 16, S], BF16, "OT")
                with Phase(fw) as ph2:
                    qh = [ph2.tile([128, S], BF16, "qh") for _ in range(2)]
                    kh = [ph2.tile([128, S], BF16, "kh") for _ in range(2)]
                    vh = [ph2.tile([128, 16, 128], BF16, "vh") for _ in range(2)]
                    eT = [ph2.tile([128, 512], F32, "eT") for _ in range(2)]
                    Lt = [ph2.tile([128, 512], BF16, "Lt") for _ in range(3)]
                    At = [ph2.tile([128, 512], BF16, "At") for _ in range(2)]
                    carry, carryB = ph2.tile([33, 512], F32, "carry")
                    hiT = [ph2.tile([33, 512], BF16, "hiT") for _ in range(3)]

                    def load_head(h):
                        fw.dma(sp, qh[h % 2][0][:], qT[h * 128:(h + 1) * 128, :], writes=[qh[h % 2][1]])
                        fw.dma(sp, kh[h % 2][0][:], kT[h * 128:(h + 1) * 128, :], writes=[kh[h % 2][1]])
                        fw.dma(sp, vh[h % 2][0][:], Vd.rearrange("(sb sp) f -> sp sb f", sp=128)[:, :, h * 128:(h + 1) * 128], writes=[vh[h % 2][1]])
                    load_head(0)
                    blocks = [(qc, kb) for qc in range(4) for kb in range(4 * qc + 3, -1, -1)]
                    NB = len(blocks)
                    for h in range(16):
                        if h + 1 < 16:
                            load_head(h + 1)
                        Q, QB = qh[h % 2]
                        K, KB = kh[h % 2]
                        V, VB_ = vh[h % 2]
                        for n in range(-2, NB + 2):
                            m = n + 2
                            if 0 <= m < NB:
                                qc, kb = blocks[m]
                                fw.op(pe, mm(ps[:, m % 4, :], K[:, kb * 128:(kb + 1) * 128], Q[:, qc * 512:(qc + 1) * 512], True, False),
                                      reads=[KB, QB], writes=[PSB[m % 4]])
                            m = n + 1
                            if 0 <= m < NB:
                                qc, kb = blocks[m]
                                first, last, diag = kb == 4 * qc + 3, kb == 0, kb >= 4 * qc
                                E, EB = eT[m % 2]
                                L, LB = Lt[m % 3]
                                fw.op(act, lambda e: e.activation(out=E[:], in_=ps[:, m % 4, :], func=AF.Exp), reads=[PSB[m % 4]], writes=[EB])
                                fw.op(act, lambda e: e.activation(out=L[:], in_=E[:], func=AF.Ln, bias=1.0), reads=[EB], writes=[LB])
                                if diag:
                                    fw.op(pool, lambda e: e.tensor_tensor(out=L[:], in0=L[:], in1=amask[:, kb - 4 * qc, :], op=ALU.mult), reads=[LB, amaskB], writes=[LB])
                                if not last:
                                    bb = 4 + m % 2
                                    H, HB = hiT[m % 3]
                                    fw.op(pe, mm(ps[0:33, bb, :], ones_b[:, 0:33], L[:], True, True), reads=[LB, onesB], writes=[PSB[bb]])
                                    if first:
                                        fw.op(dve, lambda e: e.tensor_copy(out=carry[:], in_=ps[0:33, bb, :]), reads=[PSB[bb]], writes=[carryB])
                                    else:
                                        fw.op(dve, lambda e: e.tensor_tensor(out=carry[:], in0=carry[:], in1=ps[0:33, bb, :], op=ALU.add), reads=[PSB[bb], carryB], writes=[carryB])
                                    fw.op(dve, lambda e: e.tensor_copy(out=H[:], in_=carry[:]), reads=[carryB], writes=[HB])
                                    fw.op(dve, lambda e: e.tensor_tensor(out=H[32:33, :], in0=carry[32:33, :], in1=H[32:33, :], op=ALU.subtract), reads=[carryB, HB], writes=[HB])
                            if 0 <= n < NB:
                                qc, kb = blocks[n]
                                first, diag = kb == 4 * qc + 3, kb >= 4 * qc
                                L, LB = Lt[n % 3]
                                fw.op(pe, mm(ps[:, n % 4, :], tri_b[:], L[:], False, first), reads=[LB, triB], writes=[PSB[n % 4]], inc=first)
                                if not first:
                                    H, HB = hiT[(n - 1) % 3]
                                    fw.op(pe, mm(ps[:, n % 4, :], sel33[:], H[:], False, True), reads=[HB, selB], writes=[PSB[n % 4]])
                                A, AB = At[n % 2]
                                fw.op(act, lambda e: e.activation(out=A[:], in_=ps[:, n % 4, :], func=AF.Exp), reads=[PSB[n % 4]], writes=[AB])
                                if diag:
                                    fw.op(pool, lambda e: e.tensor_tensor(out=A[:], in0=A[:], in1=amask[:, kb - 4 * qc, :], op=ALU.mult), reads=[AB, amaskB], writes=[AB])
                            m = n - 1
                            if 0 <= m < NB:
                                qc, kb = blocks[m]
                                first, last = kb == 4 * qc + 3, kb == 0
                                A, AB = At[m % 2]
                                ob_ = 6 + qc % 2
                                fw.op(pe, mm(ps[:, ob_, :], V[:, kb, :], A[:], first, last), reads=[VB_, AB], writes=[PSB[ob_]])
                                if last:
                                    fw.op(dve, lambda e: e.tensor_copy(out=OT[:, h, qc * 512:(qc + 1) * 512], in_=ps[:, ob_, :]), reads=[PSB[ob_]], writes=[OTB])
                if stop == ("attn", layer):
                    break
                proj_fm(ph, OT, OTB, 16, a_w_o[layer], D, make_resid_evac(ph, xsrc), "wo")
        else:
            if not kv_done:
                kv_done = True
                with Phase(fw) as ph:
                    kvn, kvnB = ph.tile([128, 16, S], BF16, "kvn")
                    do_norm(xT, 12, kvn, kvnB)
                    if stop == ("kvn", layer):
                        break
                    with Phase(fw) as ph2:
                        kouts = [ph2.tile([128, S], BF16, "kout") for _ in range(2)]
                        stk = {"n": 0}

                        def dst_of(c):
                            return kouts[stk["n"] % 2]

                        def after(c, half, dst, dstB):
                            if half == 1:
                                fw.dma(sp, kTB[c * 128:(c + 1) * 128, :], dst[:], reads=[dstB], owner=dstB)
                                stk["n"] += 1
                        proj_fm(ph2, kvn, kvnB, 16, b_w_kv[:, 0:1536], 1536, make_rotary_evac(ph2, dst_of, after), "wk")
                    if stop == ("kvk", layer):
                        break
                    with Phase(fw) as ph2:
                        wv = [ph2.tile([128, 16, 512], BF16, "wvb") for _ in range(2)]
                        vbig, vbigB = ph2.tile([128, 16, 512], BF16, "vbigb")
                        prm, prmB = ph2.tile([128, 16, 1024], BF16, "perm")
                        nb = 0
                        for g in range(3):
                            dil = (1, 4, 16)[g]
                            Wt, WB = wv[g % 2]
                            fw.dma(pool, Wt[:], kview(b_w_kv[:, 1536 + g * 512:1536 + (g + 1) * 512]), writes=[WB])
                            for half in range(2):
                                if dil == 1:
                                    src, srcB, off = kvn, kvnB, half * 1024
                                else:
                                    for kc in range(16):
                                        eng = dve if kc % 2 == 0 else pool
                                        fw.op(eng, lambda e: e.tensor_copy(
                                            out=prm[:, kc, :].rearrange("p (r i) -> p r i", r=dil // 2),
                                            in_=kvn[:, kc, :].rearrange("p (i r) -> p r i", r=dil)[:, half * (dil // 2):(half + 1) * (dil // 2), :]),
                                            reads=[kvnB], writes=[prmB])
                                    src, srcB, off = prm, prmB, 0
                                for tb in range(8):
                                    bk = 4 + nb % 4
                                    nb += 1
                                    for kc in range(16):
                                        fw.op(pe, mm(ps[:, bk, :], src[:, kc, off + tb * 128:off + (tb + 1) * 128], Wt[:, kc, :], kc == 0, kc == 15),
                                              reads=[WB, srcB], writes=[PSB[bk]], inc=(kc == 15))
                                    if tb % 2 == 0:
                                        fw.op(act, lambda e: e.activation(out=vbig[:, half * 8 + tb, :], in_=ps[:, bk, :], func=AF.Copy), reads=[PSB[bk]], writes=[vbigB])
                                    else:
                                        fw.op(dve, lambda e: e.tensor_copy(out=vbig[:, half * 8 + tb, :], in_=ps[:, bk, :]), reads=[PSB[bk]], writes=[vbigB])
                            fw.dma(sp, VB[g].rearrange("(tb tp) f -> tp tb f", tp=128), vbig[:], reads=[vbigB], owner=vbigB)
            if stop == ("kv", layer):
                break
            j = layer - 2
            with Phase(fw) as ph:
                qTB, qTBB = ph.tile([128, 12, S], BF16, "qTB")
                with Phase(fw) as ph2:
                    hT, hTB = ph2.tile([128, 16, S], BF16, "hTb")
                    do_norm(xT, layer, hT, hTB)
                    proj_fm(ph2, hT, hTB, 16, b_w_q[j], 1536, make_rotary_evac(ph2, lambda c: (qTB[:, c, :], qTBB)), "wq")
                if stop == ("q", layer):
                    break
                OTb, OTbB = ph.tile([128, 12, S], BF16, "OTb")
                with Phase(fw) as ph2:
                    kts = [ph2.tile([128, S], BF16, "kt") for _ in range(2)]
                    vts = [ph2.tile([128, 16, 128], BF16, "vt") for _ in range(2)]
                    Es = [ph2.tile([128, 1024], BF16, "E") for _ in range(2)]
                    Ng = [ph2.tile([128, S], F32, "Ng") for _ in range(3)]
                    Zs, ZsB = ph2.tile([128, S], F32, "Zs")
                    seq = [(hg, g) for hg in range(4) for g in range(3)]

                    def load_kv(i):
                        hg, g = seq[i]
                        c = g * 4 + hg
                        fw.dma(sp, kts[i % 2][0][:], kTB[c * 128:(c + 1) * 128, :], writes=[kts[i % 2][1]])
                        fw.dma(sp, vts[i % 2][0][:], VB[g].rearrange("(kb kp) f -> kp kb f", kp=128)[:, :, hg * 128:(hg + 1) * 128], writes=[vts[i % 2][1]])
                    load_kv(0)
                    nsb = 0
                    for i, (hg, g) in enumerate(seq):
                        if i + 1 < len(seq):
                            load_kv(i + 1)
                        c = g * 4 + hg
                        dil = (1, 4, 16)[g]
                        sub = S // dil
                        Kt, KtB = kts[i % 2]
                        Vt, VtB = vts[i % 2]
                        N_, NB_ = Ng[g]
                        for sb in range(4):
                            k2 = nsb % 2
                            nsb += 1
                            sbk = 2 * k2
                            nbk, zbk = 4 + 2 * k2, 5 + 2 * k2
                            E, EB = Es[k2]
                            nfirst = []
                            for ql in range(4):
                                pos = sb * 512 + ql * 128
                                nblk = (pos % sub) // 128
                                nfirst.append(nblk == 0)
                                pprev = pos if nblk == 0 else pos - 128
                                col = ql * 256
                                bank, off = sbk + col // 512, col % 512
                                fw.op(pe, mm(ps[:, bank, off:off + 128], Kt[:, pprev:pprev + 128], qTB[:, c, pos:pos + 128], True, True),
                                      reads=[KtB, qTBB], writes=[PSB[bank]], inc=False)
                                fw.op(pe, mm(ps[:, bank, off + 128:off + 256], Kt[:, pos:pos + 128], qTB[:, c, pos:pos + 128], True, True),
                                      reads=[KtB, qTBB], writes=[PSB[bank]], inc=(ql == 3))
                            fw.op(act, lambda e: e.activation(out=v1024(E[:]), in_=ps[:, sbk:sbk + 2, :], func=AF.Exp, scale=QSCALE), reads=[PSB[sbk], PSB[sbk + 1]], writes=[EB])
                            mv = 2 if all(nfirst) else (0 if nfirst[0] else 1)
                            fw.op(pool, lambda e: e.tensor_tensor(out=E[:], in0=E[:], in1=bmask[:, mv, :], op=ALU.mult), reads=[EB, bmaskB], writes=[EB])
                            for (bank_, lhs_of) in ((nbk, lambda kbi: Vt[:, kbi, :]), (zbk, lambda kbi: ones_b[:])):
                                for ql in range(4):
                                    pos = sb * 512 + ql * 128
                                    kbi = pos // 128
                                    col = ql * 256
                                    if not nfirst[ql]:
                                        fw.op(pe, mm(ps[:, bank_, ql * 128:(ql + 1) * 128], lhs_of(kbi - 1), E[:, col:col + 128], True, False),
                                              reads=[EB, VtB, onesB], writes=[PSB[bank_]], inc=False)
                                    fw.op(pe, mm(ps[:, bank_, ql * 128:(ql + 1) * 128], lhs_of(kbi), E[:, col + 128:col + 256], nfirst[ql], True),
                                          reads=[EB, VtB, onesB], writes=[PSB[bank_]], inc=(ql == 3))

                            def tview(t):
                                if dil == 1:
                                    return t[:, sb * 512:(sb + 1) * 512]
                                if dil == 4:
                                    return t[:, :].rearrange("p (i r) -> p r i", r=4)[:, sb, :]
                                return t[:, :].rearrange("p (i r) -> p r i", r=16)[:, 4 * sb:4 * sb + 4, :]

                            def pview(bank__):
                                if dil == 16:
                                    return ps[:, bank__, :].rearrange("p (r i) -> p r i", r=4)
                                return ps[:, bank__, :]
                            fw.op(act, lambda e: e.activation(out=tview(N_), in_=pview(nbk), func=AF.Copy), reads=[PSB[nbk]], writes=[NB_])
                            if g == 0:
                                fw.op(dve, lambda e: e.tensor_copy(out=tview(Zs), in_=pview(zbk)), reads=[PSB[zbk]], writes=[ZsB])
                            else:
                                fw.op(dve, lambda e: e.tensor_tensor(out=tview(Zs), in0=tview(Zs), in1=pview(zbk), op=ALU.add), reads=[PSB[zbk], ZsB], writes=[ZsB])
                        if g == 2:
                            fw.op(dve, lambda e: e.reciprocal(out=Zs[:], in_=Zs[:]), reads=[ZsB], writes=[ZsB])
                            for g2 in range(3):
                                fw.op(dve, lambda e: e.tensor_tensor(out=OTb[:, g2 * 4 + hg, :], in0=Ng[g2][0][:], in1=Zs[:], op=ALU.mult),
                                      reads=[Ng[g2][1], ZsB], writes=[OTbB])
                if dbg:
                    dO = nc.dram_tensor("dbg_OTb%d" % layer, [1536, S], BF16, kind="ExternalOutput").ap()
                    fw.dma(sp, dO.rearrange("(c p) t -> p c t", p=128), OTb[:], reads=[OTbB], owner=OTbB)
                if stop == ("attn", layer):
                    break
                proj_fm(ph, OTb, OTbB, 12, b_w_o[j], D, make_resid_evac(ph, xT), "wob")
        xsrc = xT
        snap("dbg_attn%d" % layer)
        if stop == ("mix", layer):
            break
        with Phase(fw) as ph:
            hT, hTB = ph.tile([128, 16, S], BF16, "h2T")
            do_norm(xT, 4 + layer, hT, hTB)
            wg = [ph.tile([128, 16, 256], BF16, "wg") for _ in range(2)]
            wvv = [ph.tile([128, 16, 256], BF16, "wvv") for _ in range(2)]
            Ug = [ph.tile([128, S + 2], F32, "Ug") for _ in range(2)]
            Uv = [ph.tile([128, S + 2], F32, "Uv") for _ in range(2)]
            accg, accgB = ph.tile([128, S], F32, "accg")
            accv, accvB = ph.tile([128, S], F32, "accv")
            aout = [ph.tile([128, S], BF16, "aout") for _ in range(2)]
            for U, UB in Ug + Uv:
                fw.op(dve, lambda e: e.memset(U[:, 0:2], 0.0), writes=[UB])
            Wup = ffn_w_up[layer]

            def load_w(t):
                fw.dma(pool, wg[t % 2][0][:], kview(Wup)[:, :, t * 256:(t + 1) * 256], writes=[wg[t % 2][1]])
                fw.dma(pool, wvv[t % 2][0][:], kview(Wup)[:, :, DFF + t * 256:DFF + (t + 1) * 256], writes=[wvv[t % 2][1]])
            load_w(0)
            for jf in range(NFC):
                t, jj = jf // 2, jf % 2
                if jj == 0 and t + 1 < NFC // 2:
                    load_w(t + 1)
                WG, WGB = wg[t % 2]
                WV, WVB = wvv[t % 2]
                UG, UGB = Ug[jf % 2]
                UV, UVB = Uv[jf % 2]
                cs = slice(jj * 128, (jj + 1) * 128)
                for half in range(2):
                    b0 = 4 * half
                    for kc in range(16):
                        for t2 in range(2):
                            tcs = slice((half * 2 + t2) * 512, (half * 2 + t2 + 1) * 512)
                            fw.op(pe, mm(ps[:, b0 + t2, :], WG[:, kc, cs], hT[:, kc, tcs], kc == 0, kc == 15), reads=[WGB, hTB], writes=[PSB[b0 + t2]], inc=False)
                            fw.op(pe, mm(ps[:, b0 + 2 + t2, :], WV[:, kc, cs], hT[:, kc, tcs], kc == 0, kc == 15), reads=[WVB, hTB], writes=[PSB[b0 + 2 + t2]],
                                  inc=(kc == 15 and t2 == 1))
                    us = slice(2 + half * 1024, 2 + (half + 1) * 1024)
                    fw.op(act, lambda e: e.activation(out=v1024(UG[:, us]), in_=ps[:, b0:b0 + 2, :], func=AF.Copy), reads=[PSB[b0], PSB[b0 + 1]], writes=[UGB])
                    fw.op(act, lambda e: e.activation(out=v1024(UV[:, us]), in_=ps[:, b0 + 2:b0 + 4, :], func=AF.Copy), reads=[PSB[b0 + 2], PSB[b0 + 3]], writes=[UVB])
                for (U, UB, acc, accB, ci) in ((UG, UGB, accg, accgB, jf), (UV, UVB, accv, accvB, NFC + jf)):
                    fw.op(act, lambda e: e.activation(out=acc[:], in_=U[:, 2:S + 2], func=AF.Identity, scale=convw[:, layer, 2, ci:ci + 1], bias=convb[:, layer, ci:ci + 1]),
                          reads=[UB, convwB, convbB], writes=[accB])
                    fw.op(dve, lambda e: e.scalar_tensor_tensor(out=acc[:], in0=U[:, 1:S + 1], scalar=convw[:, layer, 1, ci:ci + 1], in1=acc[:], op0=ALU.mult, op1=ALU.add),
                          reads=[UB, convwB, accB], writes=[accB])
                    fw.op(dve, lambda e: e.scalar_tensor_tensor(out=acc[:], in0=U[:, 0:S], scalar=convw[:, layer, 0, ci:ci + 1], in1=acc[:], op0=ALU.mult, op1=ALU.add),
                          reads=[UB, convwB, accB], writes=[accB])
                fw.op(act, lambda e: e.activation(out=accg[:], in_=accg[:], func=AF.Silu), reads=[accgB], writes=[accgB])
                AO, AOB = aout[jf % 2]
                fw.op(dve, lambda e: e.tensor_tensor(out=AO[:], in0=accg[:], in1=accv[:], op=ALU.mult), reads=[accgB, accvB], writes=[AOB])
                fw.dma(sp, AT[jf * 128:(jf + 1) * 128, :], AO[:], reads=[AOB], owner=AOB)
        with Phase(fw) as ph:
            wd = [ph.tile([128, 22, 512], BF16, "wd") for _ in range(4)]
            at = [ph.tile([128, 22, 512], BF16, "at") for _ in range(2)]
            xt4 = [ph.tile([128, 4, 512], F32, "xt4") for _ in range(2)]
            Wdn = ffn_w_down[layer]
            ATv = AT.rearrange("(fc fp) t -> fp fc t", fp=128)
            xTv = xT.rearrange("(dc dp) t -> dp dc t", dp=128)

            def load_wd(g4):
                for kh in range(2):
                    W_, WB_ = wd[(g4 % 2) * 2 + kh]
                    fw.dma(pool, W_[:], kview(Wdn)[:, kh * 22:(kh + 1) * 22, g4 * 512:(g4 + 1) * 512], writes=[WB_])

            def load_at(tc, kh):
                fw.dma(sp, at[kh][0][:], ATv[:, kh * 22:(kh + 1) * 22, tc * 512:(tc + 1) * 512], writes=[at[kh][1]])
            load_wd(0)
            load_at(0, 0)
            load_at(0, 1)
            step = 0
            for g4 in range(4):
                if g4 + 1 < 4:
                    load_wd(g4 + 1)
                for tc in range(4):
                    b0 = 4 * (step % 2)
                    X4, X4B = xt4[step % 2]
                    fw.dma(sp, X4[:], xTv[:, g4 * 4:(g4 + 1) * 4, tc * 512:(tc + 1) * 512], writes=[X4B])
                    for kh in range(2):
                        W_, WB_ = wd[(g4 % 2) * 2 + kh]
                        A_, AB_ = at[kh]
                        for fc in range(22):
                            for dc in range(4):
                                fw.op(pe, mm(ps[:, b0 + dc, :], W_[:, fc, dc * 128:(dc + 1) * 128], A_[:, fc, :], kh == 0 and fc == 0, kh == 1 and fc == 21),
                                      reads=[WB_, AB_], writes=[PSB[b0 + dc]], inc=(fc == 21 and dc == 3))
                        nxt = step + 1
                        if nxt < 16:
                            load_at(nxt % 4, kh)
                    fw.op(dve, lambda e: e.tensor_tensor(out=X4[:], in0=X4[:], in1=ps[:, b0:b0 + 4, :], op=ALU.add), reads=[X4B] + PSB[b0:b0 + 4], writes=[X4B])
                    fw.dma(sp, xTv[:, g4 * 4:(g4 + 1) * 4, tc * 512:(tc + 1) * 512], X4[:], reads=[X4B], owner=X4B)
                    step += 1
        snap("dbg_ffn%d" % layer)
        if stop == ("ffn", layer):
            break
        with Phase(fw) as ph:
            hT, hTB = ph.tile([128, 16, S], BF16, "h3T")
            do_norm(xT, 8 + layer, hT, hTB)
            pTb, pTbB = ph.tile([128, 2, S], BF16, "pTb")
            fw.dma(pool, pTb[:], kview(pT[layer]), writes=[pTbB])
            wgt = [ph.tile([128, 16, 512], BF16, "wgt") for _ in range(2)]
            wpj = [ph.tile([128, 2, 512], BF16, "wpj") for _ in range(2)]
            sgs = [ph.tile([128, 1024], F32, "sg") for _ in range(2)]
            xts = [ph.tile([128, S], F32, "xp") for _ in range(2)]

            def load_g(g4):
                fw.dma(pool, wgt[g4 % 2][0][:], kview(ple_w_gate[layer])[:, :, g4 * 512:(g4 + 1) * 512], writes=[wgt[g4 % 2][1]])
                fw.dma(pool, wpj[g4 % 2][0][:], kview(ple_w_proj[layer])[:, :, g4 * 512:(g4 + 1) * 512], writes=[wpj[g4 % 2][1]])
            load_g(0)
            cnt = 0
            for g4 in range(4):
                if g4 + 1 < 4:
                    load_g(g4 + 1)
                WG, WGB = wgt[g4 % 2]
                WP, WPB = wpj[g4 % 2]
                for dcl in range(4):
                    dc = g4 * 4 + dcl
                    X, XB = xts[dc % 2]
                    fw.dma(sp, X[:], xT[dc * 128:(dc + 1) * 128, :], writes=[XB])
                    cs = slice(dcl * 128, (dcl + 1) * 128)
                    for half in range(2):
                        b0 = 4 * (cnt % 2)
                        SG, SGB = sgs[cnt % 2]
                        cnt += 1
                        for kc in range(16):
                            for t2 in range(2):
                                tcs = slice((half * 2 + t2) * 512, (half * 2 + t2 + 1) * 512)
                                fw.op(pe, mm(ps[:, b0 + t2, :], WG[:, kc, cs], hT[:, kc, tcs], kc == 0, kc == 15), reads=[WGB, hTB], writes=[PSB[b0 + t2]], inc=False)
                        for kc in range(2):
                            for t2 in range(2):
                                tcs = slice((half * 2 + t2) * 512, (half * 2 + t2 + 1) * 512)
                                fw.op(pe, mm(ps[:, b0 + 2 + t2, :], WP[:, kc, cs], pTb[:, kc, tcs], kc == 0, kc == 1), reads=[WPB, pTbB], writes=[PSB[b0 + 2 + t2]],
                                      inc=(kc == 1 and t2 == 1))
                        hs = slice(half * 1024, (half + 1) * 1024)
                        fw.op(act, lambda e: e.activation(out=v1024(SG[:]), in_=ps[:, b0:b0 + 2, :], func=AF.Sigmoid), reads=[PSB[b0], PSB[b0 + 1]], writes=[SGB])
                        fw.op(dve, lambda e: e.tensor_tensor(out=v1024(SG[:]), in0=v1024(SG[:]), in1=ps[:, b0 + 2:b0 + 4, :], op=ALU.mult),
                              reads=[SGB, PSB[b0 + 2], PSB[b0 + 3]], writes=[SGB])
                        fw.op(pool, lambda e: e.tensor_tensor(out=X[:, hs], in0=X[:, hs], in1=SG[:], op=ALU.add), reads=[SGB, XB], writes=[XB])
                    fw.dma(sp, xT[dc * 128:(dc + 1) * 128, :], X[:], reads=[XB], owner=XB)
        snap("dbg_ple%d" % layer)
        if stop == ("ple", layer):
            break
    else:
        do_norm(xT, 13, out_dram=yT)
    fw.barrier()
    fw.close()
    return nc


def _consts():
    j = np.arange(128)
    tri = -(j[:, None] >= j[None, :]).astype(np.float32)
    ones = np.ones((128, 128), np.float32)
    t = np.arange(512)
    amask = np.stack([((128 * i + j)[:, None] < t[None, :]).astype(np.float32) for i in range(4)], axis=1)
    up = (j[:, None] >= j[None, :]).astype(np.float32)
    lo = (j[:, None] <= j[None, :]).astype(np.float32)
    F_ = np.concatenate([np.zeros((128, 128), np.float32), lo], 1)
    R_ = np.concatenate([up, lo], 1)
    bmask = np.stack([np.concatenate([F_, R_, R_, R_], 1), np.concatenate([R_] * 4, 1), np.concatenate([F_] * 4, 1)], axis=1)
    sel33 = np.zeros((33, 128), np.float32)
    sel33[0] = -1.0
    sel33[32] = -1.0
    perm = np.zeros((32, 32), np.float32)
    for m in range(32):
        if m < 16:
            perm[m + 16, m] = -1.0
        else:
            perm[m - 16, m] = 1.0
    invf = (500000.0 ** (-np.arange(0, 32, 2, dtype=np.float32) / 32)).astype(np.float32)
    invf = np.concatenate([invf, invf]).reshape(32, 1)
    return dict(c_tri=tri, c_ones=ones, c_amask=np.ascontiguousarray(amask.reshape(128, -1)),
                c_bmask=np.ascontiguousarray(bmask.reshape(128, -1)), c_sel33=sel33, c_perm=perm, c_invf=invf)


def _pack(inputs):
    f = lambda a: np.ascontiguousarray(np.asarray(a, dtype=np.float32))
    nw = np.concatenate([np.asarray(inputs["attn_norm_w"]), np.asarray(inputs["ffn_norm_w"]), np.asarray(inputs["ple_norm_w"]),
                         np.asarray(inputs["kv_norm_w"])[None], np.asarray(inputs["final_norm_w"])[None]], 0)
    normw = f(nw.reshape(14, 16, 128).transpose(2, 0, 1).reshape(128, -1))
    convw = f(np.asarray(inputs["ffn_conv_w"]).reshape(4, 3, 88, 128).transpose(3, 0, 1, 2).reshape(128, -1))
    convb = f(np.asarray(inputs["ffn_conv_b"]).reshape(4, 88, 128).transpose(2, 0, 1).reshape(128, -1))
    shared = dict(normw=normw, convw=convw, convb=convb)
    for k in ("a_w_qkv", "a_w_o", "b_w_kv", "b_w_q", "b_w_o", "ffn_w_up", "ffn_w_down", "ple_w_gate", "ple_w_proj"):
        shared[k] = f(inputs[k])
    shared.update(_consts())
    return shared


def _core_inputs(inputs, shared, b):
    m = dict(shared)
    m["xT_in"] = np.ascontiguousarray(np.asarray(inputs["x"][b], dtype=np.float32).T)
    m["pT"] = np.ascontiguousarray(np.asarray(inputs["p"][:, b], dtype=np.float32).transpose(0, 2, 1))
    m["pos32"] = np.ascontiguousarray(np.broadcast_to(np.asarray(inputs["positions"][b], dtype=np.int32)[None, :], (32, S)))
    return m


_NC = None


def kernel(**inputs):
    global _NC
    if _NC is None:
        _NC = build_program()
    shared = _pack(inputs)
    nb = inputs["x"].shape[0]
    in_maps = [_core_inputs(inputs, shared, b) for b in range(nb)]
    res = run_bass_kernel_spmd(_NC, in_maps, core_ids=list(range(nb)))
    out = np.stack([np.asarray(r["yT"]).T for r in res.results], 0)
    return np.ascontiguousarray(out.astype(np.float32))
```

```python
from contextlib import ExitStack
import numpy as np
import concourse.bass as bass
import concourse.mybir as mybir
from concourse.bass_utils import run_bass_kernel_spmd

F32 = mybir.dt.float32
BF16 = mybir.dt.bfloat16
I32 = mybir.dt.int32
AF = mybir.ActivationFunctionType
ALU = mybir.AluOpType

D = 2048
S = 2048
DFF = 5632
NFC = DFF // 128
QSCALE = 128.0 ** -0.5
PI = float(np.pi)
SNAP = False


class DSem:
    __slots__ = ("sem", "cnt")

    def __init__(self, sem):
        self.sem, self.cnt = sem, 0


class Buf:
    __slots__ = ("name", "w", "r", "ds", "excl")

    def __init__(self, name="", excl=False):
        self.name, self.w, self.r, self.ds, self.excl = name, [], [], None, excl


class Eng:
    def __init__(self, name, obj, sem):
        self.name, self.obj, self.sem = name, obj, sem
        self.cnt = 0
        self.pending = False
        self.seen = {}

    def wait(self, tok):
        sem, val = tok
        if sem is self.sem and self.name == "pe":
            return
        k = id(sem)
        if self.seen.get(k, 0) >= val:
            return
        self.seen[k] = val
        self.obj.wait_ge(sem, val)


class FW:
    def __init__(self, nc):
        self.nc = nc
        self._cms = []
        self.pe = Eng("pe", nc.tensor, self._sem("s_pe"))
        self.act = Eng("act", nc.scalar, self._sem("s_act"))
        self.dve = Eng("dve", nc.vector, self._sem("s_dve"))
        self.pool = Eng("pool", nc.gpsimd, self._sem("s_pool"))
        self.sp = Eng("sp", nc.sync, self._sem("s_sp"))
        self.engs = [self.pe, self.act, self.dve, self.pool, self.sp]
        self.free_ds, self.all_ds, self.live = [], [], []
        self.uid = 0

    def _sem(self, name):
        cm = self.nc.semaphore(name)
        self._cms.append(cm)
        return cm.__enter__()

    def close(self):
        for cm in reversed(self._cms):
            cm.__exit__(None, None, None)

    def _ds(self, buf):
        if buf.ds is None:
            if self.free_ds:
                buf.ds = self.free_ds.pop()
            else:
                buf.ds = DSem(self._sem("d%d" % len(self.all_ds)))
                self.all_ds.append(buf.ds)
            self.live.append(buf)
        return buf.ds

    def _deps(self, eng, reads, writes):
        for b in reads:
            for t in b.w:
                eng.wait(t)
            if b.excl:
                for t in b.r:
                    if t[0] is not eng.sem:
                        eng.wait(t)
        for b in writes:
            for t in b.w:
                eng.wait(t)
            for t in b.r:
                eng.wait(t)

    def _mark(self, tok, reads, writes):
        for b in reads:
            b.r.append(tok)
            if len(b.r) > 10:
                best = {}
                for s_, v in b.r:
                    if id(s_) not in best or best[id(s_)][1] < v:
                        best[id(s_)] = (s_, v)
                b.r = list(best.values())
        for b in writes:
            b.w = [tok]
            b.r = []

    def op(self, eng, fn, reads=(), writes=(), inc=True):
        self._deps(eng, reads, writes)
        ins = fn(eng.obj)
        if inc:
            eng.cnt += 1
            ins.then_inc(eng.sem, 1)
            eng.pending = False
            tok = (eng.sem, eng.cnt)
        else:
            eng.pending = True
            tok = (eng.sem, eng.cnt + 1)
        self._mark(tok, reads, writes)
        return ins

    def dma(self, q, out_ap, in_ap, reads=(), writes=(), owner=None):
        owner = owner or (writes[0] if writes else reads[0])
        ds = self._ds(owner)
        self._deps(q, reads, writes)
        ins = q.obj.dma_start(out=out_ap, in_=in_ap)
        ds.cnt += 16
        ins.then_inc(ds.sem, 16)
        self._mark((ds.sem, ds.cnt), reads, writes)
        return ins

    def barrier(self):
        assert not self.pe.pending
        toks = [(e.sem, e.cnt) for e in self.engs if e.cnt > 0]
        toks += [(d.sem, d.cnt) for d in self.all_ds if d.cnt > 0]
        for e in self.engs:
            for t in toks:
                e.wait(t)
        for b in self.live:
            self.free_ds.append(b.ds)
            b.ds = None
        self.live = []


class Phase:
    def __init__(self, fw):
        self.fw, self.es = fw, ExitStack()

    def __enter__(self):
        return self

    def tile(self, shape, dtype, name="t"):
        self.fw.uid += 1
        t = self.es.enter_context(self.fw.nc.sbuf_tensor("%s_%d" % (name, self.fw.uid), list(shape), dtype))
        return t, Buf(name)

    def __exit__(self, *a):
        self.fw.barrier()
        self.es.close()
        return False


def build_program(nlayers=4, dbg=False, stop=None, start=0, lite=()):
    nc = bass.Bass("TRN2", target_bir_lowering=False)
    fw = FW(nc)
    pe, act, dve, pool, sp = fw.pe, fw.act, fw.dve, fw.pool, fw.sp

    def din(name, shape, dt=F32):
        if name in lite:
            return nc.dram_tensor(name, [1] * len(shape), dt, kind="Internal").ap()
        return nc.dram_tensor(name, list(shape), dt, kind="ExternalInput").ap()

    def dscr(name, shape, dt):
        return nc.dram_tensor(name, list(shape), dt, kind=("ExternalOutput" if dbg else "Internal")).ap()

    xT_in = din("xT_in", [D, S])
    pT = din("pT", [4, 256, S])
    pos32 = din("pos32", [32, S], I32)
    normw_d = din("normw", [128, 14 * 16])
    convw_d = din("convw", [128, 4 * 3 * 88])
    convb_d = din("convb", [128, 4 * 88])
    a_w_qkv = din("a_w_qkv", [2, D, 3 * D])
    a_w_o = din("a_w_o", [2, D, D])
    b_w_kv = din("b_w_kv", [D, 3072])
    b_w_q = din("b_w_q", [2, D, 1536])
    b_w_o = din("b_w_o", [2, 1536, D])
    ffn_w_up = din("ffn_w_up", [4, D, 2 * DFF])
    ffn_w_down = din("ffn_w_down", [4, DFF, D])
    ple_w_gate = din("ple_w_gate", [4, D, D])
    ple_w_proj = din("ple_w_proj", [4, 256, D])
    c_tri = din("c_tri", [128, 128])
    c_ones = din("c_ones", [128, 128])
    c_amask = din("c_amask", [128, 4 * 512])
    c_bmask = din("c_bmask", [128, 3 * 1024])
    c_sel33 = din("c_sel33", [33, 128])
    c_perm = din("c_perm", [32, 32])
    c_invf = din("c_invf", [32, 1])
    yT = nc.dram_tensor("yT", [D, S], F32, kind="ExternalOutput").ap()

    xT = dscr("xT_s", [D, S], F32)
    qT = dscr("qT_s", [D, S], BF16)
    kT = dscr("kT_s", [D, S], BF16)
    Vd = dscr("V_s", [S, D], BF16)
    AT = dscr("AT_s", [DFF, S], BF16)
    kTB = dscr("kTB_s", [1536, S], BF16)
    VB = dscr("VB_s", [3, S, 512], BF16)

    def ptile(name, shape, dt):
        return nc.alloc_sbuf_tensor(name, list(shape), dt), Buf(name)

    ones_b, onesB = ptile("ones_b", [128, 128], BF16)
    tri_b, triB = ptile("tri_b", [128, 128], BF16)
    amask, amaskB = ptile("amask", [128, 4, 512], BF16)
    bmask, bmaskB = ptile("bmask", [128, 3, 1024], BF16)
    sel33, selB = ptile("sel33", [33, 128], BF16)
    perm32, permB = ptile("perm32", [32, 32], BF16)
    invf, invfB = ptile("invf", [32, 1], F32)
    normw, normwB = ptile("normw_t", [128, 14, 16], F32)
    convw, convwB = ptile("convw_t", [128, 4, 3, 88], F32)
    convb, convbB = ptile("convb_t", [128, 4, 88], F32)
    COS, cosB = ptile("COS", [32, S], F32)
    SIN, sinB = ptile("SIN", [32, S], F32)
    ps = nc.alloc_psum_tensor("ps", [128, 8, 512], F32)
    PSB = [Buf("ps%d" % i, excl=True) for i in range(8)]

    with Phase(fw) as ph:
        fw.dma(pool, ones_b[:], c_ones, writes=[onesB])
        fw.dma(pool, tri_b[:], c_tri, writes=[triB])
        fw.dma(pool, amask[:], c_amask.rearrange("p (i t) -> p i t", i=4), writes=[amaskB])
        fw.dma(pool, bmask[:], c_bmask.rearrange("p (i t) -> p i t", i=3), writes=[bmaskB])
        fw.dma(pool, sel33[:], c_sel33, writes=[selB])
        fw.dma(pool, perm32[:], c_perm, writes=[permB])
        fw.dma(sp, invf[:], c_invf, writes=[invfB])
        fw.dma(sp, normw[:], normw_d.rearrange("p (a b) -> p a b", a=14), writes=[normwB])
        fw.dma(sp, convw[:], convw_d.rearrange("p (a b c) -> p a b c", a=4, b=3), writes=[convwB])
        fw.dma(sp, convb[:], convb_d.rearrange("p (a b) -> p a b", a=4), writes=[convbB])
        pi_t, piB = ph.tile([32, S], I32, "posi")
        ang, angB = ph.tile([32, S], F32, "ang")
        tmp, tmpB = ph.tile([32, S], F32, "angt")
        fw.dma(sp, pi_t[:], pos32, writes=[piB])
        fw.op(dve, lambda e: e.tensor_copy(out=ang[:], in_=pi_t[:]), reads=[piB], writes=[angB])
        fw.op(dve, lambda e: e.tensor_scalar(out=ang[:], in0=ang[:], scalar1=invf[:, 0:1], scalar2=None, op0=ALU.mult),
              reads=[angB, invfB], writes=[angB])
        MAGIC = 12582912.0
        C1 = 6.28125
        C2 = 2.0 * PI - C1
        nf, nfB = ph.tile([32, S], F32, "nf")
        for (dst_t, dst_b, shift) in ((SIN, sinB, 0.0), (COS, cosB, 0.5 * PI)):
            fw.op(dve, lambda e: e.tensor_scalar(out=tmp[:], in0=ang[:], scalar1=shift, scalar2=None, op0=ALU.add), reads=[angB], writes=[tmpB])
            fw.op(dve, lambda e: e.tensor_scalar(out=nf[:], in0=tmp[:], scalar1=1.0 / (2.0 * PI), scalar2=MAGIC, op0=ALU.mult, op1=ALU.add), reads=[tmpB], writes=[nfB])
            fw.op(dve, lambda e: e.tensor_scalar(out=nf[:], in0=nf[:], scalar1=MAGIC, scalar2=None, op0=ALU.subtract), reads=[nfB], writes=[nfB])
            fw.op(dve, lambda e: e.scalar_tensor_tensor(out=tmp[:], in0=nf[:], scalar=-C1, in1=tmp[:], op0=ALU.mult, op1=ALU.add), reads=[nfB, tmpB], writes=[tmpB])
            fw.op(dve, lambda e: e.scalar_tensor_tensor(out=tmp[:], in0=nf[:], scalar=-C2, in1=tmp[:], op0=ALU.mult, op1=ALU.add), reads=[nfB, tmpB], writes=[tmpB])
            fw.op(act, lambda e: e.activation(out=dst_t[:], in_=tmp[:], func=AF.Sin), reads=[tmpB], writes=[dst_b])

    snapB = Buf("snap")

    def snap(name):
        if not (dbg and SNAP):
            return
        d_ = nc.dram_tensor(name, [D, S], F32, kind="ExternalOutput").ap()
        fw.barrier()
        fw.dma(sp, d_, xT, owner=snapB)
        fw.barrier()

    def kview(ap2d, rows=128):
        return ap2d.rearrange("(kc kp) f -> kp kc f", kp=rows)

    def mm(out, lhsT, rhs, start, stop):
        return lambda e: e.matmul(out, lhsT=lhsT, rhs=rhs, start=start, stop=stop)

    def do_norm(xsrc, widx, hT=None, hTB=None, out_dram=None):
        NT = 256
        with Phase(fw) as ph:
            xs = [ph.tile([128, 16, NT], F32, "nx") for _ in range(2)]
            sq, sqB = ph.tile([128, 16, NT], BF16, "nsq")
            r1 = [ph.tile([128, NT], F32, "nr1") for _ in range(2)]
            r2 = [ph.tile([128, NT], F32, "nr2") for _ in range(2)]
            ob = [ph.tile([128, 16, NT], F32, "nob") for _ in range(2)] if out_dram is not None else None
            for tc in range(S // NT):
                X, XB = xs[tc % 2]
                tsl = slice(tc * NT, (tc + 1) * NT)
                fw.dma(sp, X[:], kview(xsrc)[:, :, tsl], writes=[XB])
                fw.op(act, lambda e: e.activation(out=sq[:], in_=X[:], func=AF.Square), reads=[XB], writes=[sqB])
                bk = tc % 2
                for kc in range(16):
                    fw.op(pe, mm(ps[:, bk, 0:NT], ones_b[:], sq[:, kc, :], kc == 0, kc == 15), reads=[sqB, onesB], writes=[PSB[bk]], inc=(kc == 15))
                R1, R1B = r1[tc % 2]
                R2, R2B = r2[tc % 2]
                fw.op(act, lambda e: e.activation(out=R1[:], in_=ps[:, bk, 0:NT], func=AF.Sqrt, scale=1.0 / D, bias=1e-6), reads=[PSB[bk]], writes=[R1B])
                fw.op(dve, lambda e: e.reciprocal(out=R2[:], in_=R1[:]), reads=[R1B], writes=[R2B])
                if out_dram is None:
                    for kc in range(16):
                        fw.op(dve, lambda e: e.scalar_tensor_tensor(out=hT[:, kc, tsl], in0=X[:, kc, :], scalar=normw[:, widx, kc:kc + 1], in1=R2[:], op0=ALU.mult, op1=ALU.mult),
                              reads=[XB, R2B, normwB], writes=[hTB])
                else:
                    O, OB = ob[tc % 2]
                    for kc in range(16):
                        fw.op(dve, lambda e: e.scalar_tensor_tensor(out=O[:, kc, :], in0=X[:, kc, :], scalar=normw[:, widx, kc:kc + 1], in1=R2[:], op0=ALU.mult, op1=ALU.mult),
                              reads=[XB, R2B, normwB], writes=[OB])
                    fw.dma(sp, kview(out_dram)[:, :, tsl], O[:], reads=[OB], owner=OB)

    class PSet:
        def __init__(self):
            self.n = 0

        def next(self):
            s = self.n % 2
            self.n += 1
            return s

    def proj_fm(ph, hT, hTB, KCn, W2d, ncols, evac, wname="w"):
        wb = [ph.tile([128, KCn, 512], BF16, wname) for _ in range(2)]
        pset = PSet()
        ng = ncols // 512
        fw.dma(pool, wb[0][0][:], kview(W2d)[:, :, 0:512], writes=[wb[0][1]])
        for g in range(ng):
            Wt, WB = wb[g % 2]
            if g + 1 < ng:
                fw.dma(pool, wb[(g + 1) % 2][0][:], kview(W2d)[:, :, (g + 1) * 512:(g + 2) * 512], writes=[wb[(g + 1) % 2][1]])
            for fc in range(4):
                for half in range(2):
                    s = pset.next()
                    for kc in range(KCn):
                        for t2 in range(2):
                            tcs = slice((half * 2 + t2) * 512, (half * 2 + t2 + 1) * 512)
                            fw.op(pe, mm(ps[:, 2 * s + t2, :], Wt[:, kc, fc * 128:(fc + 1) * 128], hT[:, kc, tcs], kc == 0, kc == KCn - 1),
                                  reads=[WB, hTB], writes=[PSB[2 * s + t2]], inc=(kc == KCn - 1 and t2 == 1))
                    evac(g * 4 + fc, half, s)

    def v1024(ap):
        return ap.rearrange("p (b t) -> p b t", b=2)

    def make_resid_evac(ph, xsrc):
        xts = [ph.tile([128, S], F32, "xr") for _ in range(3)]
        st = {"n": 0}

        def evac(c, half, s):
            X, XB = xts[st["n"] % 3]
            if half == 0:
                fw.dma(sp, X[:], xsrc[c * 128:(c + 1) * 128, :], writes=[XB])
            hs = slice(half * 1024, (half + 1) * 1024)
            fw.op(dve, lambda e: e.tensor_tensor(out=v1024(X[:, hs]), in0=v1024(X[:, hs]), in1=ps[:, 2 * s:2 * s + 2, :], op=ALU.add),
                  reads=[XB, PSB[2 * s], PSB[2 * s + 1]], writes=[XB])
            if half == 1:
                fw.dma(sp, xT[c * 128:(c + 1) * 128, :], X[:], reads=[XB], owner=XB)
                st["n"] += 1
        return evac

    def make_rotary_evac(ph, dst_of, after=None):
        q32s = [ph.tile([32, 1024], BF16, "q32") for _ in range(2)]
        qlos = [ph.tile([32, 1024], BF16, "qlo") for _ in range(2)]
        qfs = [ph.tile([32, 1024], F32, "qf") for _ in range(2)]
        t2s = [ph.tile([32, 1024], F32, "rt2") for _ in range(2)]
        st = {"n": 0}

        def evac(c, half, s):
            k = st["n"] % 2
            st["n"] += 1
            dil = (1, 4, 16)[c // 4]
            ni = 1024 // dil
            dst, dstB = dst_of(c)
            Q, QB = q32s[k]
            T2, T2B = t2s[k]
            hs = slice(half * 1024, (half + 1) * 1024)
            src = [PSB[2 * s], PSB[2 * s + 1]]
            rb = 4 + 2 * k
            QL, QLB = qlos[k]
            QF, QFB = qfs[k]
            fw.op(act, lambda e: e.activation(out=v1024(QF[:]), in_=ps[0:32, 2 * s:2 * s + 2, :], func=AF.Copy), reads=src, writes=[QFB])
            fw.op(dve, lambda e: e.tensor_copy(out=Q[:], in_=QF[:]), reads=[QFB], writes=[QB])
            fw.op(dve, lambda e: e.tensor_tensor(out=QL[:], in0=QF[:], in1=Q[:], op=ALU.subtract), reads=[QFB, QB], writes=[QLB])
            for t2 in range(2):
                fw.op(pe, mm(ps[0:32, rb + t2, :], perm32[:], Q[:, t2 * 512:(t2 + 1) * 512], True, False), reads=[QB, permB], writes=[PSB[rb + t2]], inc=False)
                fw.op(pe, mm(ps[0:32, rb + t2, :], perm32[:], QL[:, t2 * 512:(t2 + 1) * 512], False, True), reads=[QLB, permB], writes=[PSB[rb + t2]], inc=(t2 == 1))
            fw.op(dve, lambda e: e.tensor_tensor(out=QF[:], in0=QF[:], in1=COS[:, hs], op=ALU.mult), reads=[QFB, cosB], writes=[QFB])
            T1, T1B = QF, QFB
            fw.op(dve, lambda e: e.tensor_tensor(out=v1024(T2[:]), in0=ps[0:32, rb:rb + 2, :], in1=v1024(SIN[:, hs]), op=ALU.mult),
                  reads=[PSB[rb], PSB[rb + 1], sinB], writes=[T2B])

            def dview(rows):
                if dil == 1:
                    return dst[rows, hs]
                return dst[rows, :].rearrange("p (r i) -> p r i", r=dil)[:, :, half * ni:(half + 1) * ni]

            def sview(ap):
                if dil == 1:
                    return ap
                return ap.rearrange("p (i r) -> p r i", r=dil)
            all_src = ps[:, 2 * s:2 * s + 2, :].rearrange("p b t -> p (b t)")
            fw.op(act, lambda e: e.activation(out=dview(slice(0, 128)), in_=sview(all_src), func=AF.Copy), reads=src, writes=[dstB])
            fw.op(dve, lambda e: e.tensor_tensor(out=dview(slice(0, 32)), in0=sview(T1[:]), in1=sview(T2[:]), op=ALU.add),
                  reads=[T1B, T2B], writes=[dstB])
            if after is not None:
                after(c, half, dst, dstB)
        return evac

    xsrc = xT_in
    kv_done = False
    if start > 0:
        xT = xT_in
        xsrc = xT
    for layer in range(start, nlayers):
        if layer < 2:
            with Phase(fw) as ph:
                hT, hTB = ph.tile([128, 16, S], BF16, "hT")
                do_norm(xsrc, layer, hT, hTB)
                obs = [ph.tile([128, S], BF16, "qk_o") for _ in range(2)]
                stq = {"n": 0}

                def evac_qk(c, half, s):
                    O, OB = obs[stq["n"] % 2]
                    hs = slice(half * 1024, (half + 1) * 1024)
                    srcb = [PSB[2 * s], PSB[2 * s + 1]]
                    if c < 16:
                        fw.op(act, lambda e: e.activation(out=v1024(O[:, hs]), in_=ps[:, 2 * s:2 * s + 2, :], func=AF.Copy, scale=QSCALE), reads=srcb, writes=[OB])
                    else:
                        fw.op(dve, lambda e: e.tensor_copy(out=v1024(O[:, hs]), in_=ps[:, 2 * s:2 * s + 2, :]), reads=srcb, writes=[OB])
                    if half == 1:
                        dstd = qT if c < 16 else kT
                        cc = c % 16
                        fw.dma(sp, dstd[cc * 128:(cc + 1) * 128, :], O[:], reads=[OB], owner=OB)
                        stq["n"] += 1
                proj_fm(ph, hT, hTB, 16, a_w_qkv[layer][:, 0:2 * D], 2 * D, evac_qk, "wqk")
                wv = [ph.tile([128, 16, 512], BF16, "wv") for _ in range(2)]
                vbig = [ph.tile([128, 16, 512], BF16, "vbig") for _ in range(2)]
                Wv2d = a_w_qkv[layer][:, 2 * D:3 * D]
                fw.dma(pool, wv[0][0][:], kview(Wv2d)[:, :, 0:512], writes=[wv[0][1]])
                nb = 0
                for g in range(4):
                    Wt, WB = wv[g % 2]
                    if g + 1 < 4:
                        fw.dma(pool, wv[(g + 1) % 2][0][:], kview(Wv2d)[:, :, (g + 1) * 512:(g + 2) * 512], writes=[wv[(g + 1) % 2][1]])
                    VBt, VBB = vbig[g % 2]
                    for tb in range(16):
                        bk = 4 + nb % 4
                        nb += 1
                        for kc in range(16):
                            fw.op(pe, mm(ps[:, bk, :], hT[:, kc, tb * 128:(tb + 1) * 128], Wt[:, kc, :], kc == 0, kc == 15), reads=[WB, hTB], writes=[PSB[bk]], inc=(kc == 15))
                        if tb % 2 == 0:
                            fw.op(act, lambda e: e.activation(out=VBt[:, tb, :], in_=ps[:, bk, :], func=AF.Copy), reads=[PSB[bk]], writes=[VBB])
                        else:
                            fw.op(dve, lambda e: e.tensor_copy(out=VBt[:, tb, :], in_=ps[:, bk, :]), reads=[PSB[bk]], writes=[VBB])
                    fw.dma(sp, Vd.rearrange("(tb tp) f -> tp tb f", tp=128)[:, :, g * 512:(g + 1) * 512], VBt[:], reads=[VBB], owner=VBB)
            if stop == ("qkv", layer):
                break
            with Phase(fw) as ph:
                OT, OTB = ph.tile([128, 16, S], BF16, "OT")
                with Phase(fw) as ph2:
                    qh = [ph2.tile([128, S], BF16, "qh") for _ in range(2)]
                    kh = [ph2.tile([128, S], BF16, "kh") for _ in range(2)]
                    vh = [ph2.tile([128, 16, 128], BF16, "vh") for _ in range(2)]
                    eT = [ph2.tile([128, 512], F32, "eT") for _ in range(2)]
                    Lt = [ph2.tile([128, 512], BF16, "Lt") for _ in range(4)]
                    At = [ph2.tile([128, 512], BF16, "At") for _ in range(2)]
                    carry, carryB = ph2.tile([33, 512], F32, "carry")
                    hiT = [ph2.tile([33, 512], BF16, "hiT") for _ in range(3)]

                    def load_head(h):
                        fw.dma(sp, qh[h % 2][0][:], qT[h * 128:(h + 1) * 128, :], writes=[qh[h % 2][1]])
                        fw.dma(sp, kh[h % 2][0][:], kT[h * 128:(h + 1) * 128, :], writes=[kh[h % 2][1]])
                        fw.dma(sp, vh[h % 2][0][:], Vd.rearrange("(sb sp) f -> sp sb f", sp=128)[:, :, h * 128:(h + 1) * 128], writes=[vh[h % 2][1]])
                    load_head(0)
                    blocks = [(qc, kb) for qc in range(4) for kb in range(4 * qc + 3, -1, -1)]
                    NB = len(blocks)
                    for h in range(16):
                        if h + 1 < 16:
                            load_head(h + 1)
                        Q, QB = qh[h % 2]
                        K, KB = kh[h % 2]
                        V, VB_ = vh[h % 2]
                        for n in range(-3, NB + 2):
                            m = n + 3
                            if 0 <= m < NB:
                                qc, kb = blocks[m]
                                fw.op(pe, mm(ps[:, m % 4, :], K[:, kb * 128:(kb + 1) * 128], Q[:, qc * 512:(qc + 1) * 512], True, False),
                                      reads=[KB, QB], writes=[PSB[m % 4]])
                            m = n + 2
                            if 0 <= m < NB:
                                qc, kb = blocks[m]
                                diag = kb >= 4 * qc
                                E, EB = eT[m % 2]
                                L, LB = Lt[m % 4]
                                fw.op(act, lambda e: e.activation(out=E[:], in_=ps[:, m % 4, :], func=AF.Exp), reads=[PSB[m % 4]], writes=[EB])
                                fw.op(act, lambda e: e.activation(out=L[:], in_=E[:], func=AF.Ln, bias=1.0), reads=[EB], writes=[LB])
                                if diag:
                                    fw.op(pool, lambda e: e.tensor_tensor(out=L[:], in0=L[:], in1=amask[:, kb - 4 * qc, :], op=ALU.mult), reads=[LB, amaskB], writes=[LB])
                            m = n + 1
                            if 0 <= m < NB:
                                qc, kb = blocks[m]
                                first, last = kb == 4 * qc + 3, kb == 0
                                L, LB = Lt[m % 4]
                                if not last:
                                    bb = 4 + m % 2
                                    H, HB = hiT[m % 3]
                                    fw.op(pe, mm(ps[0:33, bb, :], ones_b[:, 0:33], L[:], True, True), reads=[LB, onesB], writes=[PSB[bb]])
                                    if first:
                                        fw.op(dve, lambda e: e.tensor_copy(out=carry[:], in_=ps[0:33, bb, :]), reads=[PSB[bb]], writes=[carryB])
                                    else:
                                        fw.op(dve, lambda e: e.tensor_tensor(out=carry[:], in0=carry[:], in1=ps[0:33, bb, :], op=ALU.add), reads=[PSB[bb], carryB], writes=[carryB])
                                    fw.op(dve, lambda e: e.tensor_copy(out=H[:], in_=carry[:]), reads=[carryB], writes=[HB])
                                    fw.op(dve, lambda e: e.tensor_tensor(out=H[32:33, :], in0=carry[32:33, :], in1=H[32:33, :], op=ALU.subtract), reads=[carryB, HB], writes=[HB])
                            if 0 <= n < NB:
                                qc, kb = blocks[n]
                                first, diag = kb == 4 * qc + 3, kb >= 4 * qc
                                L, LB = Lt[n % 4]
                                fw.op(pe, mm(ps[:, n % 4, :], tri_b[:], L[:], False, first), reads=[LB, triB], writes=[PSB[n % 4]], inc=first)
                                if not first:
                                    H, HB = hiT[(n - 1) % 3]
                                    fw.op(pe, mm(ps[:, n % 4, :], sel33[:], H[:], False, True), reads=[HB, selB], writes=[PSB[n % 4]])
                                A, AB = At[n % 2]
                                fw.op(act, lambda e: e.activation(out=A[:], in_=ps[:, n % 4, :], func=AF.Exp), reads=[PSB[n % 4]], writes=[AB])
                                if diag:
                                    fw.op(pool, lambda e: e.tensor_tensor(out=A[:], in0=A[:], in1=amask[:, kb - 4 * qc, :], op=ALU.mult), reads=[AB, amaskB], writes=[AB])
                            m = n - 1
                            if 0 <= m < NB:
                                qc, kb = blocks[m]
                                first, last = kb == 4 * qc + 3, kb == 0
                                A, AB = At[m % 2]
                                ob_ = 6 + qc % 2
                                fw.op(pe, mm(ps[:, ob_, :], V[:, kb, :], A[:], first, last), reads=[VB_, AB], writes=[PSB[ob_]])
                                if last:
                                    fw.op(dve, lambda e: e.tensor_copy(out=OT[:, h, qc * 512:(qc + 1) * 512], in_=ps[:, ob_, :]), reads=[PSB[ob_]], writes=[OTB])
                if stop == ("attn", layer):
                    break
                proj_fm(ph, OT, OTB, 16, a_w_o[layer], D, make_resid_evac(ph, xsrc), "wo")
        else:
            if not kv_done:
                kv_done = True
                with Phase(fw) as ph:
                    kvn, kvnB = ph.tile([128, 16, S], BF16, "kvn")
                    do_norm(xT, 12, kvn, kvnB)
                    if stop == ("kvn", layer):
                        break
                    with Phase(fw) as ph2:
                        kouts = [ph2.tile([128, S], BF16, "kout") for _ in range(2)]
                        stk = {"n": 0}

                        def dst_of(c):
                            return kouts[stk["n"] % 2]

                        def after(c, half, dst, dstB):
                            if half == 1:
                                fw.dma(sp, kTB[c * 128:(c + 1) * 128, :], dst[:], reads=[dstB], owner=dstB)
                                stk["n"] += 1
                        proj_fm(ph2, kvn, kvnB, 16, b_w_kv[:, 0:1536], 1536, make_rotary_evac(ph2, dst_of, after), "wk")
                    if stop == ("kvk", layer):
                        break
                    with Phase(fw) as ph2:
                        wv = [ph2.tile([128, 16, 512], BF16, "wvb") for _ in range(2)]
                        vbig, vbigB = ph2.tile([128, 16, 512], BF16, "vbigb")
                        prm, prmB = ph2.tile([128, 16, 1024], BF16, "perm")
                        nb = 0
                        for g in range(3):
                            dil = (1, 4, 16)[g]
                            Wt, WB = wv[g % 2]
                            fw.dma(pool, Wt[:], kview(b_w_kv[:, 1536 + g * 512:1536 + (g + 1) * 512]), writes=[WB])
                            for half in range(2):
                                if dil == 1:
                                    src, srcB, off = kvn, kvnB, half * 1024
                                else:
                                    for kc in range(16):
                                        eng = dve if kc % 2 == 0 else pool
                                        fw.op(eng, lambda e: e.tensor_copy(
                                            out=prm[:, kc, :].rearrange("p (r i) -> p r i", r=dil // 2),
                                            in_=kvn[:, kc, :].rearrange("p (i r) -> p r i", r=dil)[:, half * (dil // 2):(half + 1) * (dil // 2), :]),
                                            reads=[kvnB], writes=[prmB])
                                    src, srcB, off = prm, prmB, 0
                                for tb in range(8):
                                    bk = 4 + nb % 4
                                    nb += 1
                                    for kc in range(16):
                                        fw.op(pe, mm(ps[:, bk, :], src[:, kc, off + tb * 128:off + (tb + 1) * 128], Wt[:, kc, :], kc == 0, kc == 15),
                                              reads=[WB, srcB], writes=[PSB[bk]], inc=(kc == 15))
                                    if tb % 2 == 0:
                                        fw.op(act, lambda e: e.activation(out=vbig[:, half * 8 + tb, :], in_=ps[:, bk, :], func=AF.Copy), reads=[PSB[bk]], writes=[vbigB])
                                    else:
                                        fw.op(dve, lambda e: e.tensor_copy(out=vbig[:, half * 8 + tb, :], in_=ps[:, bk, :]), reads=[PSB[bk]], writes=[vbigB])
                            fw.dma(sp, VB[g].rearrange("(tb tp) f -> tp tb f", tp=128), vbig[:], reads=[vbigB], owner=vbigB)
            if stop == ("kv", layer):
                break
            j = layer - 2
            with Phase(fw) as ph:
                qTB, qTBB = ph.tile([128, 12, S], BF16, "qTB")
                with Phase(fw) as ph2:
                    hT, hTB = ph2.tile([128, 16, S], BF16, "hTb")
                    do_norm(xT, layer, hT, hTB)
                    proj_fm(ph2, hT, hTB, 16, b_w_q[j], 1536, make_rotary_evac(ph2, lambda c: (qTB[:, c, :], qTBB)), "wq")
                if stop == ("q", layer):
                    break
                OTb, OTbB = ph.tile([128, 12, S], BF16, "OTb")
                with Phase(fw) as ph2:
                    kts = [ph2.tile([128, S], BF16, "kt") for _ in range(2)]
                    vts = [ph2.tile([128, 16, 128], BF16, "vt") for _ in range(2)]
                    Es = [ph2.tile([128, 1024], BF16, "E") for _ in range(2)]
                    Ng = [ph2.tile([128, S], F32, "Ng") for _ in range(3)]
                    Zs, ZsB = ph2.tile([128, S], F32, "Zs")
                    seq = [(hg, g) for hg in range(4) for g in range(3)]

                    def load_kv(i):
                        hg, g = seq[i]
                        c = g * 4 + hg
                        fw.dma(sp, kts[i % 2][0][:], kTB[c * 128:(c + 1) * 128, :], writes=[kts[i % 2][1]])
                        fw.dma(sp, vts[i % 2][0][:], VB[g].rearrange("(kb kp) f -> kp kb f", kp=128)[:, :, hg * 128:(hg + 1) * 128], writes=[vts[i % 2][1]])
                    load_kv(0)
                    nsb = 0
                    for i, (hg, g) in enumerate(seq):
                        if i + 1 < len(seq):
                            load_kv(i + 1)
                        c = g * 4 + hg
                        dil = (1, 4, 16)[g]
                        sub = S // dil
                        Kt, KtB = kts[i % 2]
                        Vt, VtB = vts[i % 2]
                        N_, NB_ = Ng[g]
                        for sb in range(4):
                            k2 = nsb % 2
                            nsb += 1
                            sbk = 2 * k2
                            nbk, zbk = 4 + 2 * k2, 5 + 2 * k2
                            E, EB = Es[k2]
                            nfirst = []
                            for ql in range(4):
                                pos = sb * 512 + ql * 128
                                nblk = (pos % sub) // 128
                                nfirst.append(nblk == 0)
                                pprev = pos if nblk == 0 else pos - 128
                                col = ql * 256
                                bank, off = sbk + col // 512, col % 512
                                fw.op(pe, mm(ps[:, bank, off:off + 128], Kt[:, pprev:pprev + 128], qTB[:, c, pos:pos + 128], True, True),
                                      reads=[KtB, qTBB], writes=[PSB[bank]], inc=False)
                                fw.op(pe, mm(ps[:, bank, off + 128:off + 256], Kt[:, pos:pos + 128], qTB[:, c, pos:pos + 128], True, True),
                                      reads=[KtB, qTBB], writes=[PSB[bank]], inc=(ql == 3))
                            fw.op(act, lambda e: e.activation(out=v1024(E[:]), in_=ps[:, sbk:sbk + 2, :], func=AF.Exp, scale=QSCALE), reads=[PSB[sbk], PSB[sbk + 1]], writes=[EB])
                            mv = 2 if all(nfirst) else (0 if nfirst[0] else 1)
                            fw.op(pool, lambda e: e.tensor_tensor(out=E[:], in0=E[:], in1=bmask[:, mv, :], op=ALU.mult), reads=[EB, bmaskB], writes=[EB])
                            for (bank_, lhs_of) in ((nbk, lambda kbi: Vt[:, kbi, :]), (zbk, lambda kbi: ones_b[:])):
                                for ql in range(4):
                                    pos = sb * 512 + ql * 128
                                    kbi = pos // 128
                                    col = ql * 256
                                    if not nfirst[ql]:
                                        fw.op(pe, mm(ps[:, bank_, ql * 128:(ql + 1) * 128], lhs_of(kbi - 1), E[:, col:col + 128], True, False),
                                              reads=[EB, VtB, onesB], writes=[PSB[bank_]], inc=False)
                                    fw.op(pe, mm(ps[:, bank_, ql * 128:(ql + 1) * 128], lhs_of(kbi), E[:, col + 128:col + 256], nfirst[ql], True),
                                          reads=[EB, VtB, onesB], writes=[PSB[bank_]], inc=(ql == 3))

                            def tview(t):
                                if dil == 1:
                                    return t[:, sb * 512:(sb + 1) * 512]
                                if dil == 4:
                                    return t[:, :].rearrange("p (i r) -> p r i", r=4)[:, sb, :]
                                return t[:, :].rearrange("p (i r) -> p r i", r=16)[:, 4 * sb:4 * sb + 4, :]

                            def pview(bank__):
                                if dil == 16:
                                    return ps[:, bank__, :].rearrange("p (r i) -> p r i", r=4)
                                return ps[:, bank__, :]
                            fw.op(act, lambda e: e.activation(out=tview(N_), in_=pview(nbk), func=AF.Copy), reads=[PSB[nbk]], writes=[NB_])
                            if g == 0:
                                fw.op(dve, lambda e: e.tensor_copy(out=tview(Zs), in_=pview(zbk)), reads=[PSB[zbk]], writes=[ZsB])
                            else:
                                fw.op(dve, lambda e: e.tensor_tensor(out=tview(Zs), in0=tview(Zs), in1=pview(zbk), op=ALU.add), reads=[PSB[zbk], ZsB], writes=[ZsB])
                        if g == 2:
                            fw.op(dve, lambda e: e.reciprocal(out=Zs[:], in_=Zs[:]), reads=[ZsB], writes=[ZsB])
                            for g2 in range(3):
                                fw.op(dve, lambda e: e.tensor_tensor(out=OTb[:, g2 * 4 + hg, :], in0=Ng[g2][0][:], in1=Zs[:], op=ALU.mult),
                                      reads=[Ng[g2][1], ZsB], writes=[OTbB])
                if dbg:
                    dO = nc.dram_tensor("dbg_OTb%d" % layer, [1536, S], BF16, kind="ExternalOutput").ap()
                    fw.dma(sp, dO.rearrange("(c p) t -> p c t", p=128), OTb[:], reads=[OTbB], owner=OTbB)
                if stop == ("attn", layer):
                    break
                proj_fm(ph, OTb, OTbB, 12, b_w_o[j], D, make_resid_evac(ph, xT), "wob")
        xsrc = xT
        snap("dbg_attn%d" % layer)
        if stop == ("mix", layer):
            break
        with Phase(fw) as ph:
            hT, hTB = ph.tile([128, 16, S], BF16, "h2T")
            do_norm(xT, 4 + layer, hT, hTB)
            wg = [ph.tile([128, 16, 256], BF16, "wg") for _ in range(2)]
            wvv = [ph.tile([128, 16, 256], BF16, "wvv") for _ in range(2)]
            Ug = [ph.tile([128, S + 2], F32, "Ug") for _ in range(2)]
            Uv = [ph.tile([128, S + 2], F32, "Uv") for _ in range(2)]
            accg, accgB = ph.tile([128, S], F32, "accg")
            accv, accvB = ph.tile([128, S], F32, "accv")
            aout = [ph.tile([128, S], BF16, "aout") for _ in range(2)]
            for U, UB in Ug + Uv:
                fw.op(dve, lambda e: e.memset(U[:, 0:2], 0.0), writes=[UB])
            Wup = ffn_w_up[layer]

            def load_w(t):
                fw.dma(pool, wg[t % 2][0][:], kview(Wup)[:, :, t * 256:(t + 1) * 256], writes=[wg[t % 2][1]])
                fw.dma(pool, wvv[t % 2][0][:], kview(Wup)[:, :, DFF + t * 256:DFF + (t + 1) * 256], writes=[wvv[t % 2][1]])
            load_w(0)
            for jf in range(NFC):
                t, jj = jf // 2, jf % 2
                if jj == 0 and t + 1 < NFC // 2:
                    load_w(t + 1)
                WG, WGB = wg[t % 2]
                WV, WVB = wvv[t % 2]
                UG, UGB = Ug[jf % 2]
                UV, UVB = Uv[jf % 2]
                cs = slice(jj * 128, (jj + 1) * 128)
                for half in range(2):
                    b0 = 4 * half
                    for kc in range(16):
                        for t2 in range(2):
                            tcs = slice((half * 2 + t2) * 512, (half * 2 + t2 + 1) * 512)
                            fw.op(pe, mm(ps[:, b0 + t2, :], WG[:, kc, cs], hT[:, kc, tcs], kc == 0, kc == 15), reads=[WGB, hTB], writes=[PSB[b0 + t2]], inc=False)
                            fw.op(pe, mm(ps[:, b0 + 2 + t2, :], WV[:, kc, cs], hT[:, kc, tcs], kc == 0, kc == 15), reads=[WVB, hTB], writes=[PSB[b0 + 2 + t2]],
                                  inc=(kc == 15 and t2 == 1))
                    us = slice(2 + half * 1024, 2 + (half + 1) * 1024)
                    fw.op(act, lambda e: e.activation(out=v1024(UG[:, us]), in_=ps[:, b0:b0 + 2, :], func=AF.Copy), reads=[PSB[b0], PSB[b0 + 1]], writes=[UGB])
                    fw.op(act, lambda e: e.activation(out=v1024(UV[:, us]), in_=ps[:, b0 + 2:b0 + 4, :], func=AF.Copy), reads=[PSB[b0 + 2], PSB[b0 + 3]], writes=[UVB])
                for (U, UB, acc, accB, ci) in ((UG, UGB, accg, accgB, jf), (UV, UVB, accv, accvB, NFC + jf)):
                    fw.op(act, lambda e: e.activation(out=acc[:], in_=U[:, 2:S + 2], func=AF.Identity, scale=convw[:, layer, 2, ci:ci + 1], bias=convb[:, layer, ci:ci + 1]),
                          reads=[UB, convwB, convbB], writes=[accB])
                    fw.op(dve, lambda e: e.scalar_tensor_tensor(out=acc[:], in0=U[:, 1:S + 1], scalar=convw[:, layer, 1, ci:ci + 1], in1=acc[:], op0=ALU.mult, op1=ALU.add),
                          reads=[UB, convwB, accB], writes=[accB])
                    fw.op(dve, lambda e: e.scalar_tensor_tensor(out=acc[:], in0=U[:, 0:S], scalar=convw[:, layer, 0, ci:ci + 1], in1=acc[:], op0=ALU.mult, op1=ALU.add),
                          reads=[UB, convwB, accB], writes=[accB])
                fw.op(act, lambda e: e.activation(out=accg[:], in_=accg[:], func=AF.Silu), reads=[accgB], writes=[accgB])
                AO, AOB = aout[jf % 2]
                fw.op(dve, lambda e: e.tensor_tensor(out=AO[:], in0=accg[:], in1=accv[:], op=ALU.mult), reads=[accgB, accvB], writes=[AOB])
                fw.dma(sp, AT[jf * 128:(jf + 1) * 128, :], AO[:], reads=[AOB], owner=AOB)
        with Phase(fw) as ph:
            wd = [ph.tile([128, 22, 512], BF16, "wd") for _ in range(4)]
            at = [ph.tile([128, 22, 512], BF16, "at") for _ in range(2)]
            xt4 = [ph.tile([128, 4, 512], F32, "xt4") for _ in range(2)]
            Wdn = ffn_w_down[layer]
            ATv = AT.rearrange("(fc fp) t -> fp fc t", fp=128)
            xTv = xT.rearrange("(dc dp) t -> dp dc t", dp=128)

            def load_wd(g4):
                for kh in range(2):
                    W_, WB_ = wd[(g4 % 2) * 2 + kh]
                    fw.dma(pool, W_[:], kview(Wdn)[:, kh * 22:(kh + 1) * 22, g4 * 512:(g4 + 1) * 512], writes=[WB_])

            def load_at(tc, kh):
                fw.dma(sp, at[kh][0][:], ATv[:, kh * 22:(kh + 1) * 22, tc * 512:(tc + 1) * 512], writes=[at[kh][1]])
            load_wd(0)
            load_at(0, 0)
            load_at(0, 1)
            step = 0
            for g4 in range(4):
                if g4 + 1 < 4:
                    load_wd(g4 + 1)
                for tc in range(4):
                    b0 = 4 * (step % 2)
                    X4, X4B = xt4[step % 2]
                    fw.dma(sp, X4[:], xTv[:, g4 * 4:(g4 + 1) * 4, tc * 512:(tc + 1) * 512], writes=[X4B])
                    for kh in range(2):
                        W_, WB_ = wd[(g4 % 2) * 2 + kh]
                        A_, AB_ = at[kh]
                        for fc in range(22):
                            for dc in range(4):
                                fw.op(pe, mm(ps[:, b0 + dc, :], W_[:, fc, dc * 128:(dc + 1) * 128], A_[:, fc, :], kh == 0 and fc == 0, kh == 1 and fc == 21),
                                      reads=[WB_, AB_], writes=[PSB[b0 + dc]], inc=(fc == 21 and dc == 3))
                        nxt = step + 1
                        if nxt < 16:
                            load_at(nxt % 4, kh)
                    fw.op(dve, lambda e: e.tensor_tensor(out=X4[:], in0=X4[:], in1=ps[:, b0:b0 + 4, :], op=ALU.add), reads=[X4B] + PSB[b0:b0 + 4], writes=[X4B])
                    fw.dma(sp, xTv[:, g4 * 4:(g4 + 1) * 4, tc * 512:(tc + 1) * 512], X4[:], reads=[X4B], owner=X4B)
                    step += 1
        snap("dbg_ffn%d" % layer)
        if stop == ("ffn", layer):
            break
        with Phase(fw) as ph:
            hT, hTB = ph.tile([128, 16, S], BF16, "h3T")
            do_norm(xT, 8 + layer, hT, hTB)
            pTb, pTbB = ph.tile([128, 2, S], BF16, "pTb")
            fw.dma(pool, pTb[:], kview(pT[layer]), writes=[pTbB])
            wgt = [ph.tile([128, 16, 512], BF16, "wgt") for _ in range(2)]
            wpj = [ph.tile([128, 2, 512], BF16, "wpj") for _ in range(2)]
            sgs = [ph.tile([128, 1024], F32, "sg") for _ in range(2)]
            xts = [ph.tile([128, S], F32, "xp") for _ in range(2)]

            def load_g(g4):
                fw.dma(pool, wgt[g4 % 2][0][:], kview(ple_w_gate[layer])[:, :, g4 * 512:(g4 + 1) * 512], writes=[wgt[g4 % 2][1]])
                fw.dma(pool, wpj[g4 % 2][0][:], kview(ple_w_proj[layer])[:, :, g4 * 512:(g4 + 1) * 512], writes=[wpj[g4 % 2][1]])
            load_g(0)
            cnt = 0
            for g4 in range(4):
                if g4 + 1 < 4:
                    load_g(g4 + 1)
                WG, WGB = wgt[g4 % 2]
                WP, WPB = wpj[g4 % 2]
                for dcl in range(4):
                    dc = g4 * 4 + dcl
                    X, XB = xts[dc % 2]
                    fw.dma(sp, X[:], xT[dc * 128:(dc + 1) * 128, :], writes=[XB])
                    cs = slice(dcl * 128, (dcl + 1) * 128)
                    for half in range(2):
                        b0 = 4 * (cnt % 2)
                        SG, SGB = sgs[cnt % 2]
                        cnt += 1
                        for kc in range(16):
                            for t2 in range(2):
                                tcs = slice((half * 2 + t2) * 512, (half * 2 + t2 + 1) * 512)
                                fw.op(pe, mm(ps[:, b0 + t2, :], WG[:, kc, cs], hT[:, kc, tcs], kc == 0, kc == 15), reads=[WGB, hTB], writes=[PSB[b0 + t2]], inc=False)
                        for kc in range(2):
                            for t2 in range(2):
                                tcs = slice((half * 2 + t2) * 512, (half * 2 + t2 + 1) * 512)
                                fw.op(pe, mm(ps[:, b0 + 2 + t2, :], WP[:, kc, cs], pTb[:, kc, tcs], kc == 0, kc == 1), reads=[WPB, pTbB], writes=[PSB[b0 + 2 + t2]],
                                      inc=(kc == 1 and t2 == 1))
                        hs = slice(half * 1024, (half + 1) * 1024)
                        fw.op(act, lambda e: e.activation(out=v1024(SG[:]), in_=ps[:, b0:b0 + 2, :], func=AF.Sigmoid), reads=[PSB[b0], PSB[b0 + 1]], writes=[SGB])
                        fw.op(dve, lambda e: e.tensor_tensor(out=v1024(SG[:]), in0=v1024(SG[:]), in1=ps[:, b0 + 2:b0 + 4, :], op=ALU.mult),
                              reads=[SGB, PSB[b0 + 2], PSB[b0 + 3]], writes=[SGB])
                        fw.op(pool, lambda e: e.tensor_tensor(out=X[:, hs], in0=X[:, hs], in1=SG[:], op=ALU.add), reads=[SGB, XB], writes=[XB])
                    fw.dma(sp, xT[dc * 128:(dc + 1) * 128, :], X[:], reads=[XB], owner=XB)
        snap("dbg_ple%d" % layer)
        if stop == ("ple", layer):
            break
    else:
        do_norm(xT, 13, out_dram=yT)
    fw.barrier()
    fw.close()
    return nc


def _consts():
    j = np.arange(128)
    tri = -(j[:, None] >= j[None, :]).astype(np.float32)
    ones = np.ones((128, 128), np.float32)
    t = np.arange(512)
    amask = np.stack([((128 * i + j)[:, None] < t[None, :]).astype(np.float32) for i in range(4)], axis=1)
    up = (j[:, None] >= j[None, :]).astype(np.float32)
    lo = (j[:, None] <= j[None, :]).astype(np.float32)
    F_ = np.concatenate([np.zeros((128, 128), np.float32), lo], 1)
    R_ = np.concatenate([up, lo], 1)
    bmask = np.stack([np.concatenate([F_, R_, R_, R_], 1), np.concatenate([R_] * 4, 1), np.concatenate([F_] * 4, 1)], axis=1)
    sel33 = np.zeros((33, 128), np.float32)
    sel33[0] = -1.0
    sel33[32] = -1.0
    perm = np.zeros((32, 32), np.float32)
    for m in range(32):
        if m < 16:
            perm[m + 16, m] = -1.0
        else:
            perm[m - 16, m] = 1.0
    invf = (500000.0 ** (-np.arange(0, 32, 2, dtype=np.float32) / 32)).astype(np.float32)
    invf = np.concatenate([invf, invf]).reshape(32, 1)
    return dict(c_tri=tri, c_ones=ones, c_amask=np.ascontiguousarray(amask.reshape(128, -1)),
                c_bmask=np.ascontiguousarray(bmask.reshape(128, -1)), c_sel33=sel33, c_perm=perm, c_invf=invf)


def _pack(inputs):
    f = lambda a: np.ascontiguousarray(np.asarray(a, dtype=np.float32))
    nw = np.concatenate([np.asarray(inputs["attn_norm_w"]), np.asarray(inputs["ffn_norm_w"]), np.asarray(inputs["ple_norm_w"]),
                         np.asarray(inputs["kv_norm_w"])[None], np.asarray(inputs["final_norm_w"])[None]], 0)
    normw = f(nw.reshape(14, 16, 128).transpose(2, 0, 1).reshape(128, -1))
    convw = f(np.asarray(inputs["ffn_conv_w"]).reshape(4, 3, 88, 128).transpose(3, 0, 1, 2).reshape(128, -1))
    convb = f(np.asarray(inputs["ffn_conv_b"]).reshape(4, 88, 128).transpose(2, 0, 1).reshape(128, -1))
    shared = dict(normw=normw, convw=convw, convb=convb)
    for k in ("a_w_qkv", "a_w_o", "b_w_kv", "b_w_q", "b_w_o", "ffn_w_up", "ffn_w_down", "ple_w_gate", "ple_w_proj"):
        shared[k] = f(inputs[k])
    shared.update(_consts())
    return shared


def _core_inputs(inputs, shared, b):
    m = dict(shared)
    m["xT_in"] = np.ascontiguousarray(np.asarray(inputs["x"][b], dtype=np.float32).T)
    m["pT"] = np.ascontiguousarray(np.asarray(inputs["p"][:, b], dtype=np.float32).transpose(0, 2, 1))
    m["pos32"] = np.ascontiguousarray(np.broadcast_to(np.asarray(inputs["positions"][b], dtype=np.int32)[None, :], (32, S)))
    return m


_NC = None


def kernel(**inputs):
    global _NC
    if _NC is None:
        _NC = build_program()
    shared = _pack(inputs)
    nb = inputs["x"].shape[0]
    in_maps = [_core_inputs(inputs, shared, b) for b in range(nb)]
    res = run_bass_kernel_spmd(_NC, in_maps, core_ids=list(range(nb)))
    out = np.stack([np.asarray(r["yT"]).T for r in res.results], 0)
    return np.ascontiguousarray(out.astype(np.float32))
```

```python
from contextlib import ExitStack
import numpy as np
import concourse.bass as bass
import concourse.mybir as mybir
from concourse.bass_utils import run_bass_kernel_spmd

F32 = mybir.dt.float32
BF16 = mybir.dt.bfloat16
I32 = mybir.dt.int32
AF = mybir.ActivationFunctionType
ALU = mybir.AluOpType

D = 2048
S = 2048
DFF = 5632
NFC = DFF // 128
QSCALE = 128.0 ** -0.5
PI = float(np.pi)
SNAP = False


class DSem:
    __slots__ = ("sem", "cnt")

    def __init__(self, sem):
        self.sem, self.cnt = sem, 0


class Buf:
    __slots__ = ("name", "w", "r", "ds", "excl")

    def __init__(self, name="", excl=False):
        self.name, self.w, self.r, self.ds, self.excl = name, [], [], None, excl


class Eng:
    def __init__(self, name, obj, sem):
        self.name, self.obj, self.sem = name, obj, sem
        self.cnt = 0
        self.pending = False
        self.seen = {}

    def wait(self, tok):
        sem, val = tok
        if sem is self.sem and self.name == "pe":
            return
        k = id(sem)
        if self.seen.get(k, 0) >= val:
            return
        self.seen[k] = val
        self.obj.wait_ge(sem, val)


class FW:
    def __init__(self, nc):
        self.nc = nc
        self._cms = []
        self.pe = Eng("pe", nc.tensor, self._sem("s_pe"))
        self.act = Eng("act", nc.scalar, self._sem("s_act"))
        self.dve = Eng("dve", nc.vector, self._sem("s_dve"))
        self.pool = Eng("pool", nc.gpsimd, self._sem("s_pool"))
        self.sp = Eng("sp", nc.sync, self._sem("s_sp"))
        self.engs = [self.pe, self.act, self.dve, self.pool, self.sp]
        self.free_ds, self.all_ds, self.live = [], [], []
        self.uid = 0

    def _sem(self, name):
        cm = self.nc.semaphore(name)
        self._cms.append(cm)
        return cm.__enter__()

    def close(self):
        for cm in reversed(self._cms):
            cm.__exit__(None, None, None)

    def _ds(self, buf):
        if buf.ds is None:
            if self.free_ds:
                buf.ds = self.free_ds.pop()
            else:
                buf.ds = DSem(self._sem("d%d" % len(self.all_ds)))
                self.all_ds.append(buf.ds)
            self.live.append(buf)
        return buf.ds

    def _deps(self, eng, reads, writes):
        for b in reads:
            for t in b.w:
                eng.wait(t)
            if b.excl:
                for t in b.r:
                    if t[0] is not eng.sem:
                        eng.wait(t)
        for b in writes:
            for t in b.w:
                eng.wait(t)
            for t in b.r:
                eng.wait(t)

    def _mark(self, tok, reads, writes):
        for b in reads:
            b.r.append(tok)
            if len(b.r) > 10:
                best = {}
                for s_, v in b.r:
                    if id(s_) not in best or best[id(s_)][1] < v:
                        best[id(s_)] = (s_, v)
                b.r = list(best.values())
        for b in writes:
            b.w = [tok]
            b.r = []

    def op(self, eng, fn, reads=(), writes=(), inc=True):
        self._deps(eng, reads, writes)
        ins = fn(eng.obj)
        if inc:
            eng.cnt += 1
            ins.then_inc(eng.sem, 1)
            eng.pending = False
            tok = (eng.sem, eng.cnt)
        else:
            eng.pending = True
            tok = (eng.sem, eng.cnt + 1)
        self._mark(tok, reads, writes)
        return ins

    def dma(self, q, out_ap, in_ap, reads=(), writes=(), owner=None):
        owner = owner or (writes[0] if writes else reads[0])
        ds = self._ds(owner)
        self._deps(q, reads, writes)
        ins = q.obj.dma_start(out=out_ap, in_=in_ap)
        ds.cnt += 16
        ins.then_inc(ds.sem, 16)
        self._mark((ds.sem, ds.cnt), reads, writes)
        return ins

    def barrier(self):
        assert not self.pe.pending
        toks = [(e.sem, e.cnt) for e in self.engs if e.cnt > 0]
        toks += [(d.sem, d.cnt) for d in self.all_ds if d.cnt > 0]
        for e in self.engs:
            for t in toks:
                e.wait(t)
        for b in self.live:
            self.free_ds.append(b.ds)
            b.ds = None
        self.live = []


class Phase:
    def __init__(self, fw):
        self.fw, self.es = fw, ExitStack()

    def __enter__(self):
        return self

    def tile(self, shape, dtype, name="t"):
        self.fw.uid += 1
        t = self.es.enter_context(self.fw.nc.sbuf_tensor("%s_%d" % (name, self.fw.uid), list(shape), dtype))
        return t, Buf(name)

    def __exit__(self, *a):
        self.fw.barrier()
        self.es.close()
        return False


def build_program(nlayers=4, dbg=False, stop=None, start=0, lite=()):
    nc = bass.Bass("TRN2", target_bir_lowering=False)
    fw = FW(nc)
    pe, act, dve, pool, sp = fw.pe, fw.act, fw.dve, fw.pool, fw.sp

    def din(name, shape, dt=F32):
        if name in lite:
            return nc.dram_tensor(name, [1] * len(shape), dt, kind="Internal").ap()
        return nc.dram_tensor(name, list(shape), dt, kind="ExternalInput").ap()

    def dscr(name, shape, dt):
        return nc.dram_tensor(name, list(shape), dt, kind=("ExternalOutput" if dbg else "Internal")).ap()

    xT_in = din("xT_in", [D, S])
    pT = din("pT", [4, 256, S])
    pos32 = din("pos32", [32, S], I32)
    normw_d = din("normw", [128, 14 * 16])
    convw_d = din("convw", [128, 4 * 3 * 88])
    convb_d = din("convb", [128, 4 * 88])
    a_w_qkv = din("a_w_qkv", [2, D, 3 * D])
    a_w_o = din("a_w_o", [2, D, D])
    b_w_kv = din("b_w_kv", [D, 3072])
    b_w_q = din("b_w_q", [2, D, 1536])
    b_w_o = din("b_w_o", [2, 1536, D])
    ffn_w_up = din("ffn_w_up", [4, D, 2 * DFF])
    ffn_w_down = din("ffn_w_down", [4, DFF, D])
    ple_w_gate = din("ple_w_gate", [4, D, D])
    ple_w_proj = din("ple_w_proj", [4, 256, D])
    c_tri = din("c_tri", [128, 128])
    c_ones = din("c_ones", [128, 128])
    c_amask = din("c_amask", [128, 4 * 512])
    c_bmask = din("c_bmask", [128, 3 * 1024])
    c_sel33 = din("c_sel33", [128, 128])
    c_perm = din("c_perm", [32, 32])
    c_invf = din("c_invf", [32, 1])
    yT = nc.dram_tensor("yT", [D, S], F32, kind="ExternalOutput").ap()

    xT = dscr("xT_s", [D, S], F32)
    qT = dscr("qT_s", [D, S], BF16)
    kT = dscr("kT_s", [D, S], BF16)
    Vd = dscr("V_s", [S, D], BF16)
    AT = dscr("AT_s", [DFF, S], BF16)
    kTB = dscr("kTB_s", [1536, S], BF16)
    VB = dscr("VB_s", [3, S, 512], BF16)

    def ptile(name, shape, dt):
        return nc.alloc_sbuf_tensor(name, list(shape), dt), Buf(name)

    ones_b, onesB = ptile("ones_b", [128, 128], BF16)
    tri_b, triB = ptile("tri_b", [128, 128], BF16)
    amask, amaskB = ptile("amask", [128, 4, 512], BF16)
    bmask, bmaskB = ptile("bmask", [128, 3, 1024], BF16)
    sel33, selB = ptile("sel33", [128, 128], BF16)
    perm32, permB = ptile("perm32", [32, 32], BF16)
    invf, invfB = ptile("invf", [32, 1], F32)
    normw, normwB = ptile("normw_t", [128, 14, 16], F32)
    convw, convwB = ptile("convw_t", [128, 4, 3, 88], F32)
    convb, convbB = ptile("convb_t", [128, 4, 88], F32)
    COS, cosB = ptile("COS", [32, S], F32)
    SIN, sinB = ptile("SIN", [32, S], F32)
    ps = nc.alloc_psum_tensor("ps", [128, 8, 512], F32)
    PSB = [Buf("ps%d" % i, excl=True) for i in range(8)]

    with Phase(fw) as ph:
        fw.dma(pool, ones_b[:], c_ones, writes=[onesB])
        fw.dma(pool, tri_b[:], c_tri, writes=[triB])
        fw.dma(pool, amask[:], c_amask.rearrange("p (i t) -> p i t", i=4), writes=[amaskB])
        fw.dma(pool, bmask[:], c_bmask.rearrange("p (i t) -> p i t", i=3), writes=[bmaskB])
        fw.dma(pool, sel33[:], c_sel33, writes=[selB])
        fw.dma(pool, perm32[:], c_perm, writes=[permB])
        fw.dma(sp, invf[:], c_invf, writes=[invfB])
        fw.dma(sp, normw[:], normw_d.rearrange("p (a b) -> p a b", a=14), writes=[normwB])
        fw.dma(sp, convw[:], convw_d.rearrange("p (a b c) -> p a b c", a=4, b=3), writes=[convwB])
        fw.dma(sp, convb[:], convb_d.rearrange("p (a b) -> p a b", a=4), writes=[convbB])
        pi_t, piB = ph.tile([32, S], I32, "posi")
        ang, angB = ph.tile([32, S], F32, "ang")
        tmp, tmpB = ph.tile([32, S], F32, "angt")
        fw.dma(sp, pi_t[:], pos32, writes=[piB])
        fw.op(dve, lambda e: e.tensor_copy(out=ang[:], in_=pi_t[:]), reads=[piB], writes=[angB])
        fw.op(dve, lambda e: e.tensor_scalar(out=ang[:], in0=ang[:], scalar1=invf[:, 0:1], scalar2=None, op0=ALU.mult),
              reads=[angB, invfB], writes=[angB])
        MAGIC = 12582912.0
        C1 = 6.28125
        C2 = 2.0 * PI - C1
        nf, nfB = ph.tile([32, S], F32, "nf")
        for (dst_t, dst_b, shift) in ((SIN, sinB, 0.0), (COS, cosB, 0.5 * PI)):
            fw.op(dve, lambda e: e.tensor_scalar(out=tmp[:], in0=ang[:], scalar1=shift, scalar2=None, op0=ALU.add), reads=[angB], writes=[tmpB])
            fw.op(dve, lambda e: e.tensor_scalar(out=nf[:], in0=tmp[:], scalar1=1.0 / (2.0 * PI), scalar2=MAGIC, op0=ALU.mult, op1=ALU.add), reads=[tmpB], writes=[nfB])
            fw.op(dve, lambda e: e.tensor_scalar(out=nf[:], in0=nf[:], scalar1=MAGIC, scalar2=None, op0=ALU.subtract), reads=[nfB], writes=[nfB])
            fw.op(dve, lambda e: e.scalar_tensor_tensor(out=tmp[:], in0=nf[:], scalar=-C1, in1=tmp[:], op0=ALU.mult, op1=ALU.add), reads=[nfB, tmpB], writes=[tmpB])
            fw.op(dve, lambda e: e.scalar_tensor_tensor(out=tmp[:], in0=nf[:], scalar=-C2, in1=tmp[:], op0=ALU.mult, op1=ALU.add), reads=[nfB, tmpB], writes=[tmpB])
            fw.op(act, lambda e: e.activation(out=dst_t[:], in_=tmp[:], func=AF.Sin), reads=[tmpB], writes=[dst_b])

    snapB = Buf("snap")

    def snap(name):
        if not (dbg and SNAP):
            return
        d_ = nc.dram_tensor(name, [D, S], F32, kind="ExternalOutput").ap()
        fw.barrier()
        fw.dma(sp, d_, xT, owner=snapB)
        fw.barrier()

    def kview(ap2d, rows=128):
        return ap2d.rearrange("(kc kp) f -> kp kc f", kp=rows)

    def mm(out, lhsT, rhs, start, stop):
        return lambda e: e.matmul(out, lhsT=lhsT, rhs=rhs, start=start, stop=stop)

    def do_norm(xsrc, widx, hT=None, hTB=None, out_dram=None):
        NT = 256
        with Phase(fw) as ph:
            xs = [ph.tile([128, 16, NT], F32, "nx") for _ in range(2)]
            sq, sqB = ph.tile([128, 16, NT], BF16, "nsq")
            r1 = [ph.tile([128, NT], F32, "nr1") for _ in range(2)]
            r2 = [ph.tile([128, NT], F32, "nr2") for _ in range(2)]
            ob = [ph.tile([128, 16, NT], F32, "nob") for _ in range(2)] if out_dram is not None else None
            for tc in range(S // NT):
                X, XB = xs[tc % 2]
                tsl = slice(tc * NT, (tc + 1) * NT)
                fw.dma(sp, X[:], kview(xsrc)[:, :, tsl], writes=[XB])
                fw.op(act, lambda e: e.activation(out=sq[:], in_=X[:], func=AF.Square), reads=[XB], writes=[sqB])
                bk = tc % 2
                for kc in range(16):
                    fw.op(pe, mm(ps[:, bk, 0:NT], ones_b[:], sq[:, kc, :], kc == 0, kc == 15), reads=[sqB, onesB], writes=[PSB[bk]], inc=(kc == 15))
                R1, R1B = r1[tc % 2]
                R2, R2B = r2[tc % 2]
                fw.op(act, lambda e: e.activation(out=R1[:], in_=ps[:, bk, 0:NT], func=AF.Sqrt, scale=1.0 / D, bias=1e-6), reads=[PSB[bk]], writes=[R1B])
                fw.op(dve, lambda e: e.reciprocal(out=R2[:], in_=R1[:]), reads=[R1B], writes=[R2B])
                if out_dram is None:
                    for kc in range(16):
                        fw.op(dve, lambda e: e.scalar_tensor_tensor(out=hT[:, kc, tsl], in0=X[:, kc, :], scalar=normw[:, widx, kc:kc + 1], in1=R2[:], op0=ALU.mult, op1=ALU.mult),
                              reads=[XB, R2B, normwB], writes=[hTB])
                else:
                    O, OB = ob[tc % 2]
                    for kc in range(16):
                        fw.op(dve, lambda e: e.scalar_tensor_tensor(out=O[:, kc, :], in0=X[:, kc, :], scalar=normw[:, widx, kc:kc + 1], in1=R2[:], op0=ALU.mult, op1=ALU.mult),
                              reads=[XB, R2B, normwB], writes=[OB])
                    fw.dma(sp, kview(out_dram)[:, :, tsl], O[:], reads=[OB], owner=OB)

    class PSet:
        def __init__(self):
            self.n = 0

        def next(self):
            s = self.n % 2
            self.n += 1
            return s

    def proj_prep(ph, KCn, W2d, wname="w"):
        wb = [ph.tile([128, KCn, 512], BF16, wname) for _ in range(2)]
        fw.dma(pool, wb[0][0][:], kview(W2d)[:, :, 0:512], writes=[wb[0][1]])
        return wb

    def proj_fm(ph, hT, hTB, KCn, W2d, ncols, evac, wname="w", wb=None):
        if wb is None:
            wb = proj_prep(ph, KCn, W2d, wname)
        pset = PSet()
        ng = ncols // 512
        for g in range(ng):
            Wt, WB = wb[g % 2]
            if g + 1 < ng:
                fw.dma(pool, wb[(g + 1) % 2][0][:], kview(W2d)[:, :, (g + 1) * 512:(g + 2) * 512], writes=[wb[(g + 1) % 2][1]])
            for fc in range(4):
                for half in range(2):
                    s = pset.next()
                    for kc in range(KCn):
                        for t2 in range(2):
                            tcs = slice((half * 2 + t2) * 512, (half * 2 + t2 + 1) * 512)
                            fw.op(pe, mm(ps[:, 2 * s + t2, :], Wt[:, kc, fc * 128:(fc + 1) * 128], hT[:, kc, tcs], kc == 0, kc == KCn - 1),
                                  reads=[WB, hTB], writes=[PSB[2 * s + t2]], inc=(kc == KCn - 1 and t2 == 1))
                    evac(g * 4 + fc, half, s)

    def v1024(ap):
        return ap.rearrange("p (b t) -> p b t", b=2)

    def make_resid_evac(ph, xsrc):
        xts = [ph.tile([128, S], F32, "xr") for _ in range(3)]
        st = {"n": 0}

        def evac(c, half, s):
            X, XB = xts[st["n"] % 3]
            if half == 0:
                fw.dma(sp, X[:], xsrc[c * 128:(c + 1) * 128, :], writes=[XB])
            hs = slice(half * 1024, (half + 1) * 1024)
            fw.op(dve, lambda e: e.tensor_tensor(out=v1024(X[:, hs]), in0=v1024(X[:, hs]), in1=ps[:, 2 * s:2 * s + 2, :], op=ALU.add),
                  reads=[XB, PSB[2 * s], PSB[2 * s + 1]], writes=[XB])
            if half == 1:
                fw.dma(sp, xT[c * 128:(c + 1) * 128, :], X[:], reads=[XB], owner=XB)
                st["n"] += 1
        return evac

    def make_rotary_evac(ph, dst_of, after=None):
        q32s = [ph.tile([32, 1024], BF16, "q32") for _ in range(2)]
        qlos = [ph.tile([32, 1024], BF16, "qlo") for _ in range(2)]
        qfs = [ph.tile([32, 1024], F32, "qf") for _ in range(2)]
        t2s = [ph.tile([32, 1024], F32, "rt2") for _ in range(2)]
        st = {"n": 0}

        def evac(c, half, s):
            k = st["n"] % 2
            st["n"] += 1
            dil = (1, 4, 16)[c // 4]
            ni = 1024 // dil
            dst, dstB = dst_of(c)
            Q, QB = q32s[k]
            T2, T2B = t2s[k]
            hs = slice(half * 1024, (half + 1) * 1024)
            src = [PSB[2 * s], PSB[2 * s + 1]]
            rb = 4 + 2 * k
            QL, QLB = qlos[k]
            QF, QFB = qfs[k]
            fw.op(act, lambda e: e.activation(out=v1024(QF[:]), in_=ps[0:32, 2 * s:2 * s + 2, :], func=AF.Copy), reads=src, writes=[QFB])
            fw.op(dve, lambda e: e.tensor_copy(out=Q[:], in_=QF[:]), reads=[QFB], writes=[QB])
            fw.op(dve, lambda e: e.tensor_tensor(out=QL[:], in0=QF[:], in1=Q[:], op=ALU.subtract), reads=[QFB, QB], writes=[QLB])
            for t2 in range(2):
                fw.op(pe, mm(ps[0:32, rb + t2, :], perm32[:], Q[:, t2 * 512:(t2 + 1) * 512], True, False), reads=[QB, permB], writes=[PSB[rb + t2]], inc=False)
                fw.op(pe, mm(ps[0:32, rb + t2, :], perm32[:], QL[:, t2 * 512:(t2 + 1) * 512], False, True), reads=[QLB, permB], writes=[PSB[rb + t2]], inc=(t2 == 1))
            fw.op(dve, lambda e: e.tensor_tensor(out=QF[:], in0=QF[:], in1=COS[:, hs], op=ALU.mult), reads=[QFB, cosB], writes=[QFB])
            T1, T1B = QF, QFB
            fw.op(dve, lambda e: e.tensor_tensor(out=v1024(T2[:]), in0=ps[0:32, rb:rb + 2, :], in1=v1024(SIN[:, hs]), op=ALU.mult),
                  reads=[PSB[rb], PSB[rb + 1], sinB], writes=[T2B])

            def dview(rows):
                if dil == 1:
                    return dst[rows, hs]
                return dst[rows, :].rearrange("p (r i) -> p r i", r=dil)[:, :, half * ni:(half + 1) * ni]

            def sview(ap):
                if dil == 1:
                    return ap
                return ap.rearrange("p (i r) -> p r i", r=dil)
            all_src = ps[:, 2 * s:2 * s + 2, :].rearrange("p b t -> p (b t)")
            fw.op(act, lambda e: e.activation(out=dview(slice(0, 128)), in_=sview(all_src), func=AF.Copy), reads=src, writes=[dstB])
            fw.op(dve, lambda e: e.tensor_tensor(out=dview(slice(0, 32)), in0=sview(T1[:]), in1=sview(T2[:]), op=ALU.add),
                  reads=[T1B, T2B], writes=[dstB])
            if after is not None:
                after(c, half, dst, dstB)
        return evac

    xsrc = xT_in
    kv_done = False
    if start > 0:
        xT = xT_in
        xsrc = xT
    for layer in range(start, nlayers):
        if layer < 2:
            with Phase(fw) as ph:
                hT, hTB = ph.tile([128, 16, S], BF16, "hT")
                wbqk = proj_prep(ph, 16, a_w_qkv[layer][:, 0:2 * D], "wqk")
                do_norm(xsrc, layer, hT, hTB)
                obs = [ph.tile([128, S], BF16, "qk_o") for _ in range(2)]
                stq = {"n": 0}

                def evac_qk(c, half, s):
                    O, OB = obs[stq["n"] % 2]
                    hs = slice(half * 1024, (half + 1) * 1024)
                    srcb = [PSB[2 * s], PSB[2 * s + 1]]
                    if c < 16:
                        fw.op(act, lambda e: e.activation(out=v1024(O[:, hs]), in_=ps[:, 2 * s:2 * s + 2, :], func=AF.Copy, scale=QSCALE), reads=srcb, writes=[OB])
                    else:
                        fw.op(dve, lambda e: e.tensor_copy(out=v1024(O[:, hs]), in_=ps[:, 2 * s:2 * s + 2, :]), reads=srcb, writes=[OB])
                    if half == 1:
                        dstd = qT if c < 16 else kT
                        cc = c % 16
                        fw.dma(sp, dstd[cc * 128:(cc + 1) * 128, :], O[:], reads=[OB], owner=OB)
                        stq["n"] += 1
                proj_fm(ph, hT, hTB, 16, a_w_qkv[layer][:, 0:2 * D], 2 * D, evac_qk, "wqk", wb=wbqk)
                wv = [ph.tile([128, 16, 512], BF16, "wv") for _ in range(2)]
                vbig = [ph.tile([128, 16, 512], BF16, "vbig") for _ in range(2)]
                Wv2d = a_w_qkv[layer][:, 2 * D:3 * D]
                fw.dma(pool, wv[0][0][:], kview(Wv2d)[:, :, 0:512], writes=[wv[0][1]])
                nb = 0
                for g in range(4):
                    Wt, WB = wv[g % 2]
                    if g + 1 < 4:
                        fw.dma(pool, wv[(g + 1) % 2][0][:], kview(Wv2d)[:, :, (g + 1) * 512:(g + 2) * 512], writes=[wv[(g + 1) % 2][1]])
                    VBt, VBB = vbig[g % 2]
                    for tb in range(16):
                        bk = 4 + nb % 4
                        nb += 1
                        for kc in range(16):
                            fw.op(pe, mm(ps[:, bk, :], hT[:, kc, tb * 128:(tb + 1) * 128], Wt[:, kc, :], kc == 0, kc == 15), reads=[WB, hTB], writes=[PSB[bk]], inc=(kc == 15))
                        if tb % 2 == 0:
                            fw.op(act, lambda e: e.activation(out=VBt[:, tb, :], in_=ps[:, bk, :], func=AF.Copy), reads=[PSB[bk]], writes=[VBB])
                        else:
                            fw.op(dve, lambda e: e.tensor_copy(out=VBt[:, tb, :], in_=ps[:, bk, :]), reads=[PSB[bk]], writes=[VBB])
                    fw.dma(sp, Vd.rearrange("(tb tp) f -> tp tb f", tp=128)[:, :, g * 512:(g + 1) * 512], VBt[:], reads=[VBB], owner=VBB)
            if stop == ("qkv", layer):
                break
            with Phase(fw) as ph:
                OT, OTB = ph.tile([128, 16, S], BF16, "OT")
                with Phase(fw) as ph2:
                    qh = [ph2.tile([128, S], BF16, "qh") for _ in range(2)]
                    kh = [ph2.tile([128, S], BF16, "kh") for _ in range(2)]
                    vh = [ph2.tile([128, 16, 128], BF16, "vh") for _ in range(2)]
                    eT = [ph2.tile([128, 512], F32, "eT") for _ in range(2)]
                    Lt = [ph2.tile([128, 512], BF16, "Lt") for _ in range(4)]
                    At = [ph2.tile([128, 512], BF16, "At") for _ in range(2)]
                    hiT = [ph2.tile([128, 512], BF16, "hiT") for _ in range(3)]
                    for H_, HB_ in hiT:
                        fw.op(dve, lambda e: e.memset(H_[:], 0.0), writes=[HB_])

                    def load_head(h):
                        fw.dma(sp, qh[h % 2][0][:], qT[h * 128:(h + 1) * 128, :], writes=[qh[h % 2][1]])
                        fw.dma(sp, kh[h % 2][0][:], kT[h * 128:(h + 1) * 128, :], writes=[kh[h % 2][1]])
                        fw.dma(sp, vh[h % 2][0][:], Vd.rearrange("(sb sp) f -> sp sb f", sp=128)[:, :, h * 128:(h + 1) * 128], writes=[vh[h % 2][1]])
                    load_head(0)
                    blocks = [(qc, kb) for qc in range(4) for kb in range(4 * qc + 3, -1, -1)]
                    NB = len(blocks)
                    for h in range(16):
                        if h + 1 < 16:
                            load_head(h + 1)
                        Q, QB = qh[h % 2]
                        K, KB = kh[h % 2]
                        V, VB_ = vh[h % 2]
                        for n in range(-3, NB + 2):
                            m = n + 3
                            if 0 <= m < NB:
                                qc, kb = blocks[m]
                                fw.op(pe, mm(ps[:, m % 4, :], K[:, kb * 128:(kb + 1) * 128], Q[:, qc * 512:(qc + 1) * 512], True, False),
                                      reads=[KB, QB], writes=[PSB[m % 4]])
                            m = n + 2
                            if 0 <= m < NB:
                                qc, kb = blocks[m]
                                diag = kb >= 4 * qc
                                E, EB = eT[m % 2]
                                L, LB = Lt[m % 4]
                                fw.op(act, lambda e: e.activation(out=E[:], in_=ps[:, m % 4, :], func=AF.Exp), reads=[PSB[m % 4]], writes=[EB])
                                fw.op(act, lambda e: e.activation(out=L[:], in_=E[:], func=AF.Ln, bias=1.0), reads=[EB], writes=[LB])
                                if diag:
                                    fw.op(pool, lambda e: e.tensor_tensor(out=L[:], in0=L[:], in1=amask[:, kb - 4 * qc, :], op=ALU.mult), reads=[LB, amaskB], writes=[LB])
                            m = n + 1
                            if 0 <= m < NB:
                                qc, kb = blocks[m]
                                first, last = kb == 4 * qc + 3, kb == 0
                                L, LB = Lt[m % 4]
                                if not last:
                                    bb = 4 + qc % 2
                                    H, HB = hiT[m % 3]
                                    fw.op(pe, mm(ps[:, bb, :], ones_b[:], L[:], first, kb == 1), reads=[LB, onesB], writes=[PSB[bb]])
                                    fw.op(dve, lambda e: e.tensor_copy(out=H[0:33, :], in_=ps[0:33, bb, :]), reads=[PSB[bb]], writes=[HB])
                                    fw.op(dve, lambda e: e.tensor_tensor(out=H[32:33, :], in0=ps[32:33, bb, :], in1=H[32:33, :], op=ALU.subtract), reads=[PSB[bb], HB], writes=[HB])
                            if 0 <= n < NB:
                                qc, kb = blocks[n]
                                first, diag = kb == 4 * qc + 3, kb >= 4 * qc
                                L, LB = Lt[n % 4]
                                fw.op(pe, mm(ps[:, n % 4, :], tri_b[:], L[:], False, first), reads=[LB, triB], writes=[PSB[n % 4]], inc=first)
                                if not first:
                                    H, HB = hiT[(n - 1) % 3]
                                    fw.op(pe, mm(ps[:, n % 4, :], sel33[:], H[:], False, True), reads=[HB, selB], writes=[PSB[n % 4]])
                                A, AB = At[n % 2]
                                fw.op(act, lambda e: e.activation(out=A[:], in_=ps[:, n % 4, :], func=AF.Exp), reads=[PSB[n % 4]], writes=[AB])
                                if diag:
                                    fw.op(pool, lambda e: e.tensor_tensor(out=A[:], in0=A[:], in1=amask[:, kb - 4 * qc, :], op=ALU.mult), reads=[AB, amaskB], writes=[AB])
                            m = n - 1
                            if 0 <= m < NB:
                                qc, kb = blocks[m]
                                first, last = kb == 4 * qc + 3, kb == 0
                                A, AB = At[m % 2]
                                ob_ = 6 + qc % 2
                                fw.op(pe, mm(ps[:, ob_, :], V[:, kb, :], A[:], first, last), reads=[VB_, AB], writes=[PSB[ob_]])
                                if last:
                                    fw.op(dve, lambda e: e.tensor_copy(out=OT[:, h, qc * 512:(qc + 1) * 512], in_=ps[:, ob_, :]), reads=[PSB[ob_]], writes=[OTB])
                if stop == ("attn", layer):
                    break
                proj_fm(ph, OT, OTB, 16, a_w_o[layer], D, make_resid_evac(ph, xsrc), "wo")
        else:
            if not kv_done:
                kv_done = True
                with Phase(fw) as ph:
                    kvn, kvnB = ph.tile([128, 16, S], BF16, "kvn")
                    do_norm(xT, 12, kvn, kvnB)
                    if stop == ("kvn", layer):
                        break
                    with Phase(fw) as ph2:
                        kouts = [ph2.tile([128, S], BF16, "kout") for _ in range(2)]
                        stk = {"n": 0}

                        def dst_of(c):
                            return kouts[stk["n"] % 2]

                        def after(c, half, dst, dstB):
                            if half == 1:
                                fw.dma(sp, kTB[c * 128:(c + 1) * 128, :], dst[:], reads=[dstB], owner=dstB)
                                stk["n"] += 1
                        proj_fm(ph2, kvn, kvnB, 16, b_w_kv[:, 0:1536], 1536, make_rotary_evac(ph2, dst_of, after), "wk")
                    if stop == ("kvk", layer):
                        break
                    with Phase(fw) as ph2:
                        wv = [ph2.tile([128, 16, 512], BF16, "wvb") for _ in range(2)]
                        vbig, vbigB = ph2.tile([128, 16, 512], BF16, "vbigb")
                        prm, prmB = ph2.tile([128, 16, 1024], BF16, "perm")
                        nb = 0
                        for g in range(3):
                            dil = (1, 4, 16)[g]
                            Wt, WB = wv[g % 2]
                            fw.dma(pool, Wt[:], kview(b_w_kv[:, 1536 + g * 512:1536 + (g + 1) * 512]), writes=[WB])
                            for half in range(2):
                                if dil == 1:
                                    src, srcB, off = kvn, kvnB, half * 1024
                                else:
                                    for kc in range(16):
                                        eng = dve if kc % 2 == 0 else pool
                                        fw.op(eng, lambda e: e.tensor_copy(
                                            out=prm[:, kc, :].rearrange("p (r i) -> p r i", r=dil // 2),
                                            in_=kvn[:, kc, :].rearrange("p (i r) -> p r i", r=dil)[:, half * (dil // 2):(half + 1) * (dil // 2), :]),
                                            reads=[kvnB], writes=[prmB])
                                    src, srcB, off = prm, prmB, 0
                                for tb in range(8):
                                    bk = 4 + nb % 4
                                    nb += 1
                                    for kc in range(16):
                                        fw.op(pe, mm(ps[:, bk, :], src[:, kc, off + tb * 128:off + (tb + 1) * 128], Wt[:, kc, :], kc == 0, kc == 15),
                                              reads=[WB, srcB], writes=[PSB[bk]], inc=(kc == 15))
                                    if tb % 2 == 0:
                                        fw.op(act, lambda e: e.activation(out=vbig[:, half * 8 + tb, :], in_=ps[:, bk, :], func=AF.Copy), reads=[PSB[bk]], writes=[vbigB])
                                    else:
                                        fw.op(dve, lambda e: e.tensor_copy(out=vbig[:, half * 8 + tb, :], in_=ps[:, bk, :]), reads=[PSB[bk]], writes=[vbigB])
                            fw.dma(sp, VB[g].rearrange("(tb tp) f -> tp tb f", tp=128), vbig[:], reads=[vbigB], owner=vbigB)
            if stop == ("kv", layer):
                break
            j = layer - 2
            with Phase(fw) as ph:
                qTB, qTBB = ph.tile([128, 12, S], BF16, "qTB")
                with Phase(fw) as ph2:
                    hT, hTB = ph2.tile([128, 16, S], BF16, "hTb")
                    do_norm(xT, layer, hT, hTB)
                    proj_fm(ph2, hT, hTB, 16, b_w_q[j], 1536, make_rotary_evac(ph2, lambda c: (qTB[:, c, :], qTBB)), "wq")
                if stop == ("q", layer):
                    break
                OTb, OTbB = ph.tile([128, 12, S], BF16, "OTb")
                with Phase(fw) as ph2:
                    kts = [ph2.tile([128, S], BF16, "kt") for _ in range(2)]
                    vts = [ph2.tile([128, 16, 128], BF16, "vt") for _ in range(2)]
                    Es = [ph2.tile([128, 1024], BF16, "E") for _ in range(2)]
                    Ng = [ph2.tile([128, S], F32, "Ng") for _ in range(3)]
                    Zs, ZsB = ph2.tile([128, S], F32, "Zs")
                    seq = [(hg, g) for hg in range(4) for g in range(3)]

                    def load_kv(i):
                        hg, g = seq[i]
                        c = g * 4 + hg
                        fw.dma(sp, kts[i % 2][0][:], kTB[c * 128:(c + 1) * 128, :], writes=[kts[i % 2][1]])
                        fw.dma(sp, vts[i % 2][0][:], VB[g].rearrange("(kb kp) f -> kp kb f", kp=128)[:, :, hg * 128:(hg + 1) * 128], writes=[vts[i % 2][1]])
                    load_kv(0)
                    nsb = 0
                    for i, (hg, g) in enumerate(seq):
                        if i + 1 < len(seq):
                            load_kv(i + 1)
                        c = g * 4 + hg
                        dil = (1, 4, 16)[g]
                        sub = S // dil
                        Kt, KtB = kts[i % 2]
                        Vt, VtB = vts[i % 2]
                        N_, NB_ = Ng[g]
                        for sb in range(4):
                            k2 = nsb % 2
                            nsb += 1
                            sbk = 2 * k2
                            nbk, zbk = 4 + 2 * k2, 5 + 2 * k2
                            E, EB = Es[k2]
                            nfirst = []
                            for ql in range(4):
                                pos = sb * 512 + ql * 128
                                nblk = (pos % sub) // 128
                                nfirst.append(nblk == 0)
                                pprev = pos if nblk == 0 else pos - 128
                                col = ql * 256
                                bank, off = sbk + col // 512, col % 512
                                fw.op(pe, mm(ps[:, bank, off:off + 128], Kt[:, pprev:pprev + 128], qTB[:, c, pos:pos + 128], True, True),
                                      reads=[KtB, qTBB], writes=[PSB[bank]], inc=False)
                                fw.op(pe, mm(ps[:, bank, off + 128:off + 256], Kt[:, pos:pos + 128], qTB[:, c, pos:pos + 128], True, True),
                                      reads=[KtB, qTBB], writes=[PSB[bank]], inc=(ql == 3))
                            fw.op(act, lambda e: e.activation(out=v1024(E[:]), in_=ps[:, sbk:sbk + 2, :], func=AF.Exp, scale=QSCALE), reads=[PSB[sbk], PSB[sbk + 1]], writes=[EB])
                            mv = 2 if all(nfirst) else (0 if nfirst[0] else 1)
                            fw.op(pool, lambda e: e.tensor_tensor(out=E[:], in0=E[:], in1=bmask[:, mv, :], op=ALU.mult), reads=[EB, bmaskB], writes=[EB])
                            for (bank_, lhs_of) in ((nbk, lambda kbi: Vt[:, kbi, :]), (zbk, lambda kbi: ones_b[:])):
                                for ql in range(4):
                                    pos = sb * 512 + ql * 128
                                    kbi = pos // 128
                                    col = ql * 256
                                    if not nfirst[ql]:
                                        fw.op(pe, mm(ps[:, bank_, ql * 128:(ql + 1) * 128], lhs_of(kbi - 1), E[:, col:col + 128], True, False),
                                              reads=[EB, VtB, onesB], writes=[PSB[bank_]], inc=False)
                                    fw.op(pe, mm(ps[:, bank_, ql * 128:(ql + 1) * 128], lhs_of(kbi), E[:, col + 128:col + 256], nfirst[ql], True),
                                          reads=[EB, VtB, onesB], writes=[PSB[bank_]], inc=(ql == 3))

                            def tview(t):
                                if dil == 1:
                                    return t[:, sb * 512:(sb + 1) * 512]
                                if dil == 4:
                                    return t[:, :].rearrange("p (i r) -> p r i", r=4)[:, sb, :]
                                return t[:, :].rearrange("p (i r) -> p r i", r=16)[:, 4 * sb:4 * sb + 4, :]

                            def pview(bank__):
                                if dil == 16:
                                    return ps[:, bank__, :].rearrange("p (r i) -> p r i", r=4)
                                return ps[:, bank__, :]
                            fw.op(act, lambda e: e.activation(out=tview(N_), in_=pview(nbk), func=AF.Copy), reads=[PSB[nbk]], writes=[NB_])
                            if g == 0:
                                fw.op(dve, lambda e: e.tensor_copy(out=tview(Zs), in_=pview(zbk)), reads=[PSB[zbk]], writes=[ZsB])
                            else:
                                fw.op(dve, lambda e: e.tensor_tensor(out=tview(Zs), in0=tview(Zs), in1=pview(zbk), op=ALU.add), reads=[PSB[zbk], ZsB], writes=[ZsB])
                        if g == 2:
                            fw.op(dve, lambda e: e.reciprocal(out=Zs[:], in_=Zs[:]), reads=[ZsB], writes=[ZsB])
                            for g2 in range(3):
                                fw.op(dve, lambda e: e.tensor_tensor(out=OTb[:, g2 * 4 + hg, :], in0=Ng[g2][0][:], in1=Zs[:], op=ALU.mult),
                                      reads=[Ng[g2][1], ZsB], writes=[OTbB])
                if dbg:
                    dO = nc.dram_tensor("dbg_OTb%d" % layer, [1536, S], BF16, kind="ExternalOutput").ap()
                    fw.dma(sp, dO.rearrange("(c p) t -> p c t", p=128), OTb[:], reads=[OTbB], owner=OTbB)
                if stop == ("attn", layer):
                    break
                proj_fm(ph, OTb, OTbB, 12, b_w_o[j], D, make_resid_evac(ph, xT), "wob")
        xsrc = xT
        snap("dbg_attn%d" % layer)
        if stop == ("mix", layer):
            break
        with Phase(fw) as ph:
            hT, hTB = ph.tile([128, 16, S], BF16, "h2T")
            wg = [ph.tile([128, 16, 256], BF16, "wg") for _ in range(2)]
            wvv = [ph.tile([128, 16, 256], BF16, "wvv") for _ in range(2)]
            Wup = ffn_w_up[layer]

            def load_w(t):
                fw.dma(pool, wg[t % 2][0][:], kview(Wup)[:, :, t * 256:(t + 1) * 256], writes=[wg[t % 2][1]])
                fw.dma(pool, wvv[t % 2][0][:], kview(Wup)[:, :, DFF + t * 256:DFF + (t + 1) * 256], writes=[wvv[t % 2][1]])
            load_w(0)
            do_norm(xT, 4 + layer, hT, hTB)
            Ug = [ph.tile([128, S + 2], F32, "Ug") for _ in range(2)]
            Uv = [ph.tile([128, S + 2], F32, "Uv") for _ in range(2)]
            accg, accgB = ph.tile([128, S], F32, "accg")
            accv, accvB = ph.tile([128, S], F32, "accv")
            aout = [ph.tile([128, S], BF16, "aout") for _ in range(2)]
            for U, UB in Ug + Uv:
                fw.op(dve, lambda e: e.memset(U[:, 0:2], 0.0), writes=[UB])
            for jf in range(NFC):
                t, jj = jf // 2, jf % 2
                if jj == 0 and t + 1 < NFC // 2:
                    load_w(t + 1)
                WG, WGB = wg[t % 2]
                WV, WVB = wvv[t % 2]
                UG, UGB = Ug[jf % 2]
                UV, UVB = Uv[jf % 2]
                cs = slice(jj * 128, (jj + 1) * 128)
                for half in range(2):
                    b0 = 4 * half
                    for kc in range(16):
                        for t2 in range(2):
                            tcs = slice((half * 2 + t2) * 512, (half * 2 + t2 + 1) * 512)
                            fw.op(pe, mm(ps[:, b0 + t2, :], WG[:, kc, cs], hT[:, kc, tcs], kc == 0, kc == 15), reads=[WGB, hTB], writes=[PSB[b0 + t2]], inc=False)
                            fw.op(pe, mm(ps[:, b0 + 2 + t2, :], WV[:, kc, cs], hT[:, kc, tcs], kc == 0, kc == 15), reads=[WVB, hTB], writes=[PSB[b0 + 2 + t2]],
                                  inc=(kc == 15 and t2 == 1))
                    us = slice(2 + half * 1024, 2 + (half + 1) * 1024)
                    fw.op(act, lambda e: e.activation(out=v1024(UG[:, us]), in_=ps[:, b0:b0 + 2, :], func=AF.Copy), reads=[PSB[b0], PSB[b0 + 1]], writes=[UGB])
                    fw.op(act, lambda e: e.activation(out=v1024(UV[:, us]), in_=ps[:, b0 + 2:b0 + 4, :], func=AF.Copy), reads=[PSB[b0 + 2], PSB[b0 + 3]], writes=[UVB])
                for (U, UB, acc, accB, ci) in ((UG, UGB, accg, accgB, jf), (UV, UVB, accv, accvB, NFC + jf)):
                    fw.op(act, lambda e: e.activation(out=acc[:], in_=U[:, 2:S + 2], func=AF.Identity, scale=convw[:, layer, 2, ci:ci + 1], bias=convb[:, layer, ci:ci + 1]),
                          reads=[UB, convwB, convbB], writes=[accB])
                    fw.op(dve, lambda e: e.scalar_tensor_tensor(out=acc[:], in0=U[:, 1:S + 1], scalar=convw[:, layer, 1, ci:ci + 1], in1=acc[:], op0=ALU.mult, op1=ALU.add),
                          reads=[UB, convwB, accB], writes=[accB])
                    fw.op(dve, lambda e: e.scalar_tensor_tensor(out=acc[:], in0=U[:, 0:S], scalar=convw[:, layer, 0, ci:ci + 1], in1=acc[:], op0=ALU.mult, op1=ALU.add),
                          reads=[UB, convwB, accB], writes=[accB])
                fw.op(act, lambda e: e.activation(out=accg[:], in_=accg[:], func=AF.Silu), reads=[accgB], writes=[accgB])
                AO, AOB = aout[jf % 2]
                fw.op(dve, lambda e: e.tensor_tensor(out=AO[:], in0=accg[:], in1=accv[:], op=ALU.mult), reads=[accgB, accvB], writes=[AOB])
                fw.dma(sp, AT[jf * 128:(jf + 1) * 128, :], AO[:], reads=[AOB], owner=AOB)
        with Phase(fw) as ph:
            wd = [ph.tile([128, 22, 512], BF16, "wd") for _ in range(4)]
            at = [ph.tile([128, 22, 512], BF16, "at") for _ in range(2)]
            xt4 = [ph.tile([128, 4, 512], F32, "xt4") for _ in range(2)]
            Wdn = ffn_w_down[layer]
            ATv = AT.rearrange("(fc fp) t -> fp fc t", fp=128)
            xTv = xT.rearrange("(dc dp) t -> dp dc t", dp=128)

            def load_wd(g4):
                for kh in range(2):
                    W_, WB_ = wd[(g4 % 2) * 2 + kh]
                    fw.dma(pool, W_[:], kview(Wdn)[:, kh * 22:(kh + 1) * 22, g4 * 512:(g4 + 1) * 512], writes=[WB_])

            def load_at(tc, kh):
                fw.dma(sp, at[kh][0][:], ATv[:, kh * 22:(kh + 1) * 22, tc * 512:(tc + 1) * 512], writes=[at[kh][1]])
            load_wd(0)
            load_at(0, 0)
            load_at(0, 1)
            step = 0
            for g4 in range(4):
                if g4 + 1 < 4:
                    load_wd(g4 + 1)
                for tc in range(4):
                    b0 = 4 * (step % 2)
                    X4, X4B = xt4[step % 2]
                    fw.dma(sp, X4[:], xTv[:, g4 * 4:(g4 + 1) * 4, tc * 512:(tc + 1) * 512], writes=[X4B])
                    for kh in range(2):
                        W_, WB_ = wd[(g4 % 2) * 2 + kh]
                        A_, AB_ = at[kh]
                        for fc in range(22):
                            for dc in range(4):
                                fw.op(pe, mm(ps[:, b0 + dc, :], W_[:, fc, dc * 128:(dc + 1) * 128], A_[:, fc, :], kh == 0 and fc == 0, kh == 1 and fc == 21),
                                      reads=[WB_, AB_], writes=[PSB[b0 + dc]], inc=(fc == 21 and dc == 3))
                        nxt = step + 1
                        if nxt < 16:
                            load_at(nxt % 4, kh)
                    fw.op(dve, lambda e: e.tensor_tensor(out=X4[:], in0=X4[:], in1=ps[:, b0:b0 + 4, :], op=ALU.add), reads=[X4B] + PSB[b0:b0 + 4], writes=[X4B])
                    fw.dma(sp, xTv[:, g4 * 4:(g4 + 1) * 4, tc * 512:(tc + 1) * 512], X4[:], reads=[X4B], owner=X4B)
                    step += 1
        snap("dbg_ffn%d" % layer)
        if stop == ("ffn", layer):
            break
        with Phase(fw) as ph:
            hT, hTB = ph.tile([128, 16, S], BF16, "h3T")
            pTb, pTbB = ph.tile([128, 2, S], BF16, "pTb")
            fw.dma(pool, pTb[:], kview(pT[layer]), writes=[pTbB])
            wgt = [ph.tile([128, 16, 512], BF16, "wgt") for _ in range(2)]
            wpj = [ph.tile([128, 2, 512], BF16, "wpj") for _ in range(2)]

            def load_g(g4):
                fw.dma(pool, wgt[g4 % 2][0][:], kview(ple_w_gate[layer])[:, :, g4 * 512:(g4 + 1) * 512], writes=[wgt[g4 % 2][1]])
                fw.dma(pool, wpj[g4 % 2][0][:], kview(ple_w_proj[layer])[:, :, g4 * 512:(g4 + 1) * 512], writes=[wpj[g4 % 2][1]])
            load_g(0)
            do_norm(xT, 8 + layer, hT, hTB)
            sgs = [ph.tile([128, 1024], F32, "sg") for _ in range(2)]
            xts = [ph.tile([128, S], F32, "xp") for _ in range(2)]

            cnt = 0
            for g4 in range(4):
                if g4 + 1 < 4:
                    load_g(g4 + 1)
                WG, WGB = wgt[g4 % 2]
                WP, WPB = wpj[g4 % 2]
                for dcl in range(4):
                    dc = g4 * 4 + dcl
                    X, XB = xts[dc % 2]
                    fw.dma(sp, X[:], xT[dc * 128:(dc + 1) * 128, :], writes=[XB])
                    cs = slice(dcl * 128, (dcl + 1) * 128)
                    for half in range(2):
                        b0 = 4 * (cnt % 2)
                        SG, SGB = sgs[cnt % 2]
                        cnt += 1
                        for kc in range(16):
                            for t2 in range(2):
                                tcs = slice((half * 2 + t2) * 512, (half * 2 + t2 + 1) * 512)
                                fw.op(pe, mm(ps[:, b0 + t2, :], WG[:, kc, cs], hT[:, kc, tcs], kc == 0, kc == 15), reads=[WGB, hTB], writes=[PSB[b0 + t2]], inc=False)
                        for kc in range(2):
                            for t2 in range(2):
                                tcs = slice((half * 2 + t2) * 512, (half * 2 + t2 + 1) * 512)
                                fw.op(pe, mm(ps[:, b0 + 2 + t2, :], WP[:, kc, cs], pTb[:, kc, tcs], kc == 0, kc == 1), reads=[WPB, pTbB], writes=[PSB[b0 + 2 + t2]],
                                      inc=(kc == 1 and t2 == 1))
                        hs = slice(half * 1024, (half + 1) * 1024)
                        fw.op(act, lambda e: e.activation(out=v1024(SG[:]), in_=ps[:, b0:b0 + 2, :], func=AF.Sigmoid), reads=[PSB[b0], PSB[b0 + 1]], writes=[SGB])
                        fw.op(dve, lambda e: e.tensor_tensor(out=v1024(SG[:]), in0=v1024(SG[:]), in1=ps[:, b0 + 2:b0 + 4, :], op=ALU.mult),
                              reads=[SGB, PSB[b0 + 2], PSB[b0 + 3]], writes=[SGB])
                        fw.op(pool, lambda e: e.tensor_tensor(out=X[:, hs], in0=X[:, hs], in1=SG[:], op=ALU.add), reads=[SGB, XB], writes=[XB])
                    fw.dma(sp, xT[dc * 128:(dc + 1) * 128, :], X[:], reads=[XB], owner=XB)
        snap("dbg_ple%d" % layer)
        if stop == ("ple", layer):
            break
    else:
        do_norm(xT, 13, out_dram=yT)
    fw.barrier()
    fw.close()
    return nc


def _consts():
    j = np.arange(128)
    tri = -(j[:, None] >= j[None, :]).astype(np.float32)
    ones = np.ones((128, 128), np.float32)
    t = np.arange(512)
    amask = np.stack([((128 * i + j)[:, None] < t[None, :]).astype(np.float32) for i in range(4)], axis=1)
    up = (j[:, None] >= j[None, :]).astype(np.float32)
    lo = (j[:, None] <= j[None, :]).astype(np.float32)
    F_ = np.concatenate([np.zeros((128, 128), np.float32), lo], 1)
    R_ = np.concatenate([up, lo], 1)
    bmask = np.stack([np.concatenate([F_, R_, R_, R_], 1), np.concatenate([R_] * 4, 1), np.concatenate([F_] * 4, 1)], axis=1)
    sel33 = np.zeros((128, 128), np.float32)
    sel33[0] = -1.0
    sel33[32] = -1.0
    perm = np.zeros((32, 32), np.float32)
    for m in range(32):
        if m < 16:
            perm[m + 16, m] = -1.0
        else:
            perm[m - 16, m] = 1.0
    invf = (500000.0 ** (-np.arange(0, 32, 2, dtype=np.float32) / 32)).astype(np.float32)
    invf = np.concatenate([invf, invf]).reshape(32, 1)
    return dict(c_tri=tri, c_ones=ones, c_amask=np.ascontiguousarray(amask.reshape(128, -1)),
                c_bmask=np.ascontiguousarray(bmask.reshape(128, -1)), c_sel33=sel33, c_perm=perm, c_invf=invf)


def _pack(inputs):
    f = lambda a: np.ascontiguousarray(np.asarray(a, dtype=np.float32))
    nw = np.concatenate([np.asarray(inputs["attn_norm_w"]), np.asarray(inputs["ffn_norm_w"]), np.asarray(inputs["ple_norm_w"]),
                         np.asarray(inputs["kv_norm_w"])[None], np.asarray(inputs["final_norm_w"])[None]], 0)
    normw = f(nw.reshape(14, 16, 128).transpose(2, 0, 1).reshape(128, -1))
    convw = f(np.asarray(inputs["ffn_conv_w"]).reshape(4, 3, 88, 128).transpose(3, 0, 1, 2).reshape(128, -1))
    convb = f(np.asarray(inputs["ffn_conv_b"]).reshape(4, 88, 128).transpose(2, 0, 1).reshape(128, -1))
    shared = dict(normw=normw, convw=convw, convb=convb)
    for k in ("a_w_qkv", "a_w_o", "b_w_kv", "b_w_q", "b_w_o", "ffn_w_up", "ffn_w_down", "ple_w_gate", "ple_w_proj"):
        shared[k] = f(inputs[k])
    shared.update(_consts())
    return shared


def _core_inputs(inputs, shared, b):
    m = dict(shared)
    m["xT_in"] = np.ascontiguousarray(np.asarray(inputs["x"][b], dtype=np.float32).T)
    m["pT"] = np.ascontiguousarray(np.asarray(inputs["p"][:, b], dtype=np.float32).transpose(0, 2, 1))
    m["pos32"] = np.ascontiguousarray(np.broadcast_to(np.asarray(inputs["positions"][b], dtype=np.int32)[None, :], (32, S)))
    return m


_NC = None


def kernel(**inputs):
    global _NC
    if _NC is None:
        _NC = build_program()
    shared = _pack(inputs)
    nb = inputs["x"].shape[0]
    in_maps = [_core_inputs(inputs, shared, b) for b in range(nb)]
    res = run_bass_kernel_spmd(_NC, in_maps, core_ids=list(range(nb)))
    out = np.stack([np.asarray(r["yT"]).T for r in res.results], 0)
    return np.ascontiguousarray(out.astype(np.float32))
```

```python
from contextlib import ExitStack
import numpy as np
import concourse.bass as bass
import concourse.mybir as mybir
from concourse.bass_utils import run_bass_kernel_spmd

F32 = mybir.dt.float32
BF16 = mybir.dt.bfloat16
I32 = mybir.dt.int32
AF = mybir.ActivationFunctionType
ALU = mybir.AluOpType

D = 2048
S = 2048
DFF = 5632
NFC = DFF // 128
QSCALE = 128.0 ** -0.5
PI = float(np.pi)
SNAP = False


class DSem:
    __slots__ = ("sem", "cnt")

    def __init__(self, sem):
        self.sem, self.cnt = sem, 0


class Buf:
    __slots__ = ("name", "w", "r", "ds", "excl")

    def __init__(self, name="", excl=False):
        self.name, self.w, self.r, self.ds, self.excl = name, [], [], None, excl


class Eng:
    def __init__(self, name, obj, sem):
        self.name, self.obj, self.sem = name, obj, sem
        self.cnt = 0
        self.pending = False
        self.seen = {}

    def wait(self, tok):
        sem, val = tok
        if sem is self.sem and self.name == "pe":
            return
        k = id(sem)
        if self.seen.get(k, 0) >= val:
            return
        self.seen[k] = val
        self.obj.wait_ge(sem, val)


class FW:
    def __init__(self, nc):
        self.nc = nc
        self._cms = []
        self.pe = Eng("pe", nc.tensor, self._sem("s_pe"))
        self.act = Eng("act", nc.scalar, self._sem("s_act"))
        self.dve = Eng("dve", nc.vector, self._sem("s_dve"))
        self.pool = Eng("pool", nc.gpsimd, self._sem("s_pool"))
        self.sp = Eng("sp", nc.sync, self._sem("s_sp"))
        self.engs = [self.pe, self.act, self.dve, self.pool, self.sp]
        self.free_ds, self.all_ds, self.live = [], [], []
        self.uid = 0

    def _sem(self, name):
        cm = self.nc.semaphore(name)
        self._cms.append(cm)
        return cm.__enter__()

    def close(self):
        for cm in reversed(self._cms):
            cm.__exit__(None, None, None)

    def _ds(self, buf):
        if buf.ds is None:
            if self.free_ds:
                buf.ds = self.free_ds.pop()
            else:
                buf.ds = DSem(self._sem("d%d" % len(self.all_ds)))
                self.all_ds.append(buf.ds)
            self.live.append(buf)
        return buf.ds

    def _deps(self, eng, reads, writes, nosw=False):
        for b in reads:
            for t in b.w:
                eng.wait(t)
            if b.excl:
                for t in b.r:
                    if t[0] is not eng.sem:
                        eng.wait(t)
        for b in writes:
            for t in b.w:
                if not (nosw and t[0] is eng.sem):
                    eng.wait(t)
            for t in b.r:
                if not (nosw and t[0] is eng.sem):
                    eng.wait(t)

    def _mark(self, tok, reads, writes):
        for b in reads:
            b.r.append(tok)
            if len(b.r) > 10:
                best = {}
                for s_, v in b.r:
                    if id(s_) not in best or best[id(s_)][1] < v:
                        best[id(s_)] = (s_, v)
                b.r = list(best.values())
        for b in writes:
            b.w = [tok]
            b.r = []

    def op(self, eng, fn, reads=(), writes=(), inc=True, nosw=False):
        self._deps(eng, reads, writes, nosw)
        ins = fn(eng.obj)
        if inc:
            eng.cnt += 1
            ins.then_inc(eng.sem, 1)
            eng.pending = False
            tok = (eng.sem, eng.cnt)
        else:
            eng.pending = True
            tok = (eng.sem, eng.cnt + 1)
        self._mark(tok, reads, writes)
        return ins

    def dma(self, q, out_ap, in_ap, reads=(), writes=(), owner=None):
        owner = owner or (writes[0] if writes else reads[0])
        ds = self._ds(owner)
        self._deps(q, reads, writes)
        ins = q.obj.dma_start(out=out_ap, in_=in_ap)
        ds.cnt += 16
        ins.then_inc(ds.sem, 16)
        self._mark((ds.sem, ds.cnt), reads, writes)
        return ins

    def barrier(self):
        assert not self.pe.pending
        toks = [(e.sem, e.cnt) for e in self.engs if e.cnt > 0]
        toks += [(d.sem, d.cnt) for d in self.all_ds if d.cnt > 0]
        for e in self.engs:
            for t in toks:
                e.wait(t)
        for b in self.live:
            self.free_ds.append(b.ds)
            b.ds = None
        self.live = []


class Phase:
    def __init__(self, fw):
        self.fw, self.es = fw, ExitStack()

    def __enter__(self):
        return self

    def tile(self, shape, dtype, name="t"):
        self.fw.uid += 1
        t = self.es.enter_context(self.fw.nc.sbuf_tensor("%s_%d" % (name, self.fw.uid), list(shape), dtype))
        return t, Buf(name)

    def __exit__(self, *a):
        self.fw.barrier()
        self.es.close()
        return False


def build_program(nlayers=4, dbg=False, stop=None, start=0, lite=()):
    nc = bass.Bass("TRN2", target_bir_lowering=False)
    fw = FW(nc)
    pe, act, dve, pool, sp = fw.pe, fw.act, fw.dve, fw.pool, fw.sp

    def din(name, shape, dt=F32):
        if name in lite:
            return nc.dram_tensor(name, [1] * len(shape), dt, kind="Internal").ap()
        return nc.dram_tensor(name, list(shape), dt, kind="ExternalInput").ap()

    def dscr(name, shape, dt):
        return nc.dram_tensor(name, list(shape), dt, kind=("ExternalOutput" if dbg else "Internal")).ap()

    xT_in = din("xT_in", [D, S])
    pT = din("pT", [4, 256, S])
    pos32 = din("pos32", [32, S], I32)
    normw_d = din("normw", [128, 14 * 16])
    convw_d = din("convw", [128, 4 * 3 * 88])
    convb_d = din("convb", [128, 4 * 88])
    a_w_qkv = din("a_w_qkv", [2, D, 3 * D])
    a_w_o = din("a_w_o", [2, D, D])
    b_w_kv = din("b_w_kv", [D, 3072])
    b_w_q = din("b_w_q", [2, D, 1536])
    b_w_o = din("b_w_o", [2, 1536, D])
    ffn_w_up = din("ffn_w_up", [4, D, 2 * DFF])
    ffn_w_down = din("ffn_w_down", [4, DFF, D])
    ple_w_gate = din("ple_w_gate", [4, D, D])
    ple_w_proj = din("ple_w_proj", [4, 256, D])
    c_tri = din("c_tri", [128, 128])
    c_ones = din("c_ones", [128, 128])
    c_amask = din("c_amask", [128, 4 * 512])
    c_bmask = din("c_bmask", [128, 3 * 1024])
    c_sel33 = din("c_sel33", [128, 128])
    c_perm = din("c_perm", [32, 32])
    c_invf = din("c_invf", [32, 1])
    yT = nc.dram_tensor("yT", [D, S], F32, kind="ExternalOutput").ap()

    xT = dscr("xT_s", [D, S], F32)
    qT = dscr("qT_s", [D, S], BF16)
    kT = dscr("kT_s", [D, S], BF16)
    Vd = dscr("V_s", [S, D], BF16)
    AT = dscr("AT_s", [DFF, S], BF16)
    kTB = dscr("kTB_s", [1536, S], BF16)
    VB = dscr("VB_s", [3, S, 512], BF16)

    def ptile(name, shape, dt):
        return nc.alloc_sbuf_tensor(name, list(shape), dt), Buf(name)

    ones_b, onesB = ptile("ones_b", [128, 128], BF16)
    tri_b, triB = ptile("tri_b", [128, 128], BF16)
    amask, amaskB = ptile("amask", [128, 4, 512], BF16)
    bmask, bmaskB = ptile("bmask", [128, 3, 1024], BF16)
    sel33, selB = ptile("sel33", [128, 128], BF16)
    perm32, permB = ptile("perm32", [32, 32], BF16)
    invf, invfB = ptile("invf", [32, 1], F32)
    normw, normwB = ptile("normw_t", [128, 14, 16], F32)
    convw, convwB = ptile("convw_t", [128, 4, 3, 88], F32)
    convb, convbB = ptile("convb_t", [128, 4, 88], F32)
    COS, cosB = ptile("COS", [32, S], F32)
    SIN, sinB = ptile("SIN", [32, S], F32)
    ps = nc.alloc_psum_tensor("ps", [128, 8, 512], F32)
    PSB = [Buf("ps%d" % i, excl=True) for i in range(8)]

    with Phase(fw) as ph:
        fw.dma(pool, ones_b[:], c_ones, writes=[onesB])
        fw.dma(pool, tri_b[:], c_tri, writes=[triB])
        fw.dma(pool, amask[:], c_amask.rearrange("p (i t) -> p i t", i=4), writes=[amaskB])
        fw.dma(pool, bmask[:], c_bmask.rearrange("p (i t) -> p i t", i=3), writes=[bmaskB])
        fw.dma(pool, sel33[:], c_sel33, writes=[selB])
        fw.dma(pool, perm32[:], c_perm, writes=[permB])
        fw.dma(sp, invf[:], c_invf, writes=[invfB])
        fw.dma(sp, normw[:], normw_d.rearrange("p (a b) -> p a b", a=14), writes=[normwB])
        fw.dma(sp, convw[:], convw_d.rearrange("p (a b c) -> p a b c", a=4, b=3), writes=[convwB])
        fw.dma(sp, convb[:], convb_d.rearrange("p (a b) -> p a b", a=4), writes=[convbB])
        pi_t, piB = ph.tile([32, S], I32, "posi")
        ang, angB = ph.tile([32, S], F32, "ang")
        tmp, tmpB = ph.tile([32, S], F32, "angt")
        fw.dma(sp, pi_t[:], pos32, writes=[piB])
        fw.op(dve, lambda e: e.tensor_copy(out=ang[:], in_=pi_t[:]), reads=[piB], writes=[angB])
        fw.op(dve, lambda e: e.tensor_scalar(out=ang[:], in0=ang[:], scalar1=invf[:, 0:1], scalar2=None, op0=ALU.mult),
              reads=[angB, invfB], writes=[angB])
        MAGIC = 12582912.0
        C1 = 6.28125
        C2 = 2.0 * PI - C1
        nf, nfB = ph.tile([32, S], F32, "nf")
        for (dst_t, dst_b, shift) in ((SIN, sinB, 0.0), (COS, cosB, 0.5 * PI)):
            fw.op(dve, lambda e: e.tensor_scalar(out=tmp[:], in0=ang[:], scalar1=shift, scalar2=None, op0=ALU.add), reads=[angB], writes=[tmpB])
            fw.op(dve, lambda e: e.tensor_scalar(out=nf[:], in0=tmp[:], scalar1=1.0 / (2.0 * PI), scalar2=MAGIC, op0=ALU.mult, op1=ALU.add), reads=[tmpB], writes=[nfB])
            fw.op(dve, lambda e: e.tensor_scalar(out=nf[:], in0=nf[:], scalar1=MAGIC, scalar2=None, op0=ALU.subtract), reads=[nfB], writes=[nfB])
            fw.op(dve, lambda e: e.scalar_tensor_tensor(out=tmp[:], in0=nf[:], scalar=-C1, in1=tmp[:], op0=ALU.mult, op1=ALU.add), reads=[nfB, tmpB], writes=[tmpB])
            fw.op(dve, lambda e: e.scalar_tensor_tensor(out=tmp[:], in0=nf[:], scalar=-C2, in1=tmp[:], op0=ALU.mult, op1=ALU.add), reads=[nfB, tmpB], writes=[tmpB])
            fw.op(act, lambda e: e.activation(out=dst_t[:], in_=tmp[:], func=AF.Sin), reads=[tmpB], writes=[dst_b])

    snapB = Buf("snap")

    def snap(name):
        if not (dbg and SNAP):
            return
        d_ = nc.dram_tensor(name, [D, S], F32, kind="ExternalOutput").ap()
        fw.barrier()
        fw.dma(sp, d_, xT, owner=snapB)
        fw.barrier()

    def kview(ap2d, rows=128):
        return ap2d.rearrange("(kc kp) f -> kp kc f", kp=rows)

    def mm(out, lhsT, rhs, start, stop):
        return lambda e: e.matmul(out, lhsT=lhsT, rhs=rhs, start=start, stop=stop)

    def do_norm(xsrc, widx, hT=None, hTB=None, out_dram=None):
        NT = 256
        with Phase(fw) as ph:
            xs = [ph.tile([128, 16, NT], F32, "nx") for _ in range(2)]
            sq, sqB = ph.tile([128, 16, NT], BF16, "nsq")
            r1 = [ph.tile([128, NT], F32, "nr1") for _ in range(2)]
            r2 = [ph.tile([128, NT], F32, "nr2") for _ in range(2)]
            ob = [ph.tile([128, 16, NT], F32, "nob") for _ in range(2)] if out_dram is not None else None
            for tc in range(S // NT):
                X, XB = xs[tc % 2]
                tsl = slice(tc * NT, (tc + 1) * NT)
                fw.dma(sp, X[:], kview(xsrc)[:, :, tsl], writes=[XB])
                fw.op(act, lambda e: e.activation(out=sq[:], in_=X[:], func=AF.Square), reads=[XB], writes=[sqB])
                bk = tc % 2
                for kc in range(16):
                    fw.op(pe, mm(ps[:, bk, 0:NT], ones_b[:], sq[:, kc, :], kc == 0, kc == 15), reads=[sqB, onesB], writes=[PSB[bk]], inc=(kc == 15))
                R1, R1B = r1[tc % 2]
                R2, R2B = r2[tc % 2]
                fw.op(act, lambda e: e.activation(out=R1[:], in_=ps[:, bk, 0:NT], func=AF.Sqrt, scale=1.0 / D, bias=1e-6), reads=[PSB[bk]], writes=[R1B])
                fw.op(dve, lambda e: e.reciprocal(out=R2[:], in_=R1[:]), reads=[R1B], writes=[R2B])
                if out_dram is None:
                    for kc in range(16):
                        fw.op(dve, lambda e: e.scalar_tensor_tensor(out=hT[:, kc, tsl], in0=X[:, kc, :], scalar=normw[:, widx, kc:kc + 1], in1=R2[:], op0=ALU.mult, op1=ALU.mult),
                              reads=[XB, R2B, normwB], writes=[hTB], nosw=True)
                else:
                    O, OB = ob[tc % 2]
                    for kc in range(16):
                        fw.op(dve, lambda e: e.scalar_tensor_tensor(out=O[:, kc, :], in0=X[:, kc, :], scalar=normw[:, widx, kc:kc + 1], in1=R2[:], op0=ALU.mult, op1=ALU.mult),
                              reads=[XB, R2B, normwB], writes=[OB], nosw=(kc > 0))
                    fw.dma(sp, kview(out_dram)[:, :, tsl], O[:], reads=[OB], owner=OB)

    class PSet:
        def __init__(self):
            self.n = 0

        def next(self):
            s = self.n % 2
            self.n += 1
            return s

    def proj_prep(ph, KCn, W2d, wname="w"):
        wb = [ph.tile([128, KCn, 512], BF16, wname) for _ in range(2)]
        fw.dma(pool, wb[0][0][:], kview(W2d)[:, :, 0:512], writes=[wb[0][1]])
        return wb

    def proj_fm(ph, hT, hTB, KCn, W2d, ncols, evac, wname="w", wb=None):
        if wb is None:
            wb = proj_prep(ph, KCn, W2d, wname)
        pset = PSet()
        ng = ncols // 512
        for g in range(ng):
            Wt, WB = wb[g % 2]
            if g + 1 < ng:
                fw.dma(pool, wb[(g + 1) % 2][0][:], kview(W2d)[:, :, (g + 1) * 512:(g + 2) * 512], writes=[wb[(g + 1) % 2][1]])
            for fc in range(4):
                for half in range(2):
                    s = pset.next()
                    for kc in range(KCn):
                        for t2 in range(2):
                            tcs = slice((half * 2 + t2) * 512, (half * 2 + t2 + 1) * 512)
                            fw.op(pe, mm(ps[:, 2 * s + t2, :], Wt[:, kc, fc * 128:(fc + 1) * 128], hT[:, kc, tcs], kc == 0, kc == KCn - 1),
                                  reads=[WB, hTB], writes=[PSB[2 * s + t2]], inc=(kc == KCn - 1 and t2 == 1))
                    evac(g * 4 + fc, half, s)

    def v1024(ap):
        return ap.rearrange("p (b t) -> p b t", b=2)

    def make_resid_evac(ph, xsrc):
        xts = [ph.tile([128, S], F32, "xr") for _ in range(3)]
        st = {"n": 0}

        def evac(c, half, s):
            X, XB = xts[st["n"] % 3]
            if half == 0:
                fw.dma(sp, X[:], xsrc[c * 128:(c + 1) * 128, :], writes=[XB])
            hs = slice(half * 1024, (half + 1) * 1024)
            fw.op(dve, lambda e: e.tensor_tensor(out=v1024(X[:, hs]), in0=v1024(X[:, hs]), in1=ps[:, 2 * s:2 * s + 2, :], op=ALU.add),
                  reads=[XB, PSB[2 * s], PSB[2 * s + 1]], writes=[XB])
            if half == 1:
                fw.dma(sp, xT[c * 128:(c + 1) * 128, :], X[:], reads=[XB], owner=XB)
                st["n"] += 1
        return evac

    def make_rotary_evac(ph, dst_of, after=None):
        q32s = [ph.tile([32, 1024], BF16, "q32") for _ in range(2)]
        qlos = [ph.tile([32, 1024], BF16, "qlo") for _ in range(2)]
        qfs = [ph.tile([32, 1024], F32, "qf") for _ in range(2)]
        t2s = [ph.tile([32, 1024], F32, "rt2") for _ in range(2)]
        st = {"n": 0}

        def evac(c, half, s):
            k = st["n"] % 2
            st["n"] += 1
            dil = (1, 4, 16)[c // 4]
            ni = 1024 // dil
            dst, dstB = dst_of(c)
            Q, QB = q32s[k]
            T2, T2B = t2s[k]
            hs = slice(half * 1024, (half + 1) * 1024)
            src = [PSB[2 * s], PSB[2 * s + 1]]
            rb = 4 + 2 * k
            QL, QLB = qlos[k]
            QF, QFB = qfs[k]
            fw.op(act, lambda e: e.activation(out=v1024(QF[:]), in_=ps[0:32, 2 * s:2 * s + 2, :], func=AF.Copy), reads=src, writes=[QFB])
            fw.op(dve, lambda e: e.tensor_copy(out=Q[:], in_=QF[:]), reads=[QFB], writes=[QB])
            fw.op(dve, lambda e: e.tensor_tensor(out=QL[:], in0=QF[:], in1=Q[:], op=ALU.subtract), reads=[QFB, QB], writes=[QLB])
            for t2 in range(2):
                fw.op(pe, mm(ps[0:32, rb + t2, :], perm32[:], Q[:, t2 * 512:(t2 + 1) * 512], True, False), reads=[QB, permB], writes=[PSB[rb + t2]], inc=False)
                fw.op(pe, mm(ps[0:32, rb + t2, :], perm32[:], QL[:, t2 * 512:(t2 + 1) * 512], False, True), reads=[QLB, permB], writes=[PSB[rb + t2]], inc=(t2 == 1))
            fw.op(dve, lambda e: e.tensor_tensor(out=QF[:], in0=QF[:], in1=COS[:, hs], op=ALU.mult), reads=[QFB, cosB], writes=[QFB])
            T1, T1B = QF, QFB
            fw.op(dve, lambda e: e.tensor_tensor(out=v1024(T2[:]), in0=ps[0:32, rb:rb + 2, :], in1=v1024(SIN[:, hs]), op=ALU.mult),
                  reads=[PSB[rb], PSB[rb + 1], sinB], writes=[T2B])

            def dview(rows):
                if dil == 1:
                    return dst[rows, hs]
                return dst[rows, :].rearrange("p (r i) -> p r i", r=dil)[:, :, half * ni:(half + 1) * ni]

            def sview(ap):
                if dil == 1:
                    return ap
                return ap.rearrange("p (i r) -> p r i", r=dil)
            all_src = ps[:, 2 * s:2 * s + 2, :].rearrange("p b t -> p (b t)")
            fw.op(act, lambda e: e.activation(out=dview(slice(0, 128)), in_=sview(all_src), func=AF.Copy), reads=src, writes=[dstB])
            fw.op(dve, lambda e: e.tensor_tensor(out=dview(slice(0, 32)), in0=sview(T1[:]), in1=sview(T2[:]), op=ALU.add),
                  reads=[T1B, T2B], writes=[dstB])
            if after is not None:
                after(c, half, dst, dstB)
        return evac

    xsrc = xT_in
    kv_done = False
    if start > 0:
        xT = xT_in
        xsrc = xT
    for layer in range(start, nlayers):
        if layer < 2:
            with Phase(fw) as ph:
                hT, hTB = ph.tile([128, 16, S], BF16, "hT")
                wbqk = proj_prep(ph, 16, a_w_qkv[layer][:, 0:2 * D], "wqk")
                do_norm(xsrc, layer, hT, hTB)
                obs = [ph.tile([128, S], BF16, "qk_o") for _ in range(2)]
                stq = {"n": 0}

                def evac_qk(c, half, s):
                    O, OB = obs[stq["n"] % 2]
                    hs = slice(half * 1024, (half + 1) * 1024)
                    srcb = [PSB[2 * s], PSB[2 * s + 1]]
                    if c < 16:
                        fw.op(act, lambda e: e.activation(out=v1024(O[:, hs]), in_=ps[:, 2 * s:2 * s + 2, :], func=AF.Copy, scale=QSCALE), reads=srcb, writes=[OB])
                    else:
                        fw.op(dve, lambda e: e.tensor_copy(out=v1024(O[:, hs]), in_=ps[:, 2 * s:2 * s + 2, :]), reads=srcb, writes=[OB])
                    if half == 1:
                        dstd = qT if c < 16 else kT
                        cc = c % 16
                        fw.dma(sp, dstd[cc * 128:(cc + 1) * 128, :], O[:], reads=[OB], owner=OB)
                        stq["n"] += 1
                proj_fm(ph, hT, hTB, 16, a_w_qkv[layer][:, 0:2 * D], 2 * D, evac_qk, "wqk", wb=wbqk)
                wv = [ph.tile([128, 16, 512], BF16, "wv") for _ in range(2)]
                vbig = [ph.tile([128, 16, 512], BF16, "vbig") for _ in range(2)]
                Wv2d = a_w_qkv[layer][:, 2 * D:3 * D]
                fw.dma(pool, wv[0][0][:], kview(Wv2d)[:, :, 0:512], writes=[wv[0][1]])
                nb = 0
                for g in range(4):
                    Wt, WB = wv[g % 2]
                    if g + 1 < 4:
                        fw.dma(pool, wv[(g + 1) % 2][0][:], kview(Wv2d)[:, :, (g + 1) * 512:(g + 2) * 512], writes=[wv[(g + 1) % 2][1]])
                    VBt, VBB = vbig[g % 2]
                    for tb in range(16):
                        bk = 4 + nb % 4
                        nb += 1
                        for kc in range(16):
                            fw.op(pe, mm(ps[:, bk, :], hT[:, kc, tb * 128:(tb + 1) * 128], Wt[:, kc, :], kc == 0, kc == 15), reads=[WB, hTB], writes=[PSB[bk]], inc=(kc == 15))
                        if tb % 2 == 0:
                            fw.op(act, lambda e: e.activation(out=VBt[:, tb, :], in_=ps[:, bk, :], func=AF.Copy), reads=[PSB[bk]], writes=[VBB])
                        else:
                            fw.op(dve, lambda e: e.tensor_copy(out=VBt[:, tb, :], in_=ps[:, bk, :]), reads=[PSB[bk]], writes=[VBB])
                    fw.dma(sp, Vd.rearrange("(tb tp) f -> tp tb f", tp=128)[:, :, g * 512:(g + 1) * 512], VBt[:], reads=[VBB], owner=VBB)
            if stop == ("qkv", layer):
                break
            with Phase(fw) as ph:
                OT, OTB = ph.tile([128, 16, S], BF16, "OT")
                with Phase(fw) as ph2:
                    qh = [ph2.tile([128, S], BF16, "qh") for _ in range(2)]
                    kh = [ph2.tile([128, S], BF16, "kh") for _ in range(2)]
                    vh = [ph2.tile([128, 16, 128], BF16, "vh") for _ in range(2)]
                    eT = [ph2.tile([128, 512], F32, "eT") for _ in range(2)]
                    Lt = [ph2.tile([128, 512], BF16, "Lt") for _ in range(4)]
                    At = [ph2.tile([128, 512], BF16, "At") for _ in range(2)]
                    hiT = [ph2.tile([128, 512], BF16, "hiT") for _ in range(3)]
                    for H_, HB_ in hiT:
                        fw.op(dve, lambda e: e.memset(H_[:], 0.0), writes=[HB_])

                    def load_head(h):
                        fw.dma(sp, qh[h % 2][0][:], qT[h * 128:(h + 1) * 128, :], writes=[qh[h % 2][1]])
                        fw.dma(sp, kh[h % 2][0][:], kT[h * 128:(h + 1) * 128, :], writes=[kh[h % 2][1]])
                        fw.dma(sp, vh[h % 2][0][:], Vd.rearrange("(sb sp) f -> sp sb f", sp=128)[:, :, h * 128:(h + 1) * 128], writes=[vh[h % 2][1]])
                    load_head(0)
                    blocks = [(qc, kb) for qc in range(4) for kb in range(4 * qc + 3, -1, -1)]
                    NB = len(blocks)
                    for h in range(16):
                        if h + 1 < 16:
                            load_head(h + 1)
                        Q, QB = qh[h % 2]
                        K, KB = kh[h % 2]
                        V, VB_ = vh[h % 2]
                        for n in range(-3, NB + 2):
                            m = n + 3
                            if 0 <= m < NB:
                                qc, kb = blocks[m]
                                fw.op(pe, mm(ps[:, m % 4, :], K[:, kb * 128:(kb + 1) * 128], Q[:, qc * 512:(qc + 1) * 512], True, False),
                                      reads=[KB, QB], writes=[PSB[m % 4]])
                            m = n + 2
                            if 0 <= m < NB:
                                qc, kb = blocks[m]
                                diag = kb >= 4 * qc
                                E, EB = eT[m % 2]
                                L, LB = Lt[m % 4]
                                fw.op(act, lambda e: e.activation(out=E[:], in_=ps[:, m % 4, :], func=AF.Exp), reads=[PSB[m % 4]], writes=[EB])
                                fw.op(act, lambda e: e.activation(out=L[:], in_=E[:], func=AF.Ln, bias=1.0), reads=[EB], writes=[LB])
                                if diag:
                                    fw.op(pool, lambda e: e.tensor_tensor(out=L[:], in0=L[:], in1=amask[:, kb - 4 * qc, :], op=ALU.mult), reads=[LB, amaskB], writes=[LB])
                            m = n + 1
                            if 0 <= m < NB:
                                qc, kb = blocks[m]
                                first, last = kb == 4 * qc + 3, kb == 0
                                L, LB = Lt[m % 4]
                                if not last:
                                    bb = 4 + qc % 2
                                    H, HB = hiT[m % 3]
                                    fw.op(pe, mm(ps[:, bb, :], ones_b[:], L[:], first, kb == 1), reads=[LB, onesB], writes=[PSB[bb]])
                                    fw.op(dve, lambda e: e.tensor_copy(out=H[0:33, :], in_=ps[0:33, bb, :]), reads=[PSB[bb]], writes=[HB])
                                    fw.op(dve, lambda e: e.tensor_tensor(out=H[32:33, :], in0=ps[32:33, bb, :], in1=H[32:33, :], op=ALU.subtract), reads=[PSB[bb], HB], writes=[HB])
                            if 0 <= n < NB:
                                qc, kb = blocks[n]
                                first, diag = kb == 4 * qc + 3, kb >= 4 * qc
                                L, LB = Lt[n % 4]
                                fw.op(pe, mm(ps[:, n % 4, :], tri_b[:], L[:], False, first), reads=[LB, triB], writes=[PSB[n % 4]], inc=first)
                                if not first:
                                    H, HB = hiT[(n - 1) % 3]
                                    fw.op(pe, mm(ps[:, n % 4, :], sel33[:], H[:], False, True), reads=[HB, selB], writes=[PSB[n % 4]])
                                A, AB = At[n % 2]
                                fw.op(act, lambda e: e.activation(out=A[:], in_=ps[:, n % 4, :], func=AF.Exp), reads=[PSB[n % 4]], writes=[AB])
                                if diag:
                                    fw.op(pool, lambda e: e.tensor_tensor(out=A[:], in0=A[:], in1=amask[:, kb - 4 * qc, :], op=ALU.mult), reads=[AB, amaskB], writes=[AB])
                            m = n - 1
                            if 0 <= m < NB:
                                qc, kb = blocks[m]
                                first, last = kb == 4 * qc + 3, kb == 0
                                A, AB = At[m % 2]
                                ob_ = 6 + qc % 2
                                fw.op(pe, mm(ps[:, ob_, :], V[:, kb, :], A[:], first, last), reads=[VB_, AB], writes=[PSB[ob_]])
                                if last:
                                    fw.op(dve, lambda e: e.tensor_copy(out=OT[:, h, qc * 512:(qc + 1) * 512], in_=ps[:, ob_, :]), reads=[PSB[ob_]], writes=[OTB])
                if stop == ("attn", layer):
                    break
                proj_fm(ph, OT, OTB, 16, a_w_o[layer], D, make_resid_evac(ph, xsrc), "wo")
        else:
            if not kv_done:
                kv_done = True
                with Phase(fw) as ph:
                    kvn, kvnB = ph.tile([128, 16, S], BF16, "kvn")
                    do_norm(xT, 12, kvn, kvnB)
                    if stop == ("kvn", layer):
                        break
                    with Phase(fw) as ph2:
                        kouts = [ph2.tile([128, S], BF16, "kout") for _ in range(2)]
                        stk = {"n": 0}

                        def dst_of(c):
                            return kouts[stk["n"] % 2]

                        def after(c, half, dst, dstB):
                            if half == 1:
                                fw.dma(sp, kTB[c * 128:(c + 1) * 128, :], dst[:], reads=[dstB], owner=dstB)
                                stk["n"] += 1
                        proj_fm(ph2, kvn, kvnB, 16, b_w_kv[:, 0:1536], 1536, make_rotary_evac(ph2, dst_of, after), "wk")
                    if stop == ("kvk", layer):
                        break
                    with Phase(fw) as ph2:
                        wv = [ph2.tile([128, 16, 512], BF16, "wvb") for _ in range(2)]
                        vbig, vbigB = ph2.tile([128, 16, 512], BF16, "vbigb")
                        prm, prmB = ph2.tile([128, 16, 1024], BF16, "perm")
                        prmBs = [Buf("perm%d" % i) for i in range(16)]
                        nb = 0
                        for g in range(3):
                            dil = (1, 4, 16)[g]
                            Wt, WB = wv[g % 2]
                            fw.dma(pool, Wt[:], kview(b_w_kv[:, 1536 + g * 512:1536 + (g + 1) * 512]), writes=[WB])
                            for half in range(2):
                                if dil == 1:
                                    src, srcB, off = kvn, kvnB, half * 1024
                                else:
                                    for kc in range(16):
                                        eng = dve if kc % 2 == 0 else pool
                                        fw.op(eng, lambda e: e.tensor_copy(
                                            out=prm[:, kc, :].rearrange("p (r i) -> p r i", r=dil // 2),
                                            in_=kvn[:, kc, :].rearrange("p (i r) -> p r i", r=dil)[:, half * (dil // 2):(half + 1) * (dil // 2), :]),
                                            reads=[kvnB], writes=[prmBs[kc]])
                                    src, srcB, off = prm, None, 0
                                for tb in range(8):
                                    bk = 4 + nb % 4
                                    nb += 1
                                    for kc in range(16):
                                        fw.op(pe, mm(ps[:, bk, :], src[:, kc, off + tb * 128:off + (tb + 1) * 128], Wt[:, kc, :], kc == 0, kc == 15),
                                              reads=[WB, srcB if srcB is not None else prmBs[kc]], writes=[PSB[bk]], inc=(kc == 15))
                                    if tb % 2 == 0:
                                        fw.op(act, lambda e: e.activation(out=vbig[:, half * 8 + tb, :], in_=ps[:, bk, :], func=AF.Copy), reads=[PSB[bk]], writes=[vbigB])
                                    else:
                                        fw.op(dve, lambda e: e.tensor_copy(out=vbig[:, half * 8 + tb, :], in_=ps[:, bk, :]), reads=[PSB[bk]], writes=[vbigB])
                            fw.dma(sp, VB[g].rearrange("(tb tp) f -> tp tb f", tp=128), vbig[:], reads=[vbigB], owner=vbigB)
            if stop == ("kv", layer):
                break
            j = layer - 2
            with Phase(fw) as ph:
                qTB, qTBB = ph.tile([128, 12, S], BF16, "qTB")
                with Phase(fw) as ph2:
                    hT, hTB = ph2.tile([128, 16, S], BF16, "hTb")
                    do_norm(xT, layer, hT, hTB)
                    proj_fm(ph2, hT, hTB, 16, b_w_q[j], 1536, make_rotary_evac(ph2, lambda c: (qTB[:, c, :], qTBB)), "wq")
                if stop == ("q", layer):
                    break
                OTb, OTbB = ph.tile([128, 12, S], BF16, "OTb")
                with Phase(fw) as ph2:
                    kts = [ph2.tile([128, S], BF16, "kt") for _ in range(2)]
                    vts = [ph2.tile([128, 16, 128], BF16, "vt") for _ in range(2)]
                    Es = [ph2.tile([128, 1024], BF16, "E") for _ in range(2)]
                    Ng = [ph2.tile([128, S], F32, "Ng") for _ in range(3)]
                    Zs, ZsB = ph2.tile([128, S], F32, "Zs")
                    seq = [(hg, g) for hg in range(4) for g in range(3)]

                    def load_kv(i):
                        hg, g = seq[i]
                        c = g * 4 + hg
                        fw.dma(sp, kts[i % 2][0][:], kTB[c * 128:(c + 1) * 128, :], writes=[kts[i % 2][1]])
                        fw.dma(sp, vts[i % 2][0][:], VB[g].rearrange("(kb kp) f -> kp kb f", kp=128)[:, :, hg * 128:(hg + 1) * 128], writes=[vts[i % 2][1]])
                    load_kv(0)
                    nsb = 0
                    for i, (hg, g) in enumerate(seq):
                        if i + 1 < len(seq):
                            load_kv(i + 1)
                        c = g * 4 + hg
                        dil = (1, 4, 16)[g]
                        sub = S // dil
                        Kt, KtB = kts[i % 2]
                        Vt, VtB = vts[i % 2]
                        N_, NB_ = Ng[g]
                        for sb in range(4):
                            k2 = nsb % 2
                            nsb += 1
                            sbk = 2 * k2
                            nbk, zbk = 4 + 2 * k2, 5 + 2 * k2
                            E, EB = Es[k2]
                            nfirst = []
                            for ql in range(4):
                                pos = sb * 512 + ql * 128
                                nblk = (pos % sub) // 128
                                nfirst.append(nblk == 0)
                                pprev = pos if nblk == 0 else pos - 128
                                col = ql * 256
                                bank, off = sbk + col // 512, col % 512
                                fw.op(pe, mm(ps[:, bank, off:off + 128], Kt[:, pprev:pprev + 128], qTB[:, c, pos:pos + 128], True, True),
                                      reads=[KtB, qTBB], writes=[PSB[bank]], inc=False)
                                fw.op(pe, mm(ps[:, bank, off + 128:off + 256], Kt[:, pos:pos + 128], qTB[:, c, pos:pos + 128], True, True),
                                      reads=[KtB, qTBB], writes=[PSB[bank]], inc=(ql == 3))
                            fw.op(act, lambda e: e.activation(out=v1024(E[:]), in_=ps[:, sbk:sbk + 2, :], func=AF.Exp, scale=QSCALE), reads=[PSB[sbk], PSB[sbk + 1]], writes=[EB])
                            mv = 2 if all(nfirst) else (0 if nfirst[0] else 1)
                            fw.op(pool, lambda e: e.tensor_tensor(out=E[:], in0=E[:], in1=bmask[:, mv, :], op=ALU.mult), reads=[EB, bmaskB], writes=[EB])
                            for (bank_, lhs_of) in ((nbk, lambda kbi: Vt[:, kbi, :]), (zbk, lambda kbi: ones_b[:])):
                                for ql in range(4):
                                    pos = sb * 512 + ql * 128
                                    kbi = pos // 128
                                    col = ql * 256
                                    if not nfirst[ql]:
                                        fw.op(pe, mm(ps[:, bank_, ql * 128:(ql + 1) * 128], lhs_of(kbi - 1), E[:, col:col + 128], True, False),
                                              reads=[EB, VtB, onesB], writes=[PSB[bank_]], inc=False)
                                    fw.op(pe, mm(ps[:, bank_, ql * 128:(ql + 1) * 128], lhs_of(kbi), E[:, col + 128:col + 256], nfirst[ql], True),
                                          reads=[EB, VtB, onesB], writes=[PSB[bank_]], inc=(ql == 3))

                            def tview(t):
                                if dil == 1:
                                    return t[:, sb * 512:(sb + 1) * 512]
                                if dil == 4:
                                    return t[:, :].rearrange("p (i r) -> p r i", r=4)[:, sb, :]
                                return t[:, :].rearrange("p (i r) -> p r i", r=16)[:, 4 * sb:4 * sb + 4, :]

                            def pview(bank__):
                                if dil == 16:
                                    return ps[:, bank__, :].rearrange("p (r i) -> p r i", r=4)
                                return ps[:, bank__, :]
                            fw.op(act, lambda e: e.activation(out=tview(N_), in_=pview(nbk), func=AF.Copy), reads=[PSB[nbk]], writes=[NB_])
                            if g == 0:
                                fw.op(dve, lambda e: e.tensor_copy(out=tview(Zs), in_=pview(zbk)), reads=[PSB[zbk]], writes=[ZsB])
                            else:
                                fw.op(dve, lambda e: e.tensor_tensor(out=tview(Zs), in0=tview(Zs), in1=pview(zbk), op=ALU.add), reads=[PSB[zbk], ZsB], writes=[ZsB])
                        if g == 2:
                            fw.op(dve, lambda e: e.reciprocal(out=Zs[:], in_=Zs[:]), reads=[ZsB], writes=[ZsB])
                            for g2 in range(3):
                                fw.op(dve, lambda e: e.tensor_tensor(out=OTb[:, g2 * 4 + hg, :], in0=Ng[g2][0][:], in1=Zs[:], op=ALU.mult),
                                      reads=[Ng[g2][1], ZsB], writes=[OTbB])
                if dbg:
                    dO = nc.dram_tensor("dbg_OTb%d" % layer, [1536, S], BF16, kind="ExternalOutput").ap()
                    fw.dma(sp, dO.rearrange("(c p) t -> p c t", p=128), OTb[:], reads=[OTbB], owner=OTbB)
                if stop == ("attn", layer):
                    break
                proj_fm(ph, OTb, OTbB, 12, b_w_o[j], D, make_resid_evac(ph, xT), "wob")
        xsrc = xT
        snap("dbg_attn%d" % layer)
        if stop == ("mix", layer):
            break
        with Phase(fw) as ph:
            hT, hTB = ph.tile([128, 16, S], BF16, "h2T")
            wg = [ph.tile([128, 16, 256], BF16, "wg") for _ in range(2)]
            wvv = [ph.tile([128, 16, 256], BF16, "wvv") for _ in range(2)]
            Wup = ffn_w_up[layer]

            def load_w(t):
                fw.dma(pool, wg[t % 2][0][:], kview(Wup)[:, :, t * 256:(t + 1) * 256], writes=[wg[t % 2][1]])
                fw.dma(pool, wvv[t % 2][0][:], kview(Wup)[:, :, DFF + t * 256:DFF + (t + 1) * 256], writes=[wvv[t % 2][1]])
            load_w(0)
            do_norm(xT, 4 + layer, hT, hTB)
            Ug = [ph.tile([128, S + 2], F32, "Ug") for _ in range(2)]
            Uv = [ph.tile([128, S + 2], F32, "Uv") for _ in range(2)]
            accg, accgB = ph.tile([128, S], F32, "accg")
            accv, accvB = ph.tile([128, S], F32, "accv")
            aout = [ph.tile([128, S], BF16, "aout") for _ in range(2)]
            for U, UB in Ug + Uv:
                fw.op(dve, lambda e: e.memset(U[:, 0:2], 0.0), writes=[UB])
            for jf in range(NFC):
                t, jj = jf // 2, jf % 2
                if jj == 0 and t + 1 < NFC // 2:
                    load_w(t + 1)
                WG, WGB = wg[t % 2]
                WV, WVB = wvv[t % 2]
                UG, UGB = Ug[jf % 2]
                UV, UVB = Uv[jf % 2]
                cs = slice(jj * 128, (jj + 1) * 128)
                for half in range(2):
                    b0 = 4 * half
                    for kc in range(16):
                        for t2 in range(2):
                            tcs = slice((half * 2 + t2) * 512, (half * 2 + t2 + 1) * 512)
                            fw.op(pe, mm(ps[:, b0 + t2, :], WG[:, kc, cs], hT[:, kc, tcs], kc == 0, kc == 15), reads=[WGB, hTB], writes=[PSB[b0 + t2]], inc=False)
                            fw.op(pe, mm(ps[:, b0 + 2 + t2, :], WV[:, kc, cs], hT[:, kc, tcs], kc == 0, kc == 15), reads=[WVB, hTB], writes=[PSB[b0 + 2 + t2]],
                                  inc=(kc == 15 and t2 == 1))
                    us = slice(2 + half * 1024, 2 + (half + 1) * 1024)
                    fw.op(act, lambda e: e.activation(out=v1024(UG[:, us]), in_=ps[:, b0:b0 + 2, :], func=AF.Copy), reads=[PSB[b0], PSB[b0 + 1]], writes=[UGB])
                    fw.op(act, lambda e: e.activation(out=v1024(UV[:, us]), in_=ps[:, b0 + 2:b0 + 4, :], func=AF.Copy), reads=[PSB[b0 + 2], PSB[b0 + 3]], writes=[UVB])
                for (U, UB, acc, accB, ci) in ((UG, UGB, accg, accgB, jf), (UV, UVB, accv, accvB, NFC + jf)):
                    fw.op(act, lambda e: e.activation(out=acc[:], in_=U[:, 2:S + 2], func=AF.Identity, scale=convw[:, layer, 2, ci:ci + 1], bias=convb[:, layer, ci:ci + 1]),
                          reads=[UB, convwB, convbB], writes=[accB])
                    fw.op(dve, lambda e: e.scalar_tensor_tensor(out=acc[:], in0=U[:, 1:S + 1], scalar=convw[:, layer, 1, ci:ci + 1], in1=acc[:], op0=ALU.mult, op1=ALU.add),
                          reads=[UB, convwB, accB], writes=[accB])
                    fw.op(dve, lambda e: e.scalar_tensor_tensor(out=acc[:], in0=U[:, 0:S], scalar=convw[:, layer, 0, ci:ci + 1], in1=acc[:], op0=ALU.mult, op1=ALU.add),
                          reads=[UB, convwB, accB], writes=[accB])
                fw.op(act, lambda e: e.activation(out=accg[:], in_=accg[:], func=AF.Silu), reads=[accgB], writes=[accgB])
                AO, AOB = aout[jf % 2]
                fw.op(dve, lambda e: e.tensor_tensor(out=AO[:], in0=accg[:], in1=accv[:], op=ALU.mult), reads=[accgB, accvB], writes=[AOB])
                fw.dma(sp, AT[jf * 128:(jf + 1) * 128, :], AO[:], reads=[AOB], owner=AOB)
        with Phase(fw) as ph:
            wd = [ph.tile([128, 22, 512], BF16, "wd") for _ in range(4)]
            at = [ph.tile([128, 22, 512], BF16, "at") for _ in range(2)]
            xt4 = [ph.tile([128, 4, 512], F32, "xt4") for _ in range(2)]
            Wdn = ffn_w_down[layer]
            ATv = AT.rearrange("(fc fp) t -> fp fc t", fp=128)
            xTv = xT.rearrange("(dc dp) t -> dp dc t", dp=128)

            def load_wd(g4):
                for kh in range(2):
                    W_, WB_ = wd[(g4 % 2) * 2 + kh]
                    fw.dma(pool, W_[:], kview(Wdn)[:, kh * 22:(kh + 1) * 22, g4 * 512:(g4 + 1) * 512], writes=[WB_])

            def load_at(tc, kh):
                fw.dma(sp, at[kh][0][:], ATv[:, kh * 22:(kh + 1) * 22, tc * 512:(tc + 1) * 512], writes=[at[kh][1]])
            load_wd(0)
            load_at(0, 0)
            load_at(0, 1)
            step = 0
            for g4 in range(4):
                if g4 + 1 < 4:
                    load_wd(g4 + 1)
                for tc in range(4):
                    b0 = 4 * (step % 2)
                    X4, X4B = xt4[step % 2]
                    fw.dma(sp, X4[:], xTv[:, g4 * 4:(g4 + 1) * 4, tc * 512:(tc + 1) * 512], writes=[X4B])
                    for kh in range(2):
                        W_, WB_ = wd[(g4 % 2) * 2 + kh]
                        A_, AB_ = at[kh]
                        for fc in range(22):
                            for dc in range(4):
                                fw.op(pe, mm(ps[:, b0 + dc, :], W_[:, fc, dc * 128:(dc + 1) * 128], A_[:, fc, :], kh == 0 and fc == 0, kh == 1 and fc == 21),
                                      reads=[WB_, AB_], writes=[PSB[b0 + dc]], inc=(fc == 21 and dc == 3))
                        nxt = step + 1
                        if nxt < 16:
                            load_at(nxt % 4, kh)
                    fw.op(dve, lambda e: e.tensor_tensor(out=X4[:], in0=X4[:], in1=ps[:, b0:b0 + 4, :], op=ALU.add), reads=[X4B] + PSB[b0:b0 + 4], writes=[X4B])
                    fw.dma(sp, xTv[:, g4 * 4:(g4 + 1) * 4, tc * 512:(tc + 1) * 512], X4[:], reads=[X4B], owner=X4B)
                    step += 1
        snap("dbg_ffn%d" % layer)
        if stop == ("ffn", layer):
            break
        with Phase(fw) as ph:
            hT, hTB = ph.tile([128, 16, S], BF16, "h3T")
            pTb, pTbB = ph.tile([128, 2, S], BF16, "pTb")
            fw.dma(pool, pTb[:], kview(pT[layer]), writes=[pTbB])
            wgt = [ph.tile([128, 16, 512], BF16, "wgt") for _ in range(2)]
            wpj = [ph.tile([128, 2, 512], BF16, "wpj") for _ in range(2)]

            def load_g(g4):
                fw.dma(pool, wgt[g4 % 2][0][:], kview(ple_w_gate[layer])[:, :, g4 * 512:(g4 + 1) * 512], writes=[wgt[g4 % 2][1]])
                fw.dma(pool, wpj[g4 % 2][0][:], kview(ple_w_proj[layer])[:, :, g4 * 512:(g4 + 1) * 512], writes=[wpj[g4 % 2][1]])
            load_g(0)
            do_norm(xT, 8 + layer, hT, hTB)
            sgs = [ph.tile([128, 1024], F32, "sg") for _ in range(2)]
            xts = [ph.tile([128, S], F32, "xp") for _ in range(2)]

            cnt = 0
            for g4 in range(4):
                if g4 + 1 < 4:
                    load_g(g4 + 1)
                WG, WGB = wgt[g4 % 2]
                WP, WPB = wpj[g4 % 2]
                for dcl in range(4):
                    dc = g4 * 4 + dcl
                    X, XB = xts[dc % 2]
                    fw.dma(sp, X[:], xT[dc * 128:(dc + 1) * 128, :], writes=[XB])
                    cs = slice(dcl * 128, (dcl + 1) * 128)
                    for half in range(2):
                        b0 = 4 * (cnt % 2)
                        SG, SGB = sgs[cnt % 2]
                        cnt += 1
                        for kc in range(16):
                            for t2 in range(2):
                                tcs = slice((half * 2 + t2) * 512, (half * 2 + t2 + 1) * 512)
                                fw.op(pe, mm(ps[:, b0 + t2, :], WG[:, kc, cs], hT[:, kc, tcs], kc == 0, kc == 15), reads=[WGB, hTB], writes=[PSB[b0 + t2]], inc=False)
                        for kc in range(2):
                            for t2 in range(2):
                                tcs = slice((half * 2 + t2) * 512, (half * 2 + t2 + 1) * 512)
                                fw.op(pe, mm(ps[:, b0 + 2 + t2, :], WP[:, kc, cs], pTb[:, kc, tcs], kc == 0, kc == 1), reads=[WPB, pTbB], writes=[PSB[b0 + 2 + t2]],
                                      inc=(kc == 1 and t2 == 1))
                        hs = slice(half * 1024, (half + 1) * 1024)
                        fw.op(act, lambda e: e.activation(out=v1024(SG[:]), in_=ps[:, b0:b0 + 2, :], func=AF.Sigmoid), reads=[PSB[b0], PSB[b0 + 1]], writes=[SGB])
                        fw.op(dve, lambda e: e.tensor_tensor(out=v1024(SG[:]), in0=v1024(SG[:]), in1=ps[:, b0 + 2:b0 + 4, :], op=ALU.mult),
                              reads=[SGB, PSB[b0 + 2], PSB[b0 + 3]], writes=[SGB])
                        fw.op(pool, lambda e: e.tensor_tensor(out=X[:, hs], in0=X[:, hs], in1=SG[:], op=ALU.add), reads=[SGB, XB], writes=[XB])
                    fw.dma(sp, xT[dc * 128:(dc + 1) * 128, :], X[:], reads=[XB], owner=XB)
        snap("dbg_ple%d" % layer)
        if stop == ("ple", layer):
            break
    else:
        do_norm(xT, 13, out_dram=yT)
    fw.barrier()
    fw.close()
    return nc


def _consts():
    j = np.arange(128)
    tri = -(j[:, None] >= j[None, :]).astype(np.float32)
    ones = np.ones((128, 128), np.float32)
    t = np.arange(512)
    amask = np.stack([((128 * i + j)[:, None] < t[None, :]).astype(np.float32) for i in range(4)], axis=1)
    up = (j[:, None] >= j[None, :]).astype(np.float32)
    lo = (j[:, None] <= j[None, :]).astype(np.float32)
    F_ = np.concatenate([np.zeros((128, 128), np.float32), lo], 1)
    R_ = np.concatenate([up, lo], 1)
    bmask = np.stack([np.concatenate([F_, R_, R_, R_], 1), np.concatenate([R_] * 4, 1), np.concatenate([F_] * 4, 1)], axis=1)
    sel33 = np.zeros((128, 128), np.float32)
    sel33[0] = -1.0
    sel33[32] = -1.0
    perm = np.zeros((32, 32), np.float32)
    for m in range(32):
        if m < 16:
            perm[m + 16, m] = -1.0
        else:
            perm[m - 16, m] = 1.0
    invf = (500000.0 ** (-np.arange(0, 32, 2, dtype=np.float32) / 32)).astype(np.float32)
    invf = np.concatenate([invf, invf]).reshape(32, 1)
    return dict(c_tri=tri, c_ones=ones, c_amask=np.ascontiguousarray(amask.reshape(128, -1)),
                c_bmask=np.ascontiguousarray(bmask.reshape(128, -1)), c_sel33=sel33, c_perm=perm, c_invf=invf)


def _pack(inputs):
    f = lambda a: np.ascontiguousarray(np.asarray(a, dtype=np.float32))
    nw = np.concatenate([np.asarray(inputs["attn_norm_w"]), np.asarray(inputs["ffn_norm_w"]), np.asarray(inputs["ple_norm_w"]),
                         np.asarray(inputs["kv_norm_w"])[None], np.asarray(inputs["final_norm_w"])[None]], 0)
    normw = f(nw.reshape(14, 16, 128).transpose(2, 0, 1).reshape(128, -1))
    convw = f(np.asarray(inputs["ffn_conv_w"]).reshape(4, 3, 88, 128).transpose(3, 0, 1, 2).reshape(128, -1))
    convb = f(np.asarray(inputs["ffn_conv_b"]).reshape(4, 88, 128).transpose(2, 0, 1).reshape(128, -1))
    shared = dict(normw=normw, convw=convw, convb=convb)
    for k in ("a_w_qkv", "a_w_o", "b_w_kv", "b_w_q", "b_w_o", "ffn_w_up", "ffn_w_down", "ple_w_gate", "ple_w_proj"):
        shared[k] = f(inputs[k])
    shared.update(_consts())
    return shared


def _core_inputs(inputs, shared, b):
    m = dict(shared)
    m["xT_in"] = np.ascontiguousarray(np.asarray(inputs["x"][b], dtype=np.float32).T)
    m["pT"] = np.ascontiguousarray(np.asarray(inputs["p"][:, b], dtype=np.float32).transpose(0, 2, 1))
    m["pos32"] = np.ascontiguousarray(np.broadcast_to(np.asarray(inputs["positions"][b], dtype=np.int32)[None, :], (32, S)))
    return m


_NC = None


def kernel(**inputs):
    global _NC
    if _NC is None:
        _NC = build_program()
    shared = _pack(inputs)
    nb = inputs["x"].shape[0]
    in_maps = [_core_inputs(inputs, shared, b) for b in range(nb)]
    res = run_bass_kernel_spmd(_NC, in_maps, core_ids=list(range(nb)))
    out = np.stack([np.asarray(r["yT"]).T for r in res.results], 0)
    return np.ascontiguousarray(out.astype(np.float32))
```
